# Optimizing a Trainium2 kernel written in Bass

```python
import math
import jax, jax.numpy as jnp
from jax import lax
import numpy as np

D_MODEL = 1024
BATCH = 2
SEQ = 8192
DEPTH = 2

GRID_W = 64
CTX_LEN = 256
D_FF = 4 * D_MODEL
EPS = 1e-6
N_EVEN = (DEPTH + 1) // 2
N_ODD = DEPTH // 2
MLSTM_HEADS = 6
MLSTM_DH = 128
D_A = MLSTM_HEADS * MLSTM_DH
D_B = D_MODEL - D_A
FNET_GROUPS = 4
FNET_GC = D_B // FNET_GROUPS
MLSTM_CHUNK = 64
QK_CONV_W = 3
EVEN_IN = 4 * D_A + 4 * MLSTM_HEADS + D_B
DIFF_HEADS = 6
DIFF_DH = 64
DIFF_DV = 2 * DIFF_DH
D_C = DIFF_HEADS * DIFF_DV
D_D = D_MODEL - D_C
CONV_W = 31
ODD_IN = 3 * D_C + 2 * D_D
Q_BLOCK = 128
ROPE_BASE = 10000.0

kernel_name = "hybrid_mlstm_fnet_diffattn_conformer_dit"


def rmsnorm(x, g):
    xf = x.astype(jnp.float32)
    y = xf * lax.rsqrt(jnp.mean(xf * xf, axis=-1, keepdims=True) + EPS)
    return (y * g.astype(jnp.float32)).astype(x.dtype)


def layernorm(x, g, b):
    xf = x.astype(jnp.float32)
    mu = jnp.mean(xf, axis=-1, keepdims=True)
    var = jnp.mean(jnp.square(xf - mu), axis=-1, keepdims=True)
    y = (xf - mu) * lax.rsqrt(var + EPS)
    return (y * g.astype(jnp.float32) + b.astype(jnp.float32)).astype(x.dtype)


def modulate(h, shift, scale):
    return h * (1 + scale) + shift


def squared_relu_mlp(h, w1, w2):
    return jnp.square(jax.nn.relu(h @ w1)) @ w2


def depthwise_conv(x, w):
    pad = (w.shape[0] - 1) // 2
    return lax.conv_general_dilated(
        x, w.astype(x.dtype)[:, None, :], window_strides=(1,), padding=[(pad, pad)],
        dimension_numbers=('NWC', 'WIO', 'NWC'), feature_group_count=x.shape[-1])


def heads_first(a, n):
    b_, t_, _ = a.shape
    return a.reshape(b_, t_, n, -1).transpose(0, 2, 1, 3)


def head_rmsnorm(h, g, n):
    b_, t_, dm = h.shape
    hf = h.astype(jnp.float32).reshape(b_, t_, n, dm // n)
    hf = hf * lax.rsqrt(jnp.mean(hf * hf, axis=-1, keepdims=True) + EPS)
    return (hf.reshape(b_, t_, dm) * g.astype(jnp.float32)).astype(h.dtype)


def fourier_mix(f):
    b_, t_, _ = f.shape
    g = f.astype(jnp.float32).reshape(b_, t_, FNET_GROUPS, FNET_GC)
    y = jnp.fft.fft2(g, axes=(1, 3), norm='ortho').real
    return y.reshape(b_, t_, D_B).astype(f.dtype)


def axial_rope(t_len, dh):
    rows = t_len // GRID_W
    row = jnp.repeat(jnp.arange(rows, dtype=jnp.float32), GRID_W)
    col = jnp.tile(jnp.arange(GRID_W, dtype=jnp.float32), rows)
    n_freq = dh // 4
    inv = ROPE_BASE ** (-jnp.arange(n_freq, dtype=jnp.float32) / n_freq)
    ang = jnp.concatenate([row[:, None] * inv, col[:, None] * inv], axis=-1)
    return jnp.cos(ang), jnp.sin(ang)


def apply_rope(x, cos, sin):
    cos = cos[None, :, None, None, :]
    sin = sin[None, :, None, None, :]
    x1, x2 = jnp.split(x.astype(jnp.float32), 2, axis=-1)
    return jnp.concatenate([x1 * cos - x2 * sin, x2 * cos + x1 * sin], axis=-1).astype(x.dtype)


def mlstm_scan(q, k, v, logi, logf, state):
    b_, h_, t_, d_ = q.shape
    nc = t_ // MLSTM_CHUNK

    def to_chunks(a):
        a = a.reshape(a.shape[:2] + (nc, MLSTM_CHUNK) + a.shape[3:])
        return jnp.moveaxis(a, 2, 0)

    causal = jnp.tril(jnp.ones((MLSTM_CHUNK, MLSTM_CHUNK), dtype=bool))

    def step(carry, xs):
        c_mem, n_mem, m_prev = carry
        qc, kc, vc, ic, fc = xs
        bcum = jnp.cumsum(fc, axis=-1)
        dlog = bcum[..., :, None] - bcum[..., None, :] + ic[..., None, :]
        dlog = jnp.where(causal, dlog, -jnp.inf)
        m_t = jnp.maximum(bcum + m_prev[..., None], jnp.max(dlog, axis=-1))
        s = jnp.einsum('bhtd,bhsd->bhts', qc, kc) * jnp.exp(dlog - m_t[..., None])
        inter = jnp.exp(bcum + m_prev[..., None] - m_t)
        num = jnp.einsum('bhts,bhsd->bhtd', s, vc) + inter[..., None] * jnp.einsum('bhvk,bhtk->bhtv', c_mem, qc)
        den = jnp.sum(s, axis=-1) + inter * jnp.einsum('bhk,bhtk->bht', n_mem, qc)
        h = num / jnp.maximum(jnp.abs(den), jnp.exp(-m_t))[..., None]
        g = bcum[..., -1:] - bcum + ic
        m_new = jnp.maximum(bcum[..., -1] + m_prev, jnp.max(g, axis=-1))
        w = jnp.exp(g - m_new[..., None])
        decay = jnp.exp(bcum[..., -1] + m_prev - m_new)
        c_new = decay[..., None, None] * c_mem + jnp.einsum('bhs,bhsv,bhsk->bhvk', w, vc, kc)
        n_new = decay[..., None] * n_mem + jnp.einsum('bhs,bhsk->bhk', w, kc)
        return (c_new, n_new, m_new), h

    state, hs = lax.scan(step, state, (to_chunks(q), to_chunks(k), to_chunks(v), to_chunks(logi), to_chunks(logf)))
    h = jnp.moveaxis(hs, 0, 2).reshape(b_, h_, t_, d_)
    return h, state


def even_mixer(h_lat, h_ctx, w_in, b_gate, w_qk_conv, g_head, w_out, need_ctx_out):
    f32 = jnp.float32

    def project(h):
        u = h @ w_in
        qk, v, o, gates, f = jnp.split(u, [2 * D_A, 3 * D_A, 4 * D_A, 4 * D_A + 4 * MLSTM_HEADS], axis=-1)
        qk = jax.nn.silu(depthwise_conv(qk, w_qk_conv))
        q, k = jnp.split(qk, 2, axis=-1)
        q = heads_first(q, MLSTM_HEADS).astype(f32) * (MLSTM_DH ** -0.5)
        k = heads_first(k, MLSTM_HEADS).astype(f32)
        v = heads_first(v, MLSTM_HEADS).astype(f32)
        gates = (gates + b_gate).astype(f32).transpose(0, 2, 1)
        i_f, f_f, i_b, f_b = jnp.split(gates, 4, axis=1)
        return q, k, v, (i_f, jax.nn.log_sigmoid(f_f)), (i_b, jax.nn.log_sigmoid(f_b)), o, f

    qc, kc, vc, gcf, gcb, oc, fc = project(h_ctx)
    ql, kl, vl, glf, glb, ol, fl = project(h_lat)
    b_ = h_lat.shape[0]
    zero = (jnp.zeros((b_, MLSTM_HEADS, MLSTM_DH, MLSTM_DH), f32),
            jnp.zeros((b_, MLSTM_HEADS, MLSTM_DH), f32),
            jnp.zeros((b_, MLSTM_HEADS), f32))
    flip = lambda a: jnp.flip(a, axis=2)
    hcf, st_f = mlstm_scan(qc, kc, vc, gcf[0], gcf[1], zero)
    hlf, _ = mlstm_scan(ql, kl, vl, glf[0], glf[1], st_f)
    hcb, st_b = mlstm_scan(flip(qc), flip(kc), flip(vc), flip(gcb[0]), flip(gcb[1]), zero)
    hlb, _ = mlstm_scan(flip(ql), flip(kl), flip(vl), flip(glb[0]), flip(glb[1]), st_b)

    def finish(hf, hb_rev, o, f):
        h = hf + flip(hb_rev)
        b2, _, t2, _ = h.shape
        h = h.transpose(0, 2, 1, 3).reshape(b2, t2, D_A).astype(o.dtype)
        h = head_rmsnorm(h, g_head, MLSTM_HEADS) * jax.nn.sigmoid(o)
        return jnp.concatenate([h, fourier_mix(f)], axis=-1) @ w_out

    y_lat = finish(hlf, hlb, ol, fl)
    y_ctx = finish(hcf, hcb, oc, fc) if need_ctx_out else None
    return y_lat, y_ctx


def diff_attention(q, k, v, lam):
    s = jnp.einsum('bqhmd,bkhmd->bhmqk', q, k).astype(jnp.float32) * (DIFF_DH ** -0.5)
    p = jax.nn.softmax(s, axis=-1)
    a = p[:, :, 0] - lam * p[:, :, 1]
    return jnp.einsum('bhqk,bkhd->bqhd', a.astype(v.dtype), v)


def odd_mixer(h_lat, h_ctx, w_in, lam_p, g_sub, w_dw, g_ln, b_ln, w_out, lam_init, need_ctx_out):
    def project(h):
        u = h @ w_in
        q, k, v, glu = jnp.split(u, [D_C, 2 * D_C, 3 * D_C], axis=-1)
        b_, t_, _ = h.shape
        q = q.reshape(b_, t_, DIFF_HEADS, 2, DIFF_DH)
        k = k.reshape(b_, t_, DIFF_HEADS, 2, DIFF_DH)
        v = v.reshape(b_, t_, DIFF_HEADS, DIFF_DV)
        return q, k, v, glu

    qc, kc, vc, gluc = project(h_ctx)
    ql, kl, vl, glul = project(h_lat)
    b_, t_, _ = h_lat.shape
    cos, sin = axial_rope(t_, DIFF_DH)
    ql = apply_rope(ql, cos, sin)
    kl = apply_rope(kl, cos, sin)
    lp = lam_p.astype(jnp.float32)
    lam = jnp.exp(jnp.sum(lp[0] * lp[1])) - jnp.exp(jnp.sum(lp[2] * lp[3])) + lam_init
    k_all = jnp.concatenate([kc, kl], axis=1)
    v_all = jnp.concatenate([vc, vl], axis=1)
    nb = t_ // Q_BLOCK
    qb = jnp.moveaxis(ql.reshape(b_, nb, Q_BLOCK, DIFF_HEADS, 2, DIFF_DH), 1, 0)
    ob = lax.map(lambda q_blk: diff_attention(q_blk, k_all, v_all, lam), qb)
    o_lat = jnp.moveaxis(ob, 0, 1).reshape(b_, t_, DIFF_HEADS, DIFF_DV)

    def finish(o, glu):
        b2, t2 = o.shape[0], o.shape[1]
        o = (rmsnorm(o, g_sub) * (1.0 - lam_init)).reshape(b2, t2, D_C)
        a, gte = jnp.split(glu, 2, axis=-1)
        z = depthwise_conv(a * jax.nn.sigmoid(gte), w_dw)
        z = jax.nn.silu(layernorm(z, g_ln, b_ln))
        return jnp.concatenate([o.astype(z.dtype), z], axis=-1) @ w_out

    y_lat = finish(o_lat, glul)
    y_ctx = finish(diff_attention(qc, kc, vc, lam), gluc) if need_ctx_out else None
    return y_lat, y_ctx


def setup_inputs(seed: int = 0) -> dict:
    key = jax.random.key(seed)
    ks = jax.random.split(key, 24)
    f32 = jnp.float32

    def nrm(k, shape, scale):
        return jax.random.normal(k, shape, f32) * scale

    fbias = jnp.linspace(3.0, 6.0, MLSTM_HEADS, dtype=f32)
    gate_base = jnp.array([0.0, 1.0, 0.0, 1.0], f32)[:, None] * fbias[None, :]
    b_gate = (nrm(ks[8], (N_EVEN, 4, MLSTM_HEADS), 0.1) + gate_base[None]).reshape(N_EVEN, 4 * MLSTM_HEADS)
    return {
        'x': nrm(ks[0], (BATCH, SEQ, D_MODEL), 1.0),
        'c': nrm(ks[1], (BATCH, D_MODEL), 1.0),
        'ctx': nrm(ks[2], (BATCH, CTX_LEN, D_MODEL), 1.0),
        'c_ctx': nrm(ks[3], (D_MODEL,), 1.0),
        'w_mod': nrm(ks[4], (DEPTH, D_MODEL, 6 * D_MODEL), 0.5 * D_MODEL ** -0.5),
        'b_mod': nrm(ks[5], (DEPTH, 6 * D_MODEL), 0.02),
        'g_norm': 1.0 + nrm(ks[6], (DEPTH, 2, D_MODEL), 0.02),
        'w_in_even': nrm(ks[7], (N_EVEN, D_MODEL, EVEN_IN), D_MODEL ** -0.5),
        'b_gate': b_gate,
        'w_qk_conv': nrm(ks[9], (N_EVEN, QK_CONV_W, 2 * D_A), QK_CONV_W ** -0.5),
        'g_mlstm_head': 1.0 + nrm(ks[10], (N_EVEN, D_A), 0.02),
        'w_out_even': nrm(ks[11], (N_EVEN, D_MODEL, D_MODEL), D_MODEL ** -0.5),
        'w_in_odd': nrm(ks[12], (N_ODD, D_MODEL, ODD_IN), D_MODEL ** -0.5),
        'lam_p': nrm(ks[13], (N_ODD, 4, DIFF_DH), 0.1),
        'g_subln': 1.0 + nrm(ks[14], (N_ODD, DIFF_DV), 0.02),
        'w_dw': nrm(ks[15], (N_ODD, CONV_W, D_D), CONV_W ** -0.5),
        'g_conv_ln': 1.0 + nrm(ks[16], (N_ODD, D_D), 0.02),
        'b_conv_ln': nrm(ks[17], (N_ODD, D_D), 0.02),
        'w_out_odd': nrm(ks[18], (N_ODD, D_MODEL, D_MODEL), D_MODEL ** -0.5),
        'w_ff1': nrm(ks[19], (DEPTH, D_MODEL, D_FF), D_MODEL ** -0.5),
        'w_ff2': nrm(ks[20], (DEPTH, D_FF, D_MODEL), D_FF ** -0.5),
        'g_final': 1.0 + nrm(ks[21], (D_MODEL,), 0.02),
    }


def reference(x, c, ctx, c_ctx, w_mod, b_mod, g_norm, w_in_even, b_gate, w_qk_conv, g_mlstm_head,
              w_out_even, w_in_odd, lam_p, g_subln, w_dw, g_conv_ln, b_conv_ln, w_out_odd,
              w_ff1, w_ff2, g_final):
    xc = ctx
    s_lat = jax.nn.silu(c)
    s_ctx = jax.nn.silu(c_ctx)
    for l in range(DEPTH):
        last = l == DEPTH - 1
        j = l // 2
        mod_l = jnp.split((s_lat @ w_mod[l] + b_mod[l])[:, None, :], 6, axis=-1)
        mod_c = jnp.split(s_ctx @ w_mod[l] + b_mod[l], 6, axis=-1)
        h_l = modulate(rmsnorm(x, g_norm[l, 0]), mod_l[0], mod_l[1])
        h_c = modulate(rmsnorm(xc, g_norm[l, 0]), mod_c[0], mod_c[1])
        if l % 2 == 0:
            y_l, y_c = even_mixer(h_l, h_c, w_in_even[j], b_gate[j], w_qk_conv[j], g_mlstm_head[j],
                                  w_out_even[j], not last)
        else:
            lam_init = 0.8 - 0.6 * math.exp(-0.3 * l)
            y_l, y_c = odd_mixer(h_l, h_c, w_in_odd[j], lam_p[j], g_subln[j], w_dw[j], g_conv_ln[j],
                                 b_conv_ln[j], w_out_odd[j], lam_init, not last)
        x = x + mod_l[2] * y_l
        x = x + mod_l[5] * squared_relu_mlp(modulate(rmsnorm(x, g_norm[l, 1]), mod_l[3], mod_l[4]), w_ff1[l], w_ff2[l])
        if not last:
            xc = xc + mod_c[2] * y_c
            xc = xc + mod_c[5] * squared_relu_mlp(modulate(rmsnorm(xc, g_norm[l, 1]), mod_c[3], mod_c[4]), w_ff1[l], w_ff2[l])
    return rmsnorm(x, g_final)
```

```python
import numpy as np, ml_dtypes
from contextlib import ExitStack
import concourse.bass as bass
import concourse.mybir as mybir
from concourse.bass_utils import run_bass_kernel_spmd

F32 = mybir.dt.float32; BF16 = mybir.dt.bfloat16
AF = mybir.ActivationFunctionType; ALU = mybir.AluOpType; AX = mybir.AxisListType
NPBF = ml_dtypes.bfloat16
NCORES = 8


class Buf:
    __slots__ = ("t", "w", "r")

    def __init__(self, t):
        self.t = t; self.w = None; self.r = {}

    def __getitem__(self, k):
        return self.t[k]


class KB:
    NSLOT = 40

    def __init__(self):
        self.nc = nc = bass.Bass("TRN2", target_bir_lowering=False)
        self.es = ExitStack()
        self.eng = dict(pe=nc.tensor, act=nc.scalar, dve=nc.vector, pool=nc.gpsimd, sp=nc.sync)
        self.sem = {e: self.es.enter_context(nc.semaphore("sem_" + e)) for e in self.eng}
        self.cnt = {e: 0 for e in self.eng}
        self.seen = {e: {} for e in self.eng}
        self.hist = {}
        self.dsem = [self.es.enter_context(nc.semaphore("dsem%d" % i)) for i in range(self.NSLOT)]
        self.dcnt = [0] * self.NSLOT
        self.dnext = 0
        self.out_tokens = []
        self.nps = 0
        self.started = False

    def dram(self, name, shape, dtype, kind):
        return self.nc.dram_tensor(name, list(shape), dtype, kind=kind).ap()

    def sb(self, name, shape, dtype=F32):
        return self.es.enter_context(self.nc.sbuf_tensor("sb_" + name, list(shape), dtype))

    def psum(self, name, shape, dtype=F32):
        return self.es.enter_context(self.nc.psum_tensor("pp_" + name, list(shape), dtype))

    def begin(self):
        self.es.enter_context(self.nc.Block())
        self.started = True

    def end(self):
        for tok in self.out_tokens:
            self._wait("sp", tok)
        self.es.close()
        return self.nc

    def _semh(self, key):
        return self.sem[key] if isinstance(key, str) else self.dsem[key]

    def _wait(self, e, tok):
        if tok is None:
            return
        key, val = tok
        s = self.seen[e]
        if s.get(key, 0) >= val:
            return
        if not (e == "pe" and key == "pe"):
            self.eng[e].wait_ge(self._semh(key), val)
        s[key] = val
        h = self.hist.get(tok)
        if h:
            for k2, v2 in h.items():
                if s.get(k2, 0) < v2:
                    s[k2] = v2

    def _deps(self, e, reads, writes):
        for b in reads:
            self._wait(e, b.w)
        for b in writes:
            self._wait(e, b.w)
            for k2, v2 in list(b.r.items()):
                self._wait(e, (k2, v2))

    def _commit(self, tok, e, reads, writes):
        self.hist[tok] = dict(self.seen[e])
        for b in reads:
            if b.r.get(tok[0], 0) < tok[1]:
                b.r[tok[0]] = tok[1]
        for b in writes:
            b.w = tok; b.r = {}

    def op(self, e, f, reads=(), writes=()):
        self._deps(e, reads, writes)
        ins = f()
        self.cnt[e] += 1
        ins.then_inc(self.sem[e], 1)
        tok = (e, self.cnt[e])
        self._commit(tok, e, reads, writes)
        return tok

    def dma(self, q, out_ap, in_ap, reads=(), writes=(), is_out=False):
        slot = self.dnext
        self.dnext = (self.dnext + 1) % self.NSLOT
        if self.dcnt[slot] > 0:
            self._wait(q, (slot, 16 * self.dcnt[slot]))
        self._deps(q, reads, writes)
        ins = self.eng[q].dma_start(out=out_ap, in_=in_ap)
        self.dcnt[slot] += 1
        ins.then_inc(self.dsem[slot], 16)
        tok = (slot, 16 * self.dcnt[slot])
        self._commit(tok, q, reads, writes)
        if is_out:
            self.out_tokens.append(tok)
        return tok

    def barrier(self):
        toks = [(e, self.cnt[e]) for e in self.eng if self.cnt[e] > 0]
        toks += [(i, 16 * self.dcnt[i]) for i in range(self.NSLOT) if self.dcnt[i] > 0]
        for e in self.eng:
            for t in toks:
                self._wait(e, t)


def run(kb_nc, in_maps):
    res = run_bass_kernel_spmd(kb_nc, in_maps, core_ids=list(range(NCORES)))
    return res.results


D = 1024; SEQ = 8192; CTX = 256; NB = 2
TOK = 2048; CT = 64; NT = TOK + CT
TBS = [(0, 512), (512, 512), (1024, 512), (1536, 512), (2048, 64)]
EPS = 1e-6
EVEN_IN = 3352; ODD_IN = 2816


def tb_col(i):
    return 1 if i == 4 else 0


def build_L0():
    kb = KB(); nc = kb.nc
    sT = kb.dram("sT", [128, 8, 3], F32, "ExternalInput")
    w = kb.dram("w", [2, 1024, 768], F32, "ExternalInput")
    bm = kb.dram("bm", [128, 2, 6], F32, "ExternalInput")
    out = kb.dram("out", [128, 2, 6, 3], F32, "ExternalOutput")
    s_t = kb.sb("s_t", [128, 8, 3]); s_b = Buf(s_t)
    ss_t = kb.sb("ss_t", [128, 8, 3]); ss_b = Buf(ss_t)
    w_t = kb.sb("w_t", [128, 2, 8, 768]); w_b = [Buf(w_t), Buf(w_t)]
    bm_t = kb.sb("bm_t", [128, 2, 6]); bm_b = Buf(bm_t)
    o_t = kb.sb("o_t", [128, 2, 6, 3]); o_b = Buf(o_t)
    ps_t = kb.psum("ps", [128, 2, 6, 4]); ps_b = Buf(ps_t)
    kb.begin()
    kb.dma("sp", s_t[:], sT, writes=[s_b])
    kb.dma("sp", bm_t[:], bm, writes=[bm_b])
    for l in range(2):
        kb.dma("sp", w_t[:, l], w[l].rearrange("(kc p) n -> p kc n", p=128), writes=[w_b[l]])
    kb.op("act", lambda: nc.scalar.activation(out=ss_t[:], in_=s_t[:], func=AF.Silu), reads=[s_b], writes=[ss_b])
    for l in range(2):
        for j in range(6):
            for kc in range(8):
                kb.op("pe", lambda l=l, j=j, kc=kc: nc.tensor.matmul(
                    ps_t[:, l, j, 0:3], w_t[:, l, kc, j * 128:(j + 1) * 128], ss_t[:, kc, :],
                    start=(kc == 0), stop=(kc == 7)), reads=[w_b[l], ss_b], writes=[ps_b])
    kb.op("dve", lambda: nc.vector.tensor_tensor(
        out=o_t[:], in0=ps_t[:, :, :, 0:3], in1=bm_t[:].unsqueeze(3).to_broadcast([128, 2, 6, 3]), op=ALU.add),
        reads=[ps_b, bm_b], writes=[o_b])
    kb.dma("sp", out, o_t[:], reads=[o_b], is_out=True)
    return kb.end()


def run_L0(inp):
    c = inp["c"]; c_ctx = inp["c_ctx"]
    s = np.stack([c[0], c[1], c_ctx], axis=1)
    sT = np.ascontiguousarray(s.reshape(8, 128, 3).transpose(1, 0, 2))
    in_maps = []
    for core in range(NCORES):
        w = np.ascontiguousarray(inp["w_mod"][:, :, core * 768:(core + 1) * 768])
        bm = inp["b_mod"][:, core * 768:(core + 1) * 768].reshape(2, 6, 128).transpose(2, 0, 1)
        in_maps.append({"sT": sT, "w": w, "bm": np.ascontiguousarray(bm)})
    res = run(build_L0(), in_maps)
    mod = np.zeros((2, 6144, 3), np.float32)
    for core in range(NCORES):
        o = res[core]["out"]
        mod[:, core * 768:(core + 1) * 768, :] = o.transpose(1, 2, 0, 3).reshape(2, 768, 3)
    return mod


def mod_for_core(mod, l, core):
    b = core // 4
    m = mod[l][:, [b, 2]]
    return np.ascontiguousarray(m.reshape(48, 128, 2).transpose(1, 0, 2))


class PsumRing:
    def __init__(self, kb, n=8, name="ps"):
        self.kb = kb
        self.t = [kb.psum("%s%d" % (name, i), [128, 512], F32) for i in range(n)]
        self.b = [Buf(t) for t in self.t]
        self.i = 0

    def next(self):
        b = self.b[self.i]
        self.i = (self.i + 1) % len(self.b)
        return b


def emit_modprep(kb, mod_t, mod_b, gn_t, gn_b, A_t, A_b, vscale):
    nc = kb.nc
    for col in range(2):
        kb.op("dve", lambda col=col: nc.vector.tensor_scalar(
            out=A_t[:, :, col], in0=mod_t[:, vscale * 8:(vscale + 1) * 8, col], scalar1=1.0, scalar2=32.0,
            op0=ALU.add, op1=ALU.mult), reads=[mod_b], writes=[A_b])
        kb.op("dve", lambda col=col: nc.vector.tensor_tensor(
            out=A_t[:, :, col], in0=A_t[:, :, col], in1=gn_t[:, :], op=ALU.mult), reads=[A_b, gn_b], writes=[A_b])


def emit_norm_block(kb, ring, i, x_t, xb, h_t, hb, sq_t, sqb, tmp_t, tmpb, rs_t, rsb, ones_t, ones_b,
                    A_t, A_b, mod_t, mod_b, vshift, eps_t, eps_b, hlo=None):
    nc = kb.nc
    lo, n = TBS[i]; col = tb_col(i)
    if hlo is None:
        hlo = lo
    sb = sqb[i % 2]; s_t = sq_t
    for kc in range(8):
        kb.op("act", lambda kc=kc: nc.scalar.activation(out=s_t[:, i % 2, kc, :n], in_=x_t[:, kc, lo:lo + n], func=AF.Square),
              reads=[xb], writes=[sb])
    ps = ring.next()
    for kc in range(8):
        kb.op("pe", lambda kc=kc: nc.tensor.matmul(ps.t[:, :n], ones_t[:, :], s_t[:, i % 2, kc, :n], start=(kc == 0), stop=(kc == 7)),
              reads=[sb, ones_b], writes=[ps])
    rb = rsb[i % 2]
    kb.op("act", lambda: nc.scalar.activation(out=rs_t[:, i % 2, :n], in_=ps.t[:, :n], func=AF.Sqrt, bias=eps_t[:, 0:1], scale=1.0),
          reads=[ps, eps_b], writes=[rb])
    kb.op("dve", lambda: nc.vector.reciprocal(out=rs_t[:, i % 2, :n], in_=rs_t[:, i % 2, :n]), reads=[rb], writes=[rb])
    for kc in range(8):
        tb_ = tmpb[kc % 2]
        kb.op("dve", lambda kc=kc: nc.vector.scalar_tensor_tensor(
            out=tmp_t[:, kc % 2, :n], in0=x_t[:, kc, lo:lo + n], scalar=A_t[:, kc, col:col + 1], in1=rs_t[:, i % 2, :n],
            op0=ALU.mult, op1=ALU.mult), reads=[xb, A_b, rb], writes=[tb_])
        kb.op("act", lambda kc=kc: nc.scalar.activation(
            out=h_t[:, kc, hlo:hlo + n], in_=tmp_t[:, kc % 2, :n], func=AF.Identity,
            bias=mod_t[:, vshift * 8 + kc, col:col + 1], scale=1.0), reads=[tb_, mod_b], writes=[hb])


def evac(kb, k, out_ap, in_ap, reads, writes):
    nc = kb.nc
    if k % 2 == 0:
        kb.op("act", lambda: nc.scalar.copy(out=out_ap, in_=in_ap), reads=reads, writes=writes)
    else:
        kb.op("dve", lambda: nc.vector.tensor_copy(out=out_ap, in_=in_ap), reads=reads, writes=writes)


def build_L1():
    kb = KB(); nc = kb.nc
    xT = kb.dram("xT", [128, 8, NT], F32, "ExternalInput")
    mod = kb.dram("mod", [128, 48, 2], F32, "ExternalInput")
    gn = kb.dram("gn", [128, 8], F32, "ExternalInput")
    w = kb.dram("w", [1024, EVEN_IN], F32, "ExternalInput")
    cbd = kb.dram("cbd", [128, 128], F32, "ExternalInput")
    sbd = kb.dram("sbd", [128, 128], F32, "ExternalInput")
    o_qkv = kb.dram("o_qkv", [128, 18, NT], BF16, "ExternalOutput")
    o_o = kb.dram("o_o", [128, 6, NT], F32, "ExternalOutput")
    o_g = kb.dram("o_g", [24, NT], F32, "ExternalOutput")
    o_fr = kb.dram("o_fr", [128, 2, NT], F32, "ExternalOutput")
    o_fi = kb.dram("o_fi", [128, 2, NT], F32, "ExternalOutput")

    x_t = kb.sb("x_t", [128, 8, NT]); xb = [Buf(x_t) for _ in TBS]
    h_t = kb.sb("h_t", [128, 8, NT], BF16); hb = [Buf(h_t) for _ in TBS]
    sq_t = kb.sb("sq_t", [128, 2, 8, 512], BF16); sqb = [Buf(sq_t), Buf(sq_t)]
    tmp_t = kb.sb("tmp_t", [128, 2, 512]); tmpb = [Buf(tmp_t), Buf(tmp_t)]
    rs_t = kb.sb("rs_t", [128, 2, 512]); rsb = [Buf(rs_t), Buf(rs_t)]
    mod_t = kb.sb("mod_t", [128, 48, 2]); mod_b = Buf(mod_t)
    gn_t = kb.sb("gn_t", [128, 8]); gn_b = Buf(gn_t)
    A_t = kb.sb("A_t", [128, 8, 2]); A_b = Buf(A_t)
    ones_t = kb.sb("ones_t", [128, 128], BF16); ones_b = Buf(ones_t)
    eps_t = kb.sb("eps_t", [128, 1]); eps_b = Buf(eps_t)
    cbd_t = kb.sb("cbd_t", [128, 128]); cbd_b = Buf(cbd_t)
    sbd_t = kb.sb("sbd_t", [128, 128]); sbd_b = Buf(sbd_t)
    NWB = 3
    wg_t = kb.sb("wg_t", [128, NWB, 8, 512], BF16); wgb = [Buf(wg_t) for _ in range(NWB)]
    NST = 4
    stb_t = kb.sb("stb_t", [128, NST, 512], BF16); stbb = [Buf(stb_t) for _ in range(NST)]
    stf_t = kb.sb("stf_t", [128, NST, 512]); stfb = [Buf(stf_t) for _ in range(NST)]
    f_t = kb.sb("f_t", [128, 2, NT]); fb = [[Buf(f_t) for _ in TBS] for _ in range(2)]
    ring = PsumRing(kb)
    kb.begin()

    kb.dma("sp", mod_t[:], mod, writes=[mod_b])
    kb.dma("sp", gn_t[:], gn, writes=[gn_b])
    kb.dma("sp", cbd_t[:], cbd, writes=[cbd_b])
    kb.dma("sp", sbd_t[:], sbd, writes=[sbd_b])
    for i, (lo, n) in enumerate(TBS):
        kb.dma("sp", x_t[:, :, lo:lo + n], xT[:, :, lo:lo + n], writes=[xb[i]])
    kb.op("pool", lambda: nc.gpsimd.memset(ones_t[:], 1.0), writes=[ones_b])
    kb.op("pool", lambda: nc.gpsimd.memset(eps_t[:], 1024.0 * EPS), writes=[eps_b])
    groups = [(c0, 512) for c0 in range(0, 3072, 512)] + [(3072, 280)]
    gtok = {}

    def load_group(gi):
        c0, gw = groups[gi]
        gtok[gi] = kb.dma("pool", wg_t[:, gi % NWB, :, :gw], w[:, c0:c0 + gw].rearrange("(kc p) n -> p kc n", p=128),
                          writes=[wgb[gi % NWB]])
    load_group(0); load_group(1)
    emit_modprep(kb, mod_t, mod_b, gn_t, gn_b, A_t, A_b, vscale=1)
    for i in range(len(TBS)):
        emit_norm_block(kb, ring, i, x_t, xb[i], h_t, hb[i], sq_t, sqb, tmp_t, tmpb, rs_t, rsb, ones_t, ones_b,
                        A_t, A_b, mod_t, mod_b, 0, eps_t, eps_b)
    k = 0
    for gi, (c0, gw) in enumerate(groups):
        if gi + 2 < len(groups):
            load_group(gi + 2)
        if c0 < 3072:
            chunks = [(cl, 128) for cl in range(0, gw, 128)]
        else:
            chunks = [(0, 24), (24, 128), (152, 128)]
        for (cl, m) in chunks:
            c = c0 + cl
            for i, (lo, n) in enumerate(TBS):
                ps = ring.next()
                for kc in range(8):
                    kb.op("pe", lambda kc=kc, cl=cl, m=m, lo=lo, n=n, ps=ps: nc.tensor.matmul(
                        ps.t[:m, :n], wg_t[:, gi % NWB, kc, cl:cl + m], h_t[:, kc, lo:lo + n], start=(kc == 0), stop=(kc == 7)),
                        reads=[wgb[gi % NWB], hb[i]], writes=[ps])
                k += 1
                if c < 2304:
                    sb_ = stbb[k % NST]
                    evac(kb, k, stb_t[:, k % NST, :n], ps.t[:, :n], [ps], [sb_])
                    kb.dma("sp", o_qkv[:, c // 128, lo:lo + n], stb_t[:, k % NST, :n], reads=[sb_], is_out=True)
                elif c < 3072:
                    sb_ = stfb[k % NST]
                    evac(kb, k, stf_t[:, k % NST, :n], ps.t[:, :n], [ps], [sb_])
                    kb.dma("sp", o_o[:, (c - 2304) // 128, lo:lo + n], stf_t[:, k % NST, :n], reads=[sb_], is_out=True)
                elif c == 3072:
                    sb_ = stfb[k % NST]
                    evac(kb, k, stf_t[:24, k % NST, :n], ps.t[:24, :n], [ps], [sb_])
                    kb.dma("sp", o_g[:, lo:lo + n], stf_t[:24, k % NST, :n], reads=[sb_], is_out=True)
                else:
                    j = (c - 3096) // 128
                    evac(kb, k, f_t[:, j, lo:lo + n], ps.t[:, :n], [ps], [fb[j][i]])
                    for (tab_t, tab_b, dst) in ((cbd_t, cbd_b, o_fr), (sbd_t, sbd_b, o_fi)):
                        ps2 = ring.next()
                        kb.op("pe", lambda tab_t=tab_t, ps2=ps2, j=j, lo=lo, n=n: nc.tensor.matmul(
                            ps2.t[:, :n], tab_t[:, :], f_t[:, j, lo:lo + n], start=True, stop=True),
                            reads=[tab_b, fb[j][i]], writes=[ps2])
                        k += 1
                        sb_ = stfb[k % NST]
                        evac(kb, k, stf_t[:, k % NST, :n], ps2.t[:, :n], [ps2], [sb_])
                        kb.dma("sp", dst[:, j, lo:lo + n], stf_t[:, k % NST, :n], reads=[sb_], is_out=True)
    return kb.end()


def xT_for_core(x, xc, core):
    b, q = core // 4, core % 4
    t = np.concatenate([x[b, q * TOK:(q + 1) * TOK], xc[b, q * CT:(q + 1) * CT]], axis=0)
    return np.ascontiguousarray(t.reshape(NT, 8, 128).transpose(2, 1, 0))


def fm_vec(v, nchunk):
    return np.ascontiguousarray(np.asarray(v).reshape(nchunk, 128).T)


def dft_bd():
    c = np.arange(64)
    ang = 2 * np.pi * np.outer(c, c) / 64
    cb = np.zeros((128, 128), np.float32); sb = np.zeros((128, 128), np.float32)
    for g in range(2):
        cb[g * 64:(g + 1) * 64, g * 64:(g + 1) * 64] = np.cos(ang)
        sb[g * 64:(g + 1) * 64, g * 64:(g + 1) * 64] = -np.sin(ang)
    return cb, sb


def run_L1(inp, mod, xTs):
    cb, sb = dft_bd()
    w = np.ascontiguousarray(inp["w_in_even"][0])
    gn = fm_vec(inp["g_norm"][0, 0], 8)
    in_maps = [{"xT": xTs[c], "mod": mod_for_core(mod, 0, c), "gn": gn, "w": w, "cbd": cb, "sbd": sb} for c in range(NCORES)]
    return run(build_L1(), in_maps)


NCH = 66; SL = NCH * 128
ALPHA = 128 ** -0.5


def build_L2(U=3, do_mlstm=True, do_fourier=True, phase=9, nscan=NCH):
    kb = KB(); nc = kb.nc
    qTd = [kb.dram("qT%d" % u, [128, SL], BF16, "ExternalInput") for u in range(U)]
    kTd = [kb.dram("kT%d" % u, [128, SL], BF16, "ExternalInput") for u in range(U)]
    vd = [kb.dram("v%d" % u, [128, NCH, 128], BF16, "ExternalInput") for u in range(U)]
    gid = [kb.dram("gi%d" % u, [128, NCH], F32, "ExternalInput") for u in range(U)]
    gfd = [kb.dram("gf%d" % u, [128, NCH], F32, "ExternalInput") for u in range(U)]
    miscd = [kb.dram("misc%d" % u, [128, 8], F32, "ExternalInput") for u in range(U)]
    trid = kb.dram("tri", [128, 128], F32, "ExternalInput")
    maskd = kb.dram("mask", [128, 128], F32, "ExternalInput")
    identfd = kb.dram("identf", [128, 128], F32, "ExternalInput")
    identbd = kb.dram("identb", [128, 128], BF16, "ExternalInput")
    hTd = [kb.dram("hT%d" % u, [128, SL], F32, "ExternalOutput") for u in range(U)]
    XRd = kb.dram("XR", [128, 4096], F32, "ExternalInput"); XId = kb.dram("XI", [128, 4096], F32, "ExternalInput")
    c128d = kb.dram("c128", [128, 128], F32, "ExternalInput"); s128d = kb.dram("s128", [128, 128], F32, "ExternalInput")
    ns128d = kb.dram("ns128", [128, 128], F32, "ExternalInput")
    tcd = kb.dram("tc", [128, 64], F32, "ExternalInput"); tsd = kb.dram("ts", [128, 64], F32, "ExternalInput")
    c64d = kb.dram("c64", [64, 64], F32, "ExternalInput"); s64d = kb.dram("s64", [64, 64], F32, "ExternalInput")
    XcRd = kb.dram("XcR", [128, 2, 64], F32, "ExternalInput"); XcId = kb.dram("XcI", [128, 2, 64], F32, "ExternalInput")
    c256d = kb.dram("c256", [128, 2, 256], F32, "ExternalInput"); s256d = kb.dram("s256", [128, 2, 256], F32, "ExternalInput")
    Yd = kb.dram("Y", [64, 8192], F32, "ExternalOutput"); Ycd = kb.dram("Yc", [128, 2, 64], F32, "ExternalOutput")

    def sbb(name, shape, dt=F32):
        t = kb.sb(name, shape, dt); return t, Buf(t)
    qr_t, qr_b = sbb("qr", [128, SL], BF16); kr_t, kr_b = sbb("kr", [128, SL], BF16)
    qs_t, qs_b = sbb("qs", [128, SL], BF16); ks_t, ks_b = sbb("ks", [128, SL], BF16)
    kh_t, kh_b = sbb("kh", [128, NCH, 128], BF16); v_t, v_b = sbb("v", [128, NCH, 128], BF16)
    gi_t, gi_b = sbb("gi", [128, NCH]); gf_t, gf_b = sbb("gf", [128, NCH]); misc_t, misc_b = sbb("misc", [128, 8])
    nbf_t, nbf_b = sbb("nbf", [128, 1]); l1_t, l1_b = sbb("l1", [128, NCH]); nbc_t, nbc_b = sbb("nbcol", [128, NCH])
    t1_t, t1_b = sbb("t1", [128, NCH]); a_t, a_b = sbb("acol", [128, NCH]); eL_t, eL_b = sbb("eL", [128, NCH])
    gh_t, gh_b = sbb("gh", [128, NCH])
    tri_t, tri_b = sbb("tri", [128, 128]); mask_t, mask_b = sbb("mask", [128, 128])
    idf_t, idf_b = sbb("idf", [128, 128]); idb_t, idb_b = sbb("idb", [128, 128], BF16)
    onef_t, onef_b = sbb("onef", [128, 128]); oneb_t, oneb_b = sbb("oneb", [128, 128], BF16)
    ctmp_t = kb.sb("ctmp", [128, 2, 1024]); ctmp_b = [Buf(ctmp_t), Buf(ctmp_t)]
    st_t = kb.sb("st", [128, 4, 128], BF16); st_b = [Buf(st_t) for _ in range(4)]
    dg_t = kb.sb("dg", [128, 2, 128]); dg_b = [Buf(dg_t), Buf(dg_t)]
    neb_t = kb.sb("neb", [128, 4, 128]); neb_b = [Buf(neb_t) for _ in range(4)]
    dm_t = kb.sb("dm", [128, 2, 128]); dm_b = [Buf(dm_t), Buf(dm_t)]
    Cf_t, Cf_b = sbb("Cf", [128, 128]); Cb_t, Cb_b = sbb("Cb", [128, 128], BF16)
    nf_t, nf_b = sbb("nf", [128, 128]); nbb_t, nbb_b = sbb("nb16", [128, 128], BF16)
    ho_t = kb.sb("ho", [128, 2, 1024]); ho_b = [Buf(ho_t), Buf(ho_t)]
    XR_t = qr_t[:, 0:8192].bitcast(F32); XI_t = kr_t[:, 0:8192].bitcast(F32)
    Ur_t = qs_t[:, 0:8192].bitcast(F32); Ui_t = ks_t[:, 0:8192].bitcast(F32)
    XR_b, XI_b, Ur_b, Ui_b = Buf(XR_t), Buf(XI_t), Buf(Ur_t), Buf(Ui_t)
    c128_t, c128_b = sbb("c128", [128, 128]); s128_t, s128_b = sbb("s128", [128, 128]); ns128_t, ns128_b = sbb("ns128", [128, 128])
    tc_t, tc_b = sbb("tc", [128, 64]); ts_t, ts_b = sbb("ts", [128, 64])
    c64_t, c64_b = sbb("c64", [64, 64]); s64_t, s64_b = sbb("s64", [64, 64])
    XcR_t, XcR_b = sbb("XcR", [128, 2, 64]); XcI_t, XcI_b = sbb("XcI", [128, 2, 64])
    c256_t, c256_b = sbb("c256", [128, 2, 256]); s256_t, s256_b = sbb("s256", [128, 2, 256])
    ft_t = kb.sb("ft", [128, 4, 512]); ft_b = [Buf(ft_t) for _ in range(4)]
    Z_t = kb.sb("Z", [64, 4, 512]); Z_b = [Buf(Z_t) for _ in range(4)]
    Yo_t = kb.sb("Yo", [64, 2, 512]); Yo_b = [Buf(Yo_t), Buf(Yo_t)]; Yc_t, Yc_b = sbb("Yco", [128, 2, 64])
    pst = [kb.psum("pb%d" % i, [128, 512], F32) for i in range(7)]
    ptr_t = kb.psum("ptr", [128, 1024], BF16)
    kb.begin()

    for (t, b, d) in ((tri_t, tri_b, trid), (mask_t, mask_b, maskd), (idf_t, idf_b, identfd), (idb_t, idb_b, identbd)):
        kb.dma("sp", t[:], d, writes=[b])
    kb.op("pool", lambda: nc.gpsimd.memset(onef_t[:], 1.0), writes=[onef_b])
    kb.op("pool", lambda: nc.gpsimd.memset(oneb_t[:], 1.0), writes=[oneb_b])

    def slots(bank):
        b = Buf(pst[bank]); return [b, b, b, b]
    S_ps, dC_ps, dn_ps, nbc_ps, num_ps, den_ps = (slots(i) for i in range(6))
    g_ps = Buf(pst[6])
    tr_ps = [Buf(ptr_t) for _ in range(8)]

    def sl(s):
        return slice(s * 128, (s + 1) * 128)

    for u in range(U if do_mlstm else 0):
        kb.dma("sp", qr_t[:], qTd[u], writes=[qr_b])
        kb.dma("sp", kr_t[:], kTd[u], writes=[kr_b])
        kb.dma("sp", gi_t[:], gid[u], writes=[gi_b]); kb.dma("sp", gf_t[:], gfd[u], writes=[gf_b])
        kb.dma("sp", misc_t[:], miscd[u], writes=[misc_b])
        kb.dma("sp", v_t[:], vd[u], writes=[v_b])
        kb.op("dve", lambda: nc.vector.tensor_scalar(out=nbf_t[:], in0=misc_t[:, 7:8], scalar1=-1.0, scalar2=None, op0=ALU.mult),
              reads=[misc_b], writes=[nbf_b])
        kb.op("act", lambda: nc.scalar.activation(out=l1_t[:], in_=gf_t[:], func=AF.Exp, bias=nbf_t[:, 0:1], scale=-1.0),
              reads=[gf_b, nbf_b], writes=[l1_b])
        kb.op("act", lambda: nc.scalar.activation(out=l1_t[:], in_=l1_t[:], func=AF.Ln, bias=onef_t[:, 0:1], scale=1.0),
              reads=[l1_b, onef_b], writes=[l1_b])
        kb.op("pe", lambda: nc.tensor.matmul(pst[6][:, 0:NCH], tri_t[:, :], l1_t[:, :], start=True, stop=True),
              reads=[tri_b, l1_b], writes=[g_ps])
        kb.op("pe", lambda: nc.tensor.matmul(pst[6][:, 128:128 + NCH], onef_t[:, :], l1_t[:, :], start=True, stop=True),
              reads=[onef_b, l1_b], writes=[g_ps])
        kb.op("dve", lambda: nc.vector.tensor_copy(out=nbc_t[:], in_=pst[6][:, 0:NCH]), reads=[g_ps], writes=[nbc_b])
        kb.op("dve", lambda: nc.vector.scalar_tensor_tensor(out=t1_t[:], in0=gi_t[:], scalar=misc_t[:, 6:7], in1=nbc_t[:],
                                                            op0=ALU.add, op1=ALU.add), reads=[gi_b, misc_b, nbc_b], writes=[t1_b])
        kb.op("act", lambda: nc.scalar.activation(out=a_t[:], in_=t1_t[:], func=AF.Exp), reads=[t1_b], writes=[a_b])
        kb.op("act", lambda: nc.scalar.activation(out=eL_t[:], in_=pst[6][:, 128:128 + NCH], func=AF.Exp, scale=-1.0),
              reads=[g_ps], writes=[eL_b])
        kb.op("dve", lambda: nc.vector.scalar_tensor_tensor(out=gh_t[:], in0=a_t[:], scalar=ALPHA, in1=eL_t[:],
                                                            op0=ALU.mult, op1=ALU.mult), reads=[a_b, eL_b], writes=[gh_b])
        if phase < 2:
            continue
        blocks = [(0, 256, 0, 256)] + [(256 + j * 1024, 256 + (j + 1) * 1024, 256, SL) for j in range(8)]
        bi = 0
        for (raw_t, raw_b, o_t, o_b, tb0) in ((qr_t, qr_b, qs_t, qs_b, 0), (kr_t, kr_b, ks_t, ks_b, 3)):
            for (lo, hi, a, b) in blocks:
                n = hi - lo; cb = ctmp_b[bi % 2]; ci = bi % 2; bi += 1
                kb.op("act", lambda raw_t=raw_t, lo=lo, hi=hi, n=n, ci=ci, tb0=tb0: nc.scalar.activation(
                    out=ctmp_t[:, ci, :n], in_=raw_t[:, lo:hi], func=AF.Copy, scale=misc_t[:, tb0 + 1:tb0 + 2]),
                    reads=[raw_b, misc_b], writes=[cb])
                j0 = max(lo, a + 1)
                kb.op("dve", lambda raw_t=raw_t, lo=lo, hi=hi, n=n, ci=ci, tb0=tb0, j0=j0: nc.vector.scalar_tensor_tensor(
                    out=ctmp_t[:, ci, j0 - lo:n], in0=raw_t[:, j0 - 1:hi - 1], scalar=misc_t[:, tb0:tb0 + 1],
                    in1=ctmp_t[:, ci, j0 - lo:n], op0=ALU.mult, op1=ALU.add), reads=[raw_b, misc_b, cb], writes=[cb])
                j1 = min(hi, b - 1)
                kb.op("dve", lambda raw_t=raw_t, lo=lo, n=n, ci=ci, tb0=tb0, j1=j1: nc.vector.scalar_tensor_tensor(
                    out=ctmp_t[:, ci, 0:j1 - lo], in0=raw_t[:, lo + 1:j1 + 1], scalar=misc_t[:, tb0 + 2:tb0 + 3],
                    in1=ctmp_t[:, ci, 0:j1 - lo], op0=ALU.mult, op1=ALU.add), reads=[raw_b, misc_b, cb], writes=[cb])
                kb.op("act", lambda o_t=o_t, lo=lo, hi=hi, n=n, ci=ci: nc.scalar.activation(
                    out=o_t[:, lo:hi], in_=ctmp_t[:, ci, :n], func=AF.Silu), reads=[cb], writes=[o_b])
        if phase < 3:
            continue
        for c in range(NCH):
            tp = num_ps[c % 4]
            kb.op("pe", lambda c=c: nc.tensor.matmul(pst[4][:, sl(0)], ks_t[:, sl(c)], idb_t[:, :], start=True, stop=True),
                  reads=[ks_b, idb_b], writes=[tp])
            kb.op("dve", lambda c=c: nc.vector.tensor_scalar(out=kh_t[:, c, :], in0=pst[4][:, sl(0)],
                                                             scalar1=gh_t[:, c:c + 1], scalar2=None, op0=ALU.mult),
                  reads=[tp, gh_b], writes=[kh_b])
        if phase < 4:
            continue
        kb.op("pool", lambda: nc.gpsimd.memset(Cf_t[:], 0.0), writes=[Cf_b])
        kb.op("pool", lambda: nc.gpsimd.memset(Cb_t[:], 0.0), writes=[Cb_b])
        kb.op("pool", lambda: nc.gpsimd.memset(nf_t[:], 0.0), writes=[nf_b])
        kb.op("pool", lambda: nc.gpsimd.memset(nbb_t[:], 0.0), writes=[nbb_b])

        def stage_A(c):
            s = c % 4
            kb.op("pe", lambda: nc.tensor.matmul(pst[0][:, sl(0)], ks_t[:, sl(c)], qs_t[:, sl(c)], start=True, stop=True),
                  reads=[ks_b, qs_b], writes=[S_ps[s]])
            kb.op("dve", lambda: nc.vector.scalar_tensor_tensor(out=st_t[:, s, :], in0=pst[0][:, sl(0)], scalar=a_t[:, c:c + 1],
                                                                in1=mask_t[:, :], op0=ALU.mult, op1=ALU.mult),
                  reads=[S_ps[s], a_b, mask_b], writes=[st_b[s]])
            kb.op("dve", lambda: nc.vector.tensor_scalar(out=dg_t[:, c % 2, :], in0=idf_t[:, :], scalar1=nbc_t[:, c:c + 1], scalar2=None,
                                                         op0=ALU.mult), reads=[idf_b, nbc_b], writes=[dg_b[c % 2]])
            kb.op("pe", lambda: nc.tensor.matmul(pst[3][:, sl(0)], onef_t[:, :], dg_t[:, c % 2, :], start=True, stop=True),
                  reads=[onef_b, dg_b[c % 2]], writes=[nbc_ps[s]])
            kb.op("act", lambda: nc.scalar.activation(out=neb_t[:, s, :], in_=pst[3][:, sl(0)], func=AF.Exp),
                  reads=[nbc_ps[s]], writes=[neb_b[s]])

        def stage_B(c):
            s = c % 4
            kb.op("pe", lambda: nc.tensor.matmul(pst[4][:, sl(0)], v_t[:, c, :], st_t[:, s, :], start=True, stop=False),
                  reads=[v_b, st_b[s]], writes=[num_ps[s]])
            kb.op("pe", lambda: nc.tensor.matmul(pst[4][:, sl(0)], Cb_t[:, :], qs_t[:, sl(c)], start=False, stop=True),
                  reads=[Cb_b, qs_b], writes=[num_ps[s]])
            kb.op("pe", lambda: nc.tensor.matmul(pst[5][:, sl(0)], oneb_t[:, :], st_t[:, s, :], start=True, stop=False),
                  reads=[oneb_b, st_b[s]], writes=[den_ps[s]])
            kb.op("pe", lambda: nc.tensor.matmul(pst[5][:, sl(0)], nbb_t[:, :], qs_t[:, sl(c)], start=False, stop=True),
                  reads=[nbb_b, qs_b], writes=[den_ps[s]])
            d = c % 2
            kb.op("act", lambda: nc.scalar.activation(out=dm_t[:, d, :], in_=pst[5][:, sl(0)], func=AF.Abs), reads=[den_ps[s]], writes=[dm_b[d]])
            kb.op("dve", lambda: nc.vector.tensor_tensor(out=dm_t[:, d, :], in0=dm_t[:, d, :], in1=neb_t[:, s, :], op=ALU.max),
                  reads=[dm_b[d], neb_b[s]], writes=[dm_b[d]])
            kb.op("dve", lambda: nc.vector.reciprocal(out=dm_t[:, d, :], in_=dm_t[:, d, :]), reads=[dm_b[d]], writes=[dm_b[d]])
            g = c // 8; hb_ = ho_b[g % 2]
            kb.op("dve", lambda: nc.vector.tensor_tensor(out=ho_t[:, g % 2, (c % 8) * 128:(c % 8 + 1) * 128], in0=pst[4][:, sl(0)],
                                                         in1=dm_t[:, d, :], op=ALU.mult), reads=[num_ps[s], dm_b[d]], writes=[hb_])
            if c % 8 == 7 or c == NCH - 1:
                w = (c % 8 + 1) * 128
                kb.dma("sp", hTd[u][:, g * 1024:g * 1024 + w], ho_t[:, g % 2, :w], reads=[hb_], is_out=True)

        def stage_U(c):
            s = c % 4
            kb.op("pe", lambda: nc.tensor.matmul(pst[1][:, sl(0)], kh_t[:, c, :], v_t[:, c, :], start=True, stop=True),
                  reads=[kh_b, v_b], writes=[dC_ps[s]])
            kb.op("pe", lambda: nc.tensor.matmul(pst[2][:, sl(0)], kh_t[:, c, :], oneb_t[:, :], start=True, stop=True),
                  reads=[kh_b, oneb_b], writes=[dn_ps[s]])
            kb.op("dve", lambda: nc.vector.scalar_tensor_tensor(out=Cf_t[:, :], in0=Cf_t[:, :], scalar=eL_t[:, c:c + 1], in1=pst[1][:, sl(0)],
                                                                op0=ALU.mult, op1=ALU.add), reads=[Cf_b, eL_b, dC_ps[s]], writes=[Cf_b])
            kb.op("act", lambda: nc.scalar.copy(out=Cb_t[:, :], in_=Cf_t[:, :]), reads=[Cf_b], writes=[Cb_b])
            kb.op("dve", lambda: nc.vector.scalar_tensor_tensor(out=nf_t[:, :], in0=nf_t[:, :], scalar=eL_t[:, c:c + 1], in1=pst[2][:, sl(0)],
                                                                op0=ALU.mult, op1=ALU.add), reads=[nf_b, eL_b, dn_ps[s]], writes=[nf_b])
            kb.op("act", lambda: nc.scalar.copy(out=nbb_t[:, :], in_=nf_t[:, :]), reads=[nf_b], writes=[nbb_b])

        stage_A(0)
        for c in range(nscan):
            if c + 1 < nscan:
                stage_A(c + 1)
            stage_B(c)
            stage_U(c)

    kb.barrier()
    if not do_fourier:
        return kb.end()
    fps = [Buf(pst[i]) for i in range(7)]
    for (t, b, d) in ((XR_t, XR_b, XRd), (XI_t, XI_b, XId), (c128_t, c128_b, c128d), (s128_t, s128_b, s128d), (ns128_t, ns128_b, ns128d),
                      (tc_t, tc_b, tcd), (ts_t, ts_b, tsd), (c64_t, c64_b, c64d), (s64_t, s64_b, s64d), (XcR_t, XcR_b, XcRd),
                      (XcI_t, XcI_b, XcId), (c256_t, c256_b, c256d), (s256_t, s256_b, s256d)):
        kb.dma("sp", t[:], d, writes=[b])
    pi = 0
    for blk in range(8):
        cs = slice(blk * 512, (blk + 1) * 512)
        pr = fps[pi % 7]; pi += 1; pim = fps[pi % 7]; pi += 1
        prt, pit = pr.t, pim.t
        kb.op("pe", lambda: nc.tensor.matmul(prt[:, :], c128_t[:, :], XR_t[:, cs], start=True, stop=False), reads=[c128_b, XR_b], writes=[pr])
        kb.op("pe", lambda: nc.tensor.matmul(prt[:, :], s128_t[:, :], XI_t[:, cs], start=False, stop=True), reads=[s128_b, XI_b], writes=[pr])
        kb.op("pe", lambda: nc.tensor.matmul(pit[:, :], c128_t[:, :], XI_t[:, cs], start=True, stop=False), reads=[c128_b, XI_b], writes=[pim])
        kb.op("pe", lambda: nc.tensor.matmul(pit[:, :], ns128_t[:, :], XR_t[:, cs], start=False, stop=True), reads=[ns128_b, XR_b], writes=[pim])
        tcb = tc_t[:, blk * 8:(blk + 1) * 8].unsqueeze(2).to_broadcast([128, 8, 64])
        tsb = ts_t[:, blk * 8:(blk + 1) * 8].unsqueeze(2).to_broadcast([128, 8, 64])

        def v3(ap):
            return ap.rearrange("p (a b) -> p a b", b=64)
        for (o_t, o_b, pa, pa_b, pb, pb_b, op2) in ((Ur_t, Ur_b, prt, pr, pit, pim, ALU.add), (Ui_t, Ui_b, pit, pim, prt, pr, ALU.subtract)):
            kb.op("dve", lambda pa=pa: nc.vector.tensor_tensor(out=v3(ft_t[:, 0, :]), in0=v3(pa[:, :]), in1=tcb, op=ALU.mult),
                  reads=[pa_b, tc_b], writes=[ft_b[0]])
            kb.op("dve", lambda pb=pb: nc.vector.tensor_tensor(out=v3(ft_t[:, 1, :]), in0=v3(pb[:, :]), in1=tsb, op=ALU.mult),
                  reads=[pb_b, ts_b], writes=[ft_b[1]])
            kb.op("pool", lambda o_t=o_t, op2=op2: nc.gpsimd.tensor_tensor(out=o_t[:, cs], in0=ft_t[:, 0, :], in1=ft_t[:, 1, :], op=op2),
                  reads=[ft_b[0], ft_b[1]], writes=[o_b])
    for j in range(16):
        zr = fps[pi % 7]; pi += 1; zi = fps[pi % 7]; pi += 1
        for (U_t, U_b, z) in ((Ur_t, Ur_b, zr), (Ui_t, Ui_b, zi)):
            for q in range(4):
                cp = 4 * j + q
                kb.op("pe", lambda U_t=U_t, z=z, q=q, cp=cp: nc.tensor.matmul(
                    z.t[:64, q * 128:(q + 1) * 128], U_t[:, :].rearrange("p (t c) -> p t c", c=64)[:, :, cp], idf_t[:, :], start=True, stop=True),
                    reads=[U_b, idf_b], writes=[z])
        zs = (2 * j) % 4
        kb.op("act", lambda: nc.scalar.copy(out=Z_t[:, zs, :], in_=zr.t[:64, :]), reads=[zr], writes=[Z_b[zs]])
        kb.op("dve", lambda: nc.vector.tensor_copy(out=Z_t[:, zs + 1, :], in_=zi.t[:64, :]), reads=[zi], writes=[Z_b[zs + 1]])
        yp = fps[pi % 7]; pi += 1
        kb.op("pe", lambda: nc.tensor.matmul(yp.t[:64, :], c64_t[:, :], Z_t[:, zs, :], start=True, stop=False), reads=[c64_b, Z_b[zs]], writes=[yp])
        kb.op("pe", lambda: nc.tensor.matmul(yp.t[:64, :], s64_t[:, :], Z_t[:, zs + 1, :], start=False, stop=True), reads=[s64_b, Z_b[zs + 1]], writes=[yp])
        evac(kb, j, Yo_t[:, j % 2, :], yp.t[:64, :], [yp], [Yo_b[j % 2]])
        kb.dma("sp", Yd[:, j * 512:(j + 1) * 512], Yo_t[:, j % 2, :], reads=[Yo_b[j % 2]], is_out=True)
    for k2 in range(2):
        yp = fps[pi % 7]; pi += 1
        n = 0
        for (tab_t, tab_b, X_t, X_b) in ((c256_t, c256_b, XcR_t, XcR_b), (s256_t, s256_b, XcI_t, XcI_b)):
            for tcn in range(2):
                kb.op("pe", lambda tab_t=tab_t, X_t=X_t, tcn=tcn, n=n: nc.tensor.matmul(
                    yp.t[:, :64], tab_t[:, tcn, k2 * 128:(k2 + 1) * 128], X_t[:, tcn, :], start=(n == 0), stop=(n == 3)),
                    reads=[tab_b, X_b], writes=[yp])
                n += 1
        kb.op("dve", lambda: nc.vector.tensor_copy(out=Yc_t[:, k2, :], in_=yp.t[:, :64]), reads=[yp], writes=[Yc_b])
    kb.dma("sp", Ycd, Yc_t[:], reads=[Yc_b], is_out=True)
    return kb.end()


def gather_fm(res, key, nrows_chunks, dtype):
    lat = [np.zeros((nrows_chunks * 128, SEQ), dtype) for _ in range(NB)]
    ctx = [np.zeros((nrows_chunks * 128, CTX), dtype) for _ in range(NB)]
    for core in range(NCORES):
        b, q = core // 4, core % 4
        a = res[core][key]
        a = a.transpose(1, 0, 2).reshape(nrows_chunks * 128, NT)
        lat[b][:, q * TOK:(q + 1) * TOK] = a[:, :TOK]
        ctx[b][:, q * CT:(q + 1) * CT] = a[:, TOK:]
    return lat, ctx


def l2_consts():
    s = np.arange(128)
    tri = (s[:, None] <= s[None, :]).astype(np.float32)
    mask = tri * np.float32(ALPHA)
    ident = np.eye(128, dtype=np.float32)
    t1 = np.arange(128)
    a128 = 2 * np.pi * np.outer(t1, t1) / 128
    k1 = np.arange(128)[:, None]; t2 = np.arange(64)[None, :]
    atw = 2 * np.pi * k1 * t2 / SEQ
    a64 = 2 * np.pi * np.outer(np.arange(64), np.arange(64)) / 64
    sc = 1.0 / np.sqrt(SEQ * 64.0)
    t = np.arange(256); a256 = 2 * np.pi * np.outer(t, t) / 256
    scc = 1.0 / np.sqrt(256 * 64.0)
    c256 = (np.cos(a256) * scc).reshape(2, 128, 256).transpose(1, 0, 2)
    s256 = (np.sin(a256) * scc).reshape(2, 128, 256).transpose(1, 0, 2)
    f = lambda x: np.ascontiguousarray(x, dtype=np.float32)
    return dict(tri=tri, mask=mask, identf=ident, identb=ident.astype(NPBF),
                c128=f(np.cos(a128)), s128=f(np.sin(a128)), ns128=f(-np.sin(a128)),
                tc=f(np.cos(atw)), ts=f(np.sin(atw)), c64=f(np.cos(a64) * sc), s64=f(np.sin(a64) * sc),
                c256=f(c256), s256=f(s256))


def run_L2(inp, r1):
    qkv_l, qkv_c = gather_fm(r1, "o_qkv", 18, NPBF)
    fr_l, fr_c = gather_fm(r1, "o_fr", 2, np.float32)
    fi_l, fi_c = gather_fm(r1, "o_fi", 2, np.float32)
    G_l = [np.zeros((24, SEQ), np.float32) for _ in range(NB)]; G_c = [np.zeros((24, CTX), np.float32) for _ in range(NB)]
    for core in range(NCORES):
        b, q = core // 4, core % 4
        g = r1[core]["o_g"]
        G_l[b][:, q * TOK:(q + 1) * TOK] = g[:, :TOK]; G_c[b][:, q * CT:(q + 1) * CT] = g[:, TOK:]
    consts = l2_consts()
    bg = inp["b_gate"][0]; wc = inp["w_qk_conv"][0]

    def seqcat(c, l, rev):
        if rev:
            c = c[..., ::-1]; l = l[..., ::-1]
        return np.concatenate([c, l], axis=-1)
    in_maps = []
    for core in range(NCORES):
        m = dict(consts)
        for u in range(3):
            uid = core * 3 + u
            b, hd, dr = uid // 12, (uid % 12) // 2, uid % 2
            rows = lambda base: slice(base + hd * 128, base + (hd + 1) * 128)
            m["qT%d" % u] = np.ascontiguousarray(seqcat(qkv_c[b][rows(0)], qkv_l[b][rows(0)], dr))
            m["kT%d" % u] = np.ascontiguousarray(seqcat(qkv_c[b][rows(768)], qkv_l[b][rows(768)], dr))
            vT = seqcat(qkv_c[b][rows(1536)], qkv_l[b][rows(1536)], dr)
            m["v%d" % u] = np.ascontiguousarray(vT.T.reshape(NCH, 128, 128).transpose(1, 0, 2))
            gi = seqcat(G_c[b][dr * 12 + hd], G_l[b][dr * 12 + hd], dr)
            gf = seqcat(G_c[b][dr * 12 + 6 + hd], G_l[b][dr * 12 + 6 + hd], dr)
            m["gi%d" % u] = np.ascontiguousarray(gi.reshape(NCH, 128).T)
            m["gf%d" % u] = np.ascontiguousarray(gf.reshape(NCH, 128).T)
            misc = np.zeros((128, 8), np.float32)
            qt = wc[:, hd * 128:(hd + 1) * 128]; kt = wc[:, 768 + hd * 128:768 + (hd + 1) * 128]
            if dr:
                qt = qt[::-1]; kt = kt[::-1]
            misc[:, 0:3] = qt.T; misc[:, 3:6] = kt.T
            misc[:, 6] = bg[dr * 12 + hd]; misc[:, 7] = bg[dr * 12 + 6 + hd]
            m["misc%d" % u] = misc
        b, grp = core // 4, core % 4
        rows = slice(grp * 64, (grp + 1) * 64)
        m["XR"] = np.ascontiguousarray(fr_l[b][rows].T.reshape(128, 4096))
        m["XI"] = np.ascontiguousarray(fi_l[b][rows].T.reshape(128, 4096))
        m["XcR"] = np.ascontiguousarray(fr_c[b][rows].T.reshape(2, 128, 64).transpose(1, 0, 2))
        m["XcI"] = np.ascontiguousarray(fi_c[b][rows].T.reshape(2, 128, 64).transpose(1, 0, 2))
        in_maps.append(m)
    import os
    res = run(build_L2(do_mlstm=os.environ.get('NO_MLSTM') is None, do_fourier=os.environ.get('NO_FOURIER') is None, phase=int(os.environ.get('PHASE', '9')), nscan=int(os.environ.get('NSCAN', str(NCH))), U=int(os.environ.get('NU', '3'))), in_maps)
    H = [[np.zeros((768, SL), np.float32) for _ in range(2)] for _ in range(NB)]
    for core in range(NCORES):
        for u in range(3):
            uid = core * 3 + u
            b, hd, dr = uid // 12, (uid % 12) // 2, uid % 2
            h = res[core]["hT%d" % u]
            if dr:
                h = np.concatenate([h[:, :CTX][:, ::-1], h[:, CTX:][:, ::-1]], axis=1)
            H[b][dr][hd * 128:(hd + 1) * 128] = h
    four_l = [np.zeros((256, SEQ), np.float32) for _ in range(NB)]; four_c = [np.zeros((256, CTX), np.float32) for _ in range(NB)]
    for core in range(NCORES):
        b, grp = core // 4, core % 4
        Y = res[core]["Y"].reshape(64, 64, 128)
        four_l[b][grp * 64:(grp + 1) * 64] = Y.transpose(1, 0, 2).reshape(64, SEQ)
        Yc = res[core]["Yc"]
        four_c[b][grp * 64:(grp + 1) * 64] = Yc.transpose(2, 1, 0).reshape(64, CTX)
    return H, four_l, four_c


def emit_wout_res(kb, ring, i, n, col, m_t, m_b, w_t, w_b, x_ap_fn, x_b, mod_t, mod_b, vgate):
    nc = kb.nc
    for oc in range(8):
        ps = ring.next()
        for kc in range(8):
            kb.op("pe", lambda kc=kc, oc=oc, ps=ps: nc.tensor.matmul(ps.t[:, :n], w_t[:, kc, oc * 128:(oc + 1) * 128], m_t[:, kc, :n],
                                                                   start=(kc == 0), stop=(kc == 7)), reads=[w_b, m_b], writes=[ps])
        kb.op("dve", lambda oc=oc, ps=ps: nc.vector.scalar_tensor_tensor(
            out=x_ap_fn(oc), in0=ps.t[:, :n], scalar=mod_t[:, vgate * 8 + oc, col:col + 1], in1=x_ap_fn(oc), op0=ALU.mult, op1=ALU.add),
            reads=[ps, mod_b, x_b], writes=[x_b])


def emit_headnorm(kb, ring, src_t, src_b, n, sq_t, sq_b, rs_t, rs_b, ones_t, ones_b, eps_t, eps_b):
    nc = kb.nc
    kb.op("act", lambda: nc.scalar.activation(out=sq_t, in_=src_t, func=AF.Square), reads=[src_b], writes=[sq_b])
    ps = ring.next()
    kb.op("pe", lambda: nc.tensor.matmul(ps.t[:, :n], ones_t[:, :], sq_t, start=True, stop=True), reads=[sq_b, ones_b], writes=[ps])
    kb.op("act", lambda: nc.scalar.activation(out=rs_t, in_=ps.t[:, :n], func=AF.Sqrt, bias=eps_t[:, 0:1], scale=1.0),
          reads=[ps, eps_b], writes=[rs_b])
    kb.op("dve", lambda: nc.vector.reciprocal(out=rs_t, in_=rs_t), reads=[rs_b], writes=[rs_b])


def build_L3():
    kb = KB(); nc = kb.nc
    xT = kb.dram("xT", [128, 8, NT], F32, "ExternalInput")
    hfd = kb.dram("hf", [128, 6, NT], F32, "ExternalInput"); hbd = kb.dram("hb", [128, 6, NT], F32, "ExternalInput")
    od = kb.dram("o", [128, 6, NT], F32, "ExternalInput"); yfd = kb.dram("yf", [128, 2, NT], F32, "ExternalInput")
    ghd = kb.dram("gh", [128, 6], F32, "ExternalInput"); modd = kb.dram("mod", [128, 48, 2], F32, "ExternalInput")
    wd = kb.dram("w", [1024, 1024], F32, "ExternalInput")
    xo = kb.dram("xo", [128, 8, NT], F32, "ExternalOutput")

    def sbb(name, shape, dt=F32, nb=1):
        t = kb.sb(name, shape, dt)
        return (t, Buf(t)) if nb == 1 else (t, [Buf(t) for _ in range(nb)])
    x_t, x_b = sbb("x", [128, 2, 8, 512], nb=2)
    hf_t, hf_b = sbb("hf", [128, 2, 6, 512], nb=2); hb_t, hb_b = sbb("hb", [128, 2, 6, 512], nb=2)
    o_t, o_b = sbb("o", [128, 2, 6, 512], nb=2); yf_t, yf_b = sbb("yf", [128, 2, 2, 512], nb=2)
    gh_t, gh_b = sbb("gh", [128, 6]); mod_t, mod_b = sbb("mod", [128, 48, 2])
    w_t, w_b = sbb("w", [128, 8, 1024], BF16)
    hs_t, hs_b = sbb("hs", [128, 2, 512], nb=2); sq_t, sq_b = sbb("sq", [128, 2, 512], BF16, nb=2)
    rs_t, rs_b = sbb("rs", [128, 2, 512], nb=2); sg_t, sg_b = sbb("sg", [128, 2, 512], nb=2)
    m_t, m_b = sbb("m", [128, 2, 8, 512], BF16, nb=2)
    ones_t, ones_b = sbb("ones", [128, 128], BF16); eps_t, eps_b = sbb("eps", [128, 1])
    ring = PsumRing(kb)
    kb.begin()
    kb.dma("sp", gh_t[:], ghd, writes=[gh_b]); kb.dma("sp", mod_t[:], modd, writes=[mod_b])
    kb.dma("pool", w_t[:], wd.rearrange("(kc p) n -> p kc n", p=128), writes=[w_b])
    kb.op("pool", lambda: nc.gpsimd.memset(ones_t[:], 1.0), writes=[ones_b])
    kb.op("pool", lambda: nc.gpsimd.memset(eps_t[:], 128.0 * EPS), writes=[eps_b])
    kb.op("dve", lambda: nc.vector.tensor_scalar(out=gh_t[:], in0=gh_t[:], scalar1=float(np.sqrt(128.0)), scalar2=None, op0=ALU.mult),
          reads=[gh_b], writes=[gh_b])
    for i, (lo, n) in enumerate(TBS):
        d = i % 2; col = tb_col(i)
        kb.dma("sp", x_t[:, d, :, :n], xT[:, :, lo:lo + n], writes=[x_b[d]])
        kb.dma("sp", hf_t[:, d, :, :n], hfd[:, :, lo:lo + n], writes=[hf_b[d]])
        kb.dma("sp", hb_t[:, d, :, :n], hbd[:, :, lo:lo + n], writes=[hb_b[d]])
        kb.dma("sp", o_t[:, d, :, :n], od[:, :, lo:lo + n], writes=[o_b[d]])
        kb.dma("sp", yf_t[:, d, :, :n], yfd[:, :, lo:lo + n], writes=[yf_b[d]])
        for hd in range(6):
            e = hd % 2
            kb.op("dve", lambda hd=hd, e=e: nc.vector.tensor_tensor(out=hs_t[:, e, :n], in0=hf_t[:, d, hd, :n], in1=hb_t[:, d, hd, :n], op=ALU.add),
                  reads=[hf_b[d], hb_b[d]], writes=[hs_b[e]])
            emit_headnorm(kb, ring, hs_t[:, e, :n], hs_b[e], n, sq_t[:, e, :n], sq_b[e], rs_t[:, e, :n], rs_b[e], ones_t, ones_b, eps_t, eps_b)
            kb.op("act", lambda hd=hd, e=e: nc.scalar.activation(out=sg_t[:, e, :n], in_=o_t[:, d, hd, :n], func=AF.Sigmoid),
                  reads=[o_b[d]], writes=[sg_b[e]])
            kb.op("dve", lambda hd=hd, e=e: nc.vector.scalar_tensor_tensor(out=hs_t[:, e, :n], in0=hs_t[:, e, :n], scalar=gh_t[:, hd:hd + 1],
                                                                           in1=rs_t[:, e, :n], op0=ALU.mult, op1=ALU.mult),
                  reads=[hs_b[e], gh_b, rs_b[e]], writes=[hs_b[e]])
            kb.op("pool", lambda hd=hd, e=e: nc.gpsimd.tensor_tensor(out=m_t[:, d, hd, :n], in0=hs_t[:, e, :n], in1=sg_t[:, e, :n], op=ALU.mult),
                  reads=[hs_b[e], sg_b[e]], writes=[m_b[d]])
        for j in range(2):
            evac(kb, j, m_t[:, d, 6 + j, :n], yf_t[:, d, j, :n], [yf_b[d]], [m_b[d]])
        emit_wout_res(kb, ring, i, n, col, m_t[:, d], m_b[d], w_t, w_b, lambda oc, d=d, n=n: x_t[:, d, oc, :n], x_b[d], mod_t, mod_b, 2)
        kb.dma("sp", xo[:, :, lo:lo + n], x_t[:, d, :, :n], reads=[x_b[d]], is_out=True)
    return kb.end()


def build_FFN(final):
    kb = KB(); nc = kb.nc
    xT = kb.dram("xT", [128, 8, NT], F32, "ExternalInput")
    modd = kb.dram("mod", [128, 48, 2], F32, "ExternalInput")
    gnd = kb.dram("gn", [128, 8], F32, "ExternalInput")
    w1d = kb.dram("w1", [1024, 4096], F32, "ExternalInput"); w2d = kb.dram("w2", [4096, 1024], F32, "ExternalInput")
    gfd = kb.dram("gf", [128, 8], F32, "ExternalInput")
    xo = kb.dram("xo", [128, 8, NT], F32, "ExternalOutput")

    def sbb(name, shape, dt=F32, nb=1):
        t = kb.sb(name, shape, dt)
        return (t, Buf(t)) if nb == 1 else (t, [Buf(t) for _ in range(nb)])
    x_t = kb.sb("x", [128, 8, NT]); xb = [Buf(x_t) for _ in TBS]
    h_t = kb.sb("h", [128, 8, NT], BF16); hb = [Buf(h_t) for _ in TBS]
    sq_t, sqb = sbb("sq", [128, 2, 8, 512], BF16, nb=2)
    tmp_t, tmpb = sbb("tmp", [128, 2, 512], nb=2); rs_t, rsb = sbb("rs", [128, 2, 512], nb=2)
    mod_t, mod_b = sbb("mod", [128, 48, 2]); gn_t, gn_b = sbb("gn", [128, 8]); gf_t, gf_b = sbb("gf", [128, 8])
    A_t, A_b = sbb("A", [128, 8, 2]); ones_t, ones_b = sbb("ones", [128, 128], BF16); eps_t, eps_b = sbb("eps", [128, 1])
    w1_t, w1_b = sbb("w1", [128, 2, 8, 1024], BF16, nb=2); w2_t, w2_b = sbb("w2", [128, 2, 8, 1024], BF16, nb=2)
    a_t, a_b = sbb("a", [128, 2, 8, 512], BF16, nb=2); r_t, r_b = sbb("r", [128, 2, 512], nb=2)
    ring = PsumRing(kb)
    kb.begin()
    kb.dma("sp", mod_t[:], modd, writes=[mod_b]); kb.dma("sp", gn_t[:], gnd, writes=[gn_b]); kb.dma("sp", gf_t[:], gfd, writes=[gf_b])
    for i, (lo, n) in enumerate(TBS):
        kb.dma("sp", x_t[:, :, lo:lo + n], xT[:, :, lo:lo + n], writes=[xb[i]])
    kb.op("pool", lambda: nc.gpsimd.memset(ones_t[:], 1.0), writes=[ones_b])
    kb.op("pool", lambda: nc.gpsimd.memset(eps_t[:], 1024.0 * EPS), writes=[eps_b])

    def load_w(p):
        kb.dma("pool", w1_t[:, p % 2], w1d[:, p * 1024:(p + 1) * 1024].rearrange("(kc p) n -> p kc n", p=128), writes=[w1_b[p % 2]])
        kb.dma("pool", w2_t[:, p % 2], w2d[p * 1024:(p + 1) * 1024, :].rearrange("(kc p) n -> p kc n", p=128), writes=[w2_b[p % 2]])
    load_w(0)
    emit_modprep(kb, mod_t, mod_b, gn_t, gn_b, A_t, A_b, vscale=4)
    for i in range(len(TBS)):
        emit_norm_block(kb, ring, i, x_t, xb[i], h_t, hb[i], sq_t, sqb, tmp_t, tmpb, rs_t, rsb, ones_t, ones_b,
                        A_t, A_b, mod_t, mod_b, 3, eps_t, eps_b)
    k = 0
    for p in range(4):
        if p + 1 < 4:
            load_w(p + 1)
        wq = p % 2
        for i, (lo, n) in enumerate(TBS):
            col = tb_col(i); d = k % 2; k += 1
            for fc in range(8):
                ps = ring.next()
                for kc in range(8):
                    kb.op("pe", lambda kc=kc, fc=fc, ps=ps: nc.tensor.matmul(ps.t[:, :n], w1_t[:, wq, kc, fc * 128:(fc + 1) * 128], h_t[:, kc, lo:lo + n],
                                                                           start=(kc == 0), stop=(kc == 7)), reads=[w1_b[wq], hb[i]], writes=[ps])
                e = fc % 2
                kb.op("act", lambda ps=ps, e=e: nc.scalar.activation(out=r_t[:, e, :n], in_=ps.t[:, :n], func=AF.Relu), reads=[ps], writes=[r_b[e]])
                kb.op("pool", lambda fc=fc, e=e, d=d: nc.gpsimd.tensor_tensor(out=a_t[:, d, fc, :n], in0=r_t[:, e, :n], in1=r_t[:, e, :n], op=ALU.mult),
                      reads=[r_b[e]], writes=[a_b[d]])
            emit_wout_res(kb, ring, i, n, col, a_t[:, d], a_b[d], w2_t[:, wq], w2_b[wq],
                          lambda oc, lo=lo, n=n: x_t[:, oc, lo:lo + n], xb[i], mod_t, mod_b, 5)
    if not final:
        for i, (lo, n) in enumerate(TBS):
            kb.dma("sp", xo[:, :, lo:lo + n], x_t[:, :, lo:lo + n], reads=[xb[i]], is_out=True)
        return kb.end()
    kb.op("dve", lambda: nc.vector.tensor_scalar(out=gf_t[:], in0=gf_t[:], scalar1=32.0, scalar2=None, op0=ALU.mult), reads=[gf_b], writes=[gf_b])
    for i, (lo, n) in enumerate(TBS):
        sb_ = sqb[i % 2]
        for kc in range(8):
            kb.op("act", lambda kc=kc: nc.scalar.activation(out=sq_t[:, i % 2, kc, :n], in_=x_t[:, kc, lo:lo + n], func=AF.Square), reads=[xb[i]], writes=[sb_])
        ps = ring.next()
        for kc in range(8):
            kb.op("pe", lambda kc=kc, ps=ps: nc.tensor.matmul(ps.t[:, :n], ones_t[:, :], sq_t[:, i % 2, kc, :n], start=(kc == 0), stop=(kc == 7)),
                  reads=[sb_, ones_b], writes=[ps])
        rb = rsb[i % 2]
        kb.op("act", lambda ps=ps: nc.scalar.activation(out=rs_t[:, i % 2, :n], in_=ps.t[:, :n], func=AF.Sqrt, bias=eps_t[:, 0:1], scale=1.0),
              reads=[ps, eps_b], writes=[rb])
        kb.op("dve", lambda: nc.vector.reciprocal(out=rs_t[:, i % 2, :n], in_=rs_t[:, i % 2, :n]), reads=[rb], writes=[rb])
        for kc in range(8):
            kb.op("dve", lambda kc=kc: nc.vector.scalar_tensor_tensor(out=x_t[:, kc, lo:lo + n], in0=x_t[:, kc, lo:lo + n], scalar=gf_t[:, kc:kc + 1],
                                                                      in1=rs_t[:, i % 2, :n], op0=ALU.mult, op1=ALU.mult),
                  reads=[xb[i], gf_b, rb], writes=[xb[i]])
        kb.dma("sp", xo[:, :, lo:lo + n], x_t[:, :, lo:lo + n], reads=[xb[i]], is_out=True)
    return kb.end()


def slab(lat, ctx, core, nch):
    b, q = core // 4, core % 4
    a = np.concatenate([lat[b][:, q * TOK:(q + 1) * TOK], ctx[b][:, q * CT:(q + 1) * CT]], axis=1)
    return np.ascontiguousarray(a.reshape(nch, 128, NT).transpose(1, 0, 2))


def run_L3(inp, mod, xTs, r1, H, four_l, four_c):
    o_l, o_c = gather_fm(r1, "o_o", 6, np.float32)
    w = np.ascontiguousarray(inp["w_out_even"][0]); gh = fm_vec(inp["g_mlstm_head"][0], 6)
    hf_l = [H[b][0][:, CTX:] for b in range(NB)]; hf_c = [H[b][0][:, :CTX] for b in range(NB)]
    hb_l = [H[b][1][:, CTX:] for b in range(NB)]; hb_c = [H[b][1][:, :CTX] for b in range(NB)]
    in_maps = []
    for c in range(NCORES):
        in_maps.append({"xT": xTs[c], "hf": slab(hf_l, hf_c, c, 6), "hb": slab(hb_l, hb_c, c, 6), "o": slab(o_l, o_c, c, 6),
                        "yf": slab(four_l, four_c, c, 2), "gh": gh, "mod": mod_for_core(mod, 0, c), "w": w})
    res = run(build_L3(), in_maps)
    return [res[c]["xo"] for c in range(NCORES)]


def run_FFN(inp, mod, xTs, l, final):
    gn = fm_vec(inp["g_norm"][l, 1], 8); gf = fm_vec(inp["g_final"], 8)
    w1 = np.ascontiguousarray(inp["w_ff1"][l]); w2 = np.ascontiguousarray(inp["w_ff2"][l])
    in_maps = [{"xT": xTs[c], "mod": mod_for_core(mod, l, c), "gn": gn, "w1": w1, "w2": w2, "gf": gf} for c in range(NCORES)]
    res = run(build_FFN(final), in_maps)
    return [res[c]["xo"] for c in range(NCORES)]


def unslab(xTs):
    x = np.zeros((NB, SEQ, D), np.float32); xc = np.zeros((NB, CTX, D), np.float32)
    for core in range(NCORES):
        b, q = core // 4, core % 4
        t = xTs[core].transpose(2, 1, 0).reshape(NT, D)
        x[b, q * TOK:(q + 1) * TOK] = t[:TOK]; xc[b, q * CT:(q + 1) * CT] = t[TOK:]
    return x, xc


def build_L5():
    kb = KB(); nc = kb.nc
    WC = 4352
    xT = kb.dram("xT", [128, 8, NT], F32, "ExternalInput")
    mod = kb.dram("mod", [128, 48, 2], F32, "ExternalInput")
    gn = kb.dram("gn", [128, 8], F32, "ExternalInput")
    w = kb.dram("w", [1024, WC], F32, "ExternalInput")
    cosd = kb.dram("cosT", [128, TOK], F32, "ExternalInput"); sind = kb.dram("sinT", [128, TOK], F32, "ExternalInput")
    o_qkv = kb.dram("o_qkv", [128, 18, NT], BF16, "ExternalOutput")
    o_glu = kb.dram("o_glu", [128, 4, NT], F32, "ExternalOutput")
    x_t = kb.sb("x_t", [128, 8, NT]); xb = [Buf(x_t) for _ in TBS]
    h_t = kb.sb("h_t", [128, 8, NT], BF16); hb = [Buf(h_t) for _ in TBS]
    sq_t = kb.sb("sq_t", [128, 2, 8, 512], BF16); sqb = [Buf(sq_t), Buf(sq_t)]
    tmp_t = kb.sb("tmp_t", [128, 2, 512]); tmpb = [Buf(tmp_t), Buf(tmp_t)]
    rs_t = kb.sb("rs_t", [128, 2, 512]); rsb = [Buf(rs_t), Buf(rs_t)]
    mod_t = kb.sb("mod_t", [128, 48, 2]); mod_b = Buf(mod_t)
    gn_t = kb.sb("gn_t", [128, 8]); gn_b = Buf(gn_t)
    A_t = kb.sb("A_t", [128, 8, 2]); A_b = Buf(A_t)
    ones_t = kb.sb("ones_t", [128, 128], BF16); ones_b = Buf(ones_t)
    eps_t = kb.sb("eps_t", [128, 1]); eps_b = Buf(eps_t)
    cos_t = kb.sb("cos_t", [128, TOK]); cos_b = Buf(cos_t)
    sin_t = kb.sb("sin_t", [128, TOK]); sin_b = Buf(sin_t)
    NWB = 3
    wg_t = kb.sb("wg_t", [128, NWB, 8, 512], BF16); wgb = [Buf(wg_t) for _ in range(NWB)]
    NST = 4
    stb_t = kb.sb("stb_t", [128, NST, 512], BF16); stbb = [Buf(stb_t) for _ in range(NST)]
    stf_t = kb.sb("stf_t", [128, NST, 512]); stfb = [Buf(stf_t) for _ in range(NST)]
    ta_t = kb.sb("ta_t", [128, 2, 512]); ta_b = [Buf(ta_t), Buf(ta_t)]
    tb_t = kb.sb("tb_t", [128, 2, 512]); tb_b = [Buf(tb_t), Buf(tb_t)]
    ring = PsumRing(kb)
    kb.begin()
    kb.dma("sp", mod_t[:], mod, writes=[mod_b]); kb.dma("sp", gn_t[:], gn, writes=[gn_b])
    kb.dma("sp", cos_t[:], cosd, writes=[cos_b]); kb.dma("sp", sin_t[:], sind, writes=[sin_b])
    for i, (lo, n) in enumerate(TBS):
        kb.dma("sp", x_t[:, :, lo:lo + n], xT[:, :, lo:lo + n], writes=[xb[i]])
    kb.op("pool", lambda: nc.gpsimd.memset(ones_t[:], 1.0), writes=[ones_b])
    kb.op("pool", lambda: nc.gpsimd.memset(eps_t[:], 1024.0 * EPS), writes=[eps_b])
    groups = [(c0, 512) for c0 in range(0, 3584, 512)] + [(3584, 256), (3840, 512)]

    def load_group(gi):
        c0, gw = groups[gi]
        kb.dma("pool", wg_t[:, gi % NWB, :, :gw], w[:, c0:c0 + gw].rearrange("(kc p) n -> p kc n", p=128), writes=[wgb[gi % NWB]])
    load_group(0); load_group(1)
    emit_modprep(kb, mod_t, mod_b, gn_t, gn_b, A_t, A_b, vscale=1)
    for i in range(len(TBS)):
        emit_norm_block(kb, ring, i, x_t, xb[i], h_t, hb[i], sq_t, sqb, tmp_t, tmpb, rs_t, rsb, ones_t, ones_b,
                        A_t, A_b, mod_t, mod_b, 0, eps_t, eps_b)
    k = 0

    def mm(gi, cl, i, lo, n):
        ps = ring.next()
        for kc in range(8):
            kb.op("pe", lambda kc=kc: nc.tensor.matmul(ps.t[:, :n], wg_t[:, gi % NWB, kc, cl:cl + 128], h_t[:, kc, lo:lo + n],
                                                     start=(kc == 0), stop=(kc == 7)), reads=[wgb[gi % NWB], hb[i]], writes=[ps])
        return ps
    for gi, (c0, gw) in enumerate(groups):
        if gi + 2 < len(groups):
            load_group(gi + 2)
        if gi < 6:
            items = [("rope", 0, 2 * gi), ("rope", 128, 2 * gi + 1)]
        elif gi == 6:
            items = [("v", cl, 12 + cl // 128) for cl in range(0, 512, 128)]
        elif gi == 7:
            items = [("v", 0, 16), ("v", 128, 17)]
        else:
            items = [("glu", cl, cl // 128) for cl in range(0, 512, 128)]
        for (kind, cl, idx) in items:
            for i, (lo, n) in enumerate(TBS):
                k += 1
                ps = mm(gi, cl, i, lo, n)
                if kind == "glu":
                    sb_ = stfb[k % NST]
                    evac(kb, k, stf_t[:, k % NST, :n], ps.t[:, :n], [ps], [sb_])
                    kb.dma("sp", o_glu[:, idx, lo:lo + n], stf_t[:, k % NST, :n], reads=[sb_], is_out=True)
                    continue
                sb_ = stbb[k % NST]
                if kind == "v" or i == 4:
                    evac(kb, k, stb_t[:, k % NST, :n], ps.t[:, :n], [ps], [sb_])
                else:
                    ps2 = mm(gi, 256 + cl, i, lo, n)
                    e = k % 2
                    kb.op("dve", lambda ps=ps, e=e: nc.vector.tensor_tensor(out=ta_t[:, e, :n], in0=ps.t[:, :n], in1=cos_t[:, lo:lo + n], op=ALU.mult),
                          reads=[ps, cos_b], writes=[ta_b[e]])
                    kb.op("dve", lambda ps2=ps2, e=e: nc.vector.tensor_tensor(out=tb_t[:, e, :n], in0=ps2.t[:, :n], in1=sin_t[:, lo:lo + n], op=ALU.mult),
                          reads=[ps2, sin_b], writes=[tb_b[e]])
                    kb.op("pool", lambda e=e, k=k: nc.gpsimd.tensor_tensor(out=stb_t[:, k % NST, :n], in0=ta_t[:, e, :n], in1=tb_t[:, e, :n], op=ALU.add),
                          reads=[ta_b[e], tb_b[e]], writes=[sb_])
                kb.dma("sp", o_qkv[:, idx, lo:lo + n], stb_t[:, k % NST, :n], reads=[sb_], is_out=True)
    return kb.end()


def rope_tables(q):
    t = q * TOK + np.arange(TOK)
    row = (t // 64).astype(np.float64); colp = (t % 64).astype(np.float64)
    inv = 10000.0 ** (-np.arange(16) / 16.0)
    ang = np.concatenate([row[:, None] * inv[None, :], colp[:, None] * inv[None, :]], axis=1)
    cosT = np.zeros((128, TOK), np.float32); sinT = np.zeros((128, TOK), np.float32)
    for r in range(128):
        d = r % 64; half = d // 32; fi = d % 32
        cosT[r] = np.cos(ang[:, fi])
        sinT[r] = (-1.0 if half == 0 else 1.0) * np.sin(ang[:, fi])
    return cosT, sinT


def run_L5(inp, mod, xTs):
    W = inp["w_in_odd"][0]
    perm = np.arange(1536)
    r = perm % 128
    partner = np.where((r % 64) < 32, perm + 32, perm - 32)
    Wsw = W[:, :1536][:, partner]
    cols = []
    for g in range(6):
        cols += [W[:, 256 * g:256 * g + 256], Wsw[:, 256 * g:256 * g + 256]]
    cols += [W[:, 1536:2304], W[:, 2304:2816]]
    wcat = np.ascontiguousarray(np.concatenate(cols, axis=1))
    gn = fm_vec(inp["g_norm"][1, 0], 8)
    tabs = [rope_tables(q) for q in range(4)]
    in_maps = [{"xT": xTs[c], "mod": mod_for_core(mod, 1, c), "gn": gn, "w": wcat, "cosT": tabs[c % 4][0], "sinT": tabs[c % 4][1]}
               for c in range(NCORES)]
    return run(build_L5(), in_maps)


LAM_INIT = 0.8 - 0.6 * float(np.exp(-0.3))
HALO = 15; TH = TOK + 2 * HALO


def build_L6(nh=6, nqb=4, dbg=False):
    kb = KB(); nc = kb.nc
    xT = kb.dram("xT", [128, 8, TOK], F32, "ExternalInput")
    qd = kb.dram("qT", [128, 6, TOK], BF16, "ExternalInput")
    kd = kb.dram("kT", [128, 6, SL], BF16, "ExternalInput")
    vd = kb.dram("v", [128, NCH, 6, 128], BF16, "ExternalInput")
    agd = kb.dram("ag", [128, 2, 2, TH], F32, "ExternalInput")
    wdwd = kb.dram("wdw", [128, 2, 31], F32, "ExternalInput")
    lnd = kb.dram("ln", [128, 2, 2], F32, "ExternalInput")
    lampd = kb.dram("lamp", [128, 4, 64], F32, "ExternalInput")
    gsubd = kb.dram("gsub", [128, 1], F32, "ExternalInput")
    modd = kb.dram("mod", [128, 48, 2], F32, "ExternalInput")
    wd = kb.dram("w", [1024, 1024], F32, "ExternalInput")
    xo = kb.dram("xo", [128, 8, TOK], F32, "ExternalOutput")
    mdbg = kb.dram("mdbg", [128, 8, TOK], BF16, "ExternalOutput") if dbg else None

    def sbb(name, shape, dt=F32, nb=1):
        t = kb.sb(name, shape, dt)
        return (t, Buf(t)) if nb == 1 else (t, [Buf(t) for _ in range(nb)])
    kt_t, kt_b = sbb("kt", [128, 2, SL], BF16, nb=2); v_t, v_b = sbb("vv", [128, 2, NCH, 128], BF16, nb=2)
    q_t, q_b = sbb("q", [128, 6, TOK], BF16)
    m_t = kb.sb("m", [128, 8, TOK], BF16); m_b = [Buf(m_t) for _ in range(4)]
    p_t, p_b = sbb("p", [128, 3, 512], BF16, nb=3)
    x_t, x_b = sbb("x", [128, 2, 8, 512], nb=2)
    w_t, w_b = sbb("w", [128, 8, 1024], BF16)
    wdw_t, wdw_b = sbb("wdw", [128, 2, 31]); ln_t, ln_b = sbb("ln", [128, 2, 2])
    lamp_t, lamp_b = sbb("lamp", [128, 4, 64]); lpr_t, lpr_b = sbb("lpr", [128, 2, 64]); ls_t, ls_b = sbb("ls", [128, 2])
    nl_t, nl_b = sbb("nl", [128, 1]); gs_t, gs_b = sbb("gs", [128, 1]); mod_t, mod_b = sbb("mod", [128, 48, 2])
    ones_t, ones_b = sbb("ones", [128, 128], BF16); onef_t, onef_b = sbb("onef", [128, 128])
    eps_t, eps_b = sbb("eps", [128, 1]); epsl_t, epsl_b = sbb("epsl", [128, 1])
    rz_t, rz_b = sbb("rz", [128, 2, 512], nb=2); od_t, od_b = sbb("od", [128, 2, 512], nb=2)
    sq_t, sq_b = sbb("sq", [128, 512], BF16); rs_t, rs_b = sbb("rs", [128, 512])
    xc_t, xc_b = sbb("xc", [128, 2, 512], nb=2); sqf_t, sqf_b = sbb("sqf", [128, 2, 512], nb=2)
    pst = [kb.psum("pb%d" % i, [128, 512], F32) for i in range(8)]
    kb.begin()
    AG = kt_t[:, :, :].rearrange("p a s -> p (a s)").bitcast(F32)
    a_v = AG[:, 0:2 * TH].rearrange("p (c t) -> p c t", c=2); g_v = AG[:, 2 * TH:4 * TH].rearrange("p (c t) -> p c t", c=2)
    ag_b = Buf(AG)
    ACC = v_t[:, :, :, :].rearrange("p a c d -> p (a c d)").bitcast(F32)
    accD = ACC[:, 0:2 * TOK].rearrange("p (c t) -> p c t", c=2); accP = ACC[:, 2 * TOK:4 * TOK].rearrange("p (c t) -> p c t", c=2)
    accD_b = Buf(ACC); accP_b = Buf(ACC)
    bankb = [Buf(pst[i]) for i in range(8)]
    sring = [bankb[0], bankb[1], bankb[2], bankb[7]]
    sri = [0]
    only7 = [False]

    def rnext():
        if only7[0]:
            return bankb[7]
        b = sring[sri[0] % 4]; sri[0] += 1
        return b
    class R:
        def next(self):
            return rnext()
    ring = R()
    for (t, b, d) in ((wdw_t, wdw_b, wdwd), (ln_t, ln_b, lnd), (lamp_t, lamp_b, lampd), (gs_t, gs_b, gsubd), (mod_t, mod_b, modd), (q_t, q_b, qd)):
        kb.dma("sp", t[:], d, writes=[b])
    kb.dma("sp", a_v, agd[:, 0], writes=[ag_b]); kb.dma("sp", g_v, agd[:, 1], writes=[ag_b])
    kb.dma("pool", w_t[:], wd.rearrange("(kc p) n -> p kc n", p=128), writes=[w_b])
    kb.op("pool", lambda: nc.gpsimd.memset(ones_t[:], 1.0), writes=[ones_b])
    kb.op("pool", lambda: nc.gpsimd.memset(onef_t[:], 1.0), writes=[onef_b])
    kb.op("pool", lambda: nc.gpsimd.memset(eps_t[:], 128.0 * EPS), writes=[eps_b])
    kb.op("pool", lambda: nc.gpsimd.memset(epsl_t[:], EPS), writes=[epsl_b])
    kb.op("dve", lambda: nc.vector.tensor_tensor(out=lpr_t[:], in0=lamp_t[:, 0::2, :], in1=lamp_t[:, 1::2, :], op=ALU.mult), reads=[lamp_b], writes=[lpr_b])
    kb.op("dve", lambda: nc.vector.tensor_reduce(out=ls_t[:], in_=lpr_t[:], axis=AX.X, op=ALU.add), reads=[lpr_b], writes=[ls_b])
    kb.op("act", lambda: nc.scalar.activation(out=ls_t[:], in_=ls_t[:], func=AF.Exp), reads=[ls_b], writes=[ls_b])
    kb.op("dve", lambda: nc.vector.tensor_tensor(out=nl_t[:], in0=ls_t[:, 1:2], in1=ls_t[:, 0:1], op=ALU.subtract), reads=[ls_b], writes=[nl_b])
    kb.op("dve", lambda: nc.vector.tensor_scalar(out=nl_t[:], in0=nl_t[:], scalar1=-LAM_INIT, scalar2=None, op0=ALU.add), reads=[nl_b], writes=[nl_b])
    kb.op("dve", lambda: nc.vector.tensor_scalar(out=gs_t[:], in0=gs_t[:], scalar1=float((1.0 - LAM_INIT) * np.sqrt(128.0)), scalar2=None, op0=ALU.mult),
          reads=[gs_b], writes=[gs_b])
    kb.op("act", lambda: nc.scalar.activation(out=g_v, in_=g_v, func=AF.Sigmoid), reads=[ag_b], writes=[ag_b])
    kb.op("dve", lambda: nc.vector.tensor_tensor(out=a_v, in0=a_v, in1=g_v, op=ALU.mult), reads=[ag_b], writes=[ag_b])
    for j in range(2):
        for k in range(31):
            if k == 0:
                kb.op("dve", lambda k=k: nc.vector.tensor_scalar(out=accD[:, j, :], in0=a_v[:, j, k:k + TOK], scalar1=wdw_t[:, j, k:k + 1], scalar2=None, op0=ALU.mult),
                      reads=[ag_b, wdw_b], writes=[accD_b])
            else:
                kb.op("dve", lambda k=k: nc.vector.scalar_tensor_tensor(out=accD[:, j, :], in0=a_v[:, j, k:k + TOK], scalar=wdw_t[:, j, k:k + 1], in1=accD[:, j, :],
                                                                      op0=ALU.mult, op1=ALU.add), reads=[ag_b, wdw_b, accD_b], writes=[accD_b])
    for tb in range(4):
        cs = slice(tb * 512, (tb + 1) * 512); e = tb % 2
        ps = rnext()
        for j in range(2):
            kb.op("pe", lambda j=j, ps=ps: nc.tensor.matmul(ps.t[:, :], onef_t[:, :], accD[:, j, cs], start=(j == 0), stop=(j == 1)), reads=[onef_b, accD_b], writes=[ps])
        for j in range(2):
            kb.op("dve", lambda j=j, ps=ps: nc.vector.scalar_tensor_tensor(out=xc_t[:, e, :] if j == 0 else sqf_t[:, e, :], in0=ps.t[:, :], scalar=-1.0 / 256.0,
                                                                           in1=accD[:, j, cs], op0=ALU.mult, op1=ALU.add),
                  reads=[ps, accD_b], writes=[xc_b[e] if j == 0 else sqf_b[e]])
        kb.op("act", lambda: nc.scalar.activation(out=od_t[:, e, :], in_=xc_t[:, e, :], func=AF.Square), reads=[xc_b[e]], writes=[od_b[e]])
        kb.op("act", lambda: nc.scalar.activation(out=rz_t[:, e, :], in_=sqf_t[:, e, :], func=AF.Square), reads=[sqf_b[e]], writes=[rz_b[e]])
        ps2 = rnext()
        kb.op("pe", lambda ps2=ps2: nc.tensor.matmul(ps2.t[:, :], onef_t[:, :], od_t[:, e, :], start=True, stop=False), reads=[onef_b, od_b[e]], writes=[ps2])
        kb.op("pe", lambda ps2=ps2: nc.tensor.matmul(ps2.t[:, :], onef_t[:, :], rz_t[:, e, :], start=False, stop=True), reads=[onef_b, rz_b[e]], writes=[ps2])
        kb.op("act", lambda ps2=ps2: nc.scalar.activation(out=rs_t[:, :], in_=ps2.t[:, :], func=AF.Sqrt, bias=epsl_t[:, 0:1], scale=1.0 / 256.0),
              reads=[ps2, epsl_b], writes=[rs_b])
        kb.op("dve", lambda: nc.vector.reciprocal(out=rs_t[:, :], in_=rs_t[:, :]), reads=[rs_b], writes=[rs_b])
        for j, (src_t, src_b) in enumerate(((xc_t, xc_b), (sqf_t, sqf_b))):
            kb.op("dve", lambda j=j, src_t=src_t: nc.vector.scalar_tensor_tensor(out=src_t[:, e, :], in0=src_t[:, e, :], scalar=ln_t[:, 0, j:j + 1], in1=rs_t[:, :],
                                                                                 op0=ALU.mult, op1=ALU.mult), reads=[src_b[e], ln_b, rs_b], writes=[src_b[e]])
            kb.op("act", lambda j=j, src_t=src_t: nc.scalar.activation(out=m_t[:, 6 + j, cs], in_=src_t[:, e, :], func=AF.Silu, bias=ln_t[:, 1, j:j + 1], scale=1.0),
                  reads=[src_b[e], ln_b], writes=[m_b[tb]])
    kb.barrier()
    kt_b = [Buf(kt_t), Buf(kt_t)]; v_b = [Buf(v_t), Buf(v_t)]
    S_b = [bankb[0], bankb[1], bankb[2]]
    O_b = [bankb[3], bankb[4]]; Z_b = [bankb[5], bankb[6]]
    only7[0] = True
    si = 0
    for hd in range(nh):
        d = hd % 2
        kb.dma("sp", kt_t[:, d, :], kd[:, hd, :], writes=[kt_b[d]])
        kb.dma("sp", v_t[:, d], vd[:, :, hd, :], writes=[v_b[d]])
        for qb in range(nqb):
            qs = slice(qb * 512, (qb + 1) * 512)
            for mp in range(2):
                pr = slice(64 * mp, 64 * mp + 64)
                for kt in range(NCH):
                    sb_ = S_b[si % 3]; pb_ = p_b[si % 3]; pi_ = si % 3; si += 1
                    kb.op("pe", lambda sb_=sb_, kt=kt: nc.tensor.matmul(sb_.t[:, :], kt_t[pr, d, kt * 128:(kt + 1) * 128], q_t[pr, hd, qs], start=True, stop=True),
                          reads=[kt_b[d], q_b], writes=[sb_])
                    kb.op("act", lambda sb_=sb_, pi_=pi_: nc.scalar.activation(out=p_t[:, pi_, :], in_=sb_.t[:, :], func=AF.Exp, scale=0.125), reads=[sb_], writes=[pb_])
                    kb.op("pe", lambda pi_=pi_, kt=kt: nc.tensor.matmul(pst[3 + mp][:, :], v_t[:, d, kt, :], p_t[:, pi_, :], start=(kt == 0), stop=(kt == NCH - 1)),
                          reads=[v_b[d], pb_], writes=[O_b[mp]])
                    kb.op("pe", lambda pi_=pi_, kt=kt: nc.tensor.matmul(pst[5 + mp][:, :], ones_t[:, :], p_t[:, pi_, :], start=(kt == 0), stop=(kt == NCH - 1)),
                          reads=[ones_b, pb_], writes=[Z_b[mp]])
            e = (hd * 4 + qb) % 2
            for mp in range(2):
                kb.op("dve", lambda mp=mp: nc.vector.reciprocal(out=rz_t[:, e, :], in_=pst[5 + mp][:, :]), reads=[Z_b[mp]], writes=[rz_b[e]])
                kb.op("dve", lambda mp=mp: nc.vector.tensor_tensor(out=(od_t[:, e, :] if mp == 0 else xc_t[:, e, :]), in0=pst[3 + mp][:, :], in1=rz_t[:, e, :], op=ALU.mult),
                      reads=[O_b[mp], rz_b[e]], writes=[od_b[e] if mp == 0 else xc_b[e]])
            kb.op("dve", lambda: nc.vector.scalar_tensor_tensor(out=od_t[:, e, :], in0=xc_t[:, e, :], scalar=nl_t[:, 0:1], in1=od_t[:, e, :], op0=ALU.mult, op1=ALU.add),
                  reads=[xc_b[e], nl_b, od_b[e]], writes=[od_b[e]])
            emit_headnorm(kb, ring, od_t[:, e, :], od_b[e], 512, sq_t[:, :], sq_b, rs_t[:, :], rs_b, ones_t, ones_b, eps_t, eps_b)
            kb.op("dve", lambda: nc.vector.scalar_tensor_tensor(out=m_t[:, hd, qs], in0=od_t[:, e, :], scalar=gs_t[:, 0:1], in1=rs_t[:, :], op0=ALU.mult, op1=ALU.mult),
                  reads=[od_b[e], gs_b, rs_b], writes=[m_b[qb]])
    only7[0] = False
    if dbg:
        kb.dma("sp", mdbg, m_t[:], reads=m_b, is_out=True)
    for tb in range(4):
        d = tb % 2; lo = tb * 512
        kb.dma("sp", x_t[:, d], xT[:, :, lo:lo + 512], writes=[x_b[d]])
        emit_wout_res(kb, ring, tb, 512, 0, m_t[:, :, lo:lo + 512], m_b[tb], w_t, w_b, lambda oc, d=d: x_t[:, d, oc, :], x_b[d], mod_t, mod_b, 2)
        kb.dma("sp", xo[:, :, lo:lo + 512], x_t[:, d], reads=[x_b[d]], is_out=True)
    return kb.end()


def run_L6(inp, mod, x4, r5):
    qkv_l, qkv_c = gather_fm(r5, "o_qkv", 18, NPBF)
    glu_l, _ = gather_fm(r5, "o_glu", 4, np.float32)
    w = np.ascontiguousarray(inp["w_out_odd"][0])
    wdw = np.ascontiguousarray(inp["w_dw"][0].T.reshape(2, 128, 31).transpose(1, 0, 2))
    ln = np.ascontiguousarray(np.stack([fm_vec(inp["g_conv_ln"][0], 2), fm_vec(inp["b_conv_ln"][0], 2)], axis=1))
    lamp = np.ascontiguousarray(np.broadcast_to(inp["lam_p"][0][None], (128, 4, 64)))
    gsub = np.ascontiguousarray(inp["g_subln"][0].reshape(128, 1))
    in_maps = []
    for c in range(NCORES):
        b, q = c // 4, c % 4
        kall = np.concatenate([qkv_c[b][768:1536], qkv_l[b][768:1536]], axis=1)
        vall = np.concatenate([qkv_c[b][1536:2304], qkv_l[b][1536:2304]], axis=1)
        kT = np.ascontiguousarray(kall.reshape(6, 128, SL).transpose(1, 0, 2))
        v = np.ascontiguousarray(vall.reshape(6, 128, NCH, 128).transpose(3, 2, 0, 1))
        qT = np.ascontiguousarray(qkv_l[b][0:768, q * TOK:(q + 1) * TOK].reshape(6, 128, TOK).transpose(1, 0, 2))
        gp = np.zeros((512, SEQ + 2 * HALO), np.float32); gp[:, HALO:HALO + SEQ] = glu_l[b]
        seg = gp[:, q * TOK:q * TOK + TH]
        ag = np.ascontiguousarray(seg.reshape(2, 2, 128, TH).transpose(2, 0, 1, 3))
        in_maps.append({"xT": np.ascontiguousarray(x4[c][:, :, :TOK]), "qT": qT, "kT": kT, "v": v, "ag": ag, "wdw": wdw, "ln": ln,
                        "lamp": lamp, "gsub": gsub, "mod": mod_for_core(mod, 1, c), "w": w})
    import os
    dbg = os.environ.get('L6_DBG') is not None
    res = run(build_L6(nh=int(os.environ.get('L6_NH', '6')), nqb=int(os.environ.get('L6_NQB', '4')), dbg=dbg), in_maps)
    if dbg:
        return res
    out = []
    for c in range(NCORES):
        t = np.zeros((128, 8, NT), np.float32); t[:, :, :TOK] = res[c]["xo"]
        out.append(t)
    return out


def kernel(**inp):
    inp = {k: np.asarray(v) for k, v in inp.items()}
    mod = run_L0(inp)
    xTs = [xT_for_core(inp["x"], inp["ctx"], c) for c in range(NCORES)]
    r1 = run_L1(inp, mod, xTs)
    H, four_l, four_c = run_L2(inp, r1)
    x3 = run_L3(inp, mod, xTs, r1, H, four_l, four_c)
    x4 = run_FFN(inp, mod, x3, 0, False)
    r5 = run_L5(inp, mod, x4)
    x6 = run_L6(inp, mod, x4, r5)
    x7 = run_FFN(inp, mod, x6, 1, True)
    out, _ = unslab(x7)
    return out.astype(np.float32)
```

```python
import numpy as np, ml_dtypes
from contextlib import ExitStack
import concourse.bass as bass
import concourse.mybir as mybir
from concourse.bass_utils import run_bass_kernel_spmd

F32 = mybir.dt.float32; BF16 = mybir.dt.bfloat16
AF = mybir.ActivationFunctionType; ALU = mybir.AluOpType; AX = mybir.AxisListType
NPBF = ml_dtypes.bfloat16
NCORES = 8


class Buf:
    __slots__ = ("t", "w", "r")

    def __init__(self, t):
        self.t = t; self.w = None; self.r = {}

    def __getitem__(self, k):
        return self.t[k]


class KB:
    NSLOT = 40

    def __init__(self):
        self.nc = nc = bass.Bass("TRN2", target_bir_lowering=False)
        self.es = ExitStack()
        self.eng = dict(pe=nc.tensor, act=nc.scalar, dve=nc.vector, pool=nc.gpsimd, sp=nc.sync)
        self.sem = {e: self.es.enter_context(nc.semaphore("sem_" + e)) for e in self.eng}
        self.cnt = {e: 0 for e in self.eng}
        self.seen = {e: {} for e in self.eng}
        self.hist = {}
        self.dsem = [self.es.enter_context(nc.semaphore("dsem%d" % i)) for i in range(self.NSLOT)]
        self.dcnt = [0] * self.NSLOT
        self.dnext = 0
        self.out_tokens = []
        self.nps = 0
        self.started = False

    def dram(self, name, shape, dtype, kind):
        return self.nc.dram_tensor(name, list(shape), dtype, kind=kind).ap()

    def sb(self, name, shape, dtype=F32):
        return self.es.enter_context(self.nc.sbuf_tensor("sb_" + name, list(shape), dtype))

    def psum(self, name, shape, dtype=F32):
        return self.es.enter_context(self.nc.psum_tensor("pp_" + name, list(shape), dtype))

    def begin(self):
        self.es.enter_context(self.nc.Block())
        self.started = True

    def end(self):
        for tok in self.out_tokens:
            self._wait("sp", tok)
        self.es.close()
        return self.nc

    def _semh(self, key):
        return self.sem[key] if isinstance(key, str) else self.dsem[key]

    def _wait(self, e, tok):
        if tok is None:
            return
        key, val = tok
        s = self.seen[e]
        if s.get(key, 0) >= val:
            return
        if not (e == "pe" and key == "pe"):
            self.eng[e].wait_ge(self._semh(key), val)
        s[key] = val
        h = self.hist.get(tok)
        if h:
            for k2, v2 in h.items():
                if s.get(k2, 0) < v2:
                    s[k2] = v2

    def _deps(self, e, reads, writes):
        for b in reads:
            self._wait(e, b.w)
        for b in writes:
            self._wait(e, b.w)
            for k2, v2 in list(b.r.items()):
                self._wait(e, (k2, v2))

    def _commit(self, tok, e, reads, writes):
        self.hist[tok] = dict(self.seen[e])
        for b in reads:
            if b.r.get(tok[0], 0) < tok[1]:
                b.r[tok[0]] = tok[1]
        for b in writes:
            b.w = tok; b.r = {}

    def op(self, e, f, reads=(), writes=()):
        self._deps(e, reads, writes)
        ins = f()
        self.cnt[e] += 1
        ins.then_inc(self.sem[e], 1)
        tok = (e, self.cnt[e])
        self._commit(tok, e, reads, writes)
        return tok

    def dma(self, q, out_ap, in_ap, reads=(), writes=(), is_out=False):
        slot = self.dnext
        self.dnext = (self.dnext + 1) % self.NSLOT
        if self.dcnt[slot] > 0:
            self._wait(q, (slot, 16 * self.dcnt[slot]))
        self._deps(q, reads, writes)
        ins = self.eng[q].dma_start(out=out_ap, in_=in_ap)
        self.dcnt[slot] += 1
        ins.then_inc(self.dsem[slot], 16)
        tok = (slot, 16 * self.dcnt[slot])
        self._commit(tok, q, reads, writes)
        if is_out:
            self.out_tokens.append(tok)
        return tok

    def barrier(self):
        toks = [(e, self.cnt[e]) for e in self.eng if self.cnt[e] > 0]
        toks += [(i, 16 * self.dcnt[i]) for i in range(self.NSLOT) if self.dcnt[i] > 0]
        for e in self.eng:
            for t in toks:
                self._wait(e, t)


def run(kb_nc, in_maps):
    res = run_bass_kernel_spmd(kb_nc, in_maps, core_ids=list(range(NCORES)))
    if getattr(res, "exec_time_ns", None):
        print("  [exec_time_ns]", res.exec_time_ns, flush=True)
    return res.results


D = 1024; SEQ = 8192; CTX = 256; NB = 2
TOK = 2048; CT = 64; NT = TOK + CT
TBS = [(0, 512), (512, 512), (1024, 512), (1536, 512), (2048, 64)]
EPS = 1e-6
EVEN_IN = 3352; ODD_IN = 2816


def tb_col(i):
    return 1 if i == 4 else 0


def build_L0():
    kb = KB(); nc = kb.nc
    sT = kb.dram("sT", [128, 8, 3], F32, "ExternalInput")
    w = kb.dram("w", [2, 1024, 768], F32, "ExternalInput")
    bm = kb.dram("bm", [128, 2, 6], F32, "ExternalInput")
    out = kb.dram("out", [128, 2, 6, 3], F32, "ExternalOutput")
    s_t = kb.sb("s_t", [128, 8, 3]); s_b = Buf(s_t)
    ss_t = kb.sb("ss_t", [128, 8, 3]); ss_b = Buf(ss_t)
    w_t = kb.sb("w_t", [128, 2, 8, 768]); w_b = [Buf(w_t), Buf(w_t)]
    bm_t = kb.sb("bm_t", [128, 2, 6]); bm_b = Buf(bm_t)
    o_t = kb.sb("o_t", [128, 2, 6, 3]); o_b = Buf(o_t)
    ps_t = kb.psum("ps", [128, 2, 6, 4]); ps_b = Buf(ps_t)
    kb.begin()
    kb.dma("sp", s_t[:], sT, writes=[s_b])
    kb.dma("sp", bm_t[:], bm, writes=[bm_b])
    for l in range(2):
        kb.dma("sp", w_t[:, l], w[l].rearrange("(kc p) n -> p kc n", p=128), writes=[w_b[l]])
    kb.op("act", lambda: nc.scalar.activation(out=ss_t[:], in_=s_t[:], func=AF.Silu), reads=[s_b], writes=[ss_b])
    for l in range(2):
        for j in range(6):
            for kc in range(8):
                kb.op("pe", lambda l=l, j=j, kc=kc: nc.tensor.matmul(
                    ps_t[:, l, j, 0:3], w_t[:, l, kc, j * 128:(j + 1) * 128], ss_t[:, kc, :],
                    start=(kc == 0), stop=(kc == 7)), reads=[w_b[l], ss_b], writes=[ps_b])
    kb.op("dve", lambda: nc.vector.tensor_tensor(
        out=o_t[:], in0=ps_t[:, :, :, 0:3], in1=bm_t[:].unsqueeze(3).to_broadcast([128, 2, 6, 3]), op=ALU.add),
        reads=[ps_b, bm_b], writes=[o_b])
    kb.dma("sp", out, o_t[:], reads=[o_b], is_out=True)
    return kb.end()


def run_L0(inp):
    c = inp["c"]; c_ctx = inp["c_ctx"]
    s = np.stack([c[0], c[1], c_ctx], axis=1)
    sT = np.ascontiguousarray(s.reshape(8, 128, 3).transpose(1, 0, 2))
    in_maps = []
    for core in range(NCORES):
        w = np.ascontiguousarray(inp["w_mod"][:, :, core * 768:(core + 1) * 768])
        bm = inp["b_mod"][:, core * 768:(core + 1) * 768].reshape(2, 6, 128).transpose(2, 0, 1)
        in_maps.append({"sT": sT, "w": w, "bm": np.ascontiguousarray(bm)})
    res = run(build_L0(), in_maps)
    mod = np.zeros((2, 6144, 3), np.float32)
    for core in range(NCORES):
        o = res[core]["out"]
        mod[:, core * 768:(core + 1) * 768, :] = o.transpose(1, 2, 0, 3).reshape(2, 768, 3)
    return mod


def mod_for_core(mod, l, core):
    b = core // 4
    m = mod[l][:, [b, 2]]
    return np.ascontiguousarray(m.reshape(48, 128, 2).transpose(1, 0, 2))


class PsumRing:
    def __init__(self, kb, n=8, name="ps"):
        self.kb = kb
        self.t = [kb.psum("%s%d" % (name, i), [128, 512], F32) for i in range(n)]
        self.b = [Buf(t) for t in self.t]
        self.i = 0

    def next(self):
        b = self.b[self.i]
        self.i = (self.i + 1) % len(self.b)
        return b


def emit_modprep(kb, mod_t, mod_b, gn_t, gn_b, A_t, A_b, vscale):
    nc = kb.nc
    for col in range(2):
        kb.op("dve", lambda col=col: nc.vector.tensor_scalar(
            out=A_t[:, :, col], in0=mod_t[:, vscale * 8:(vscale + 1) * 8, col], scalar1=1.0, scalar2=32.0,
            op0=ALU.add, op1=ALU.mult), reads=[mod_b], writes=[A_b])
        kb.op("dve", lambda col=col: nc.vector.tensor_tensor(
            out=A_t[:, :, col], in0=A_t[:, :, col], in1=gn_t[:, :], op=ALU.mult), reads=[A_b, gn_b], writes=[A_b])


def emit_norm_block(kb, ring, i, x_t, xb, h_t, hb, sq_t, sqb, tmp_t, tmpb, rs_t, rsb, ones_t, ones_b,
                    A_t, A_b, mod_t, mod_b, vshift, eps_t, eps_b, hlo=None):
    nc = kb.nc
    lo, n = TBS[i]; col = tb_col(i)
    if hlo is None:
        hlo = lo
    sb = sqb[i % 2]; s_t = sq_t
    for kc in range(8):
        kb.op("act", lambda kc=kc: nc.scalar.activation(out=s_t[:, i % 2, kc, :n], in_=x_t[:, kc, lo:lo + n], func=AF.Square),
              reads=[xb], writes=[sb])
    ps = ring.next()
    for kc in range(8):
        kb.op("pe", lambda kc=kc: nc.tensor.matmul(ps.t[:, :n], ones_t[:, :], s_t[:, i % 2, kc, :n], start=(kc == 0), stop=(kc == 7)),
              reads=[sb, ones_b], writes=[ps])
    rb = rsb[i % 2]
    kb.op("act", lambda: nc.scalar.activation(out=rs_t[:, i % 2, :n], in_=ps.t[:, :n], func=AF.Sqrt, bias=eps_t[:, 0:1], scale=1.0),
          reads=[ps, eps_b], writes=[rb])
    kb.op("dve", lambda: nc.vector.reciprocal(out=rs_t[:, i % 2, :n], in_=rs_t[:, i % 2, :n]), reads=[rb], writes=[rb])
    for kc in range(8):
        tb_ = tmpb[kc % 2]
        kb.op("dve", lambda kc=kc: nc.vector.scalar_tensor_tensor(
            out=tmp_t[:, kc % 2, :n], in0=x_t[:, kc, lo:lo + n], scalar=A_t[:, kc, col:col + 1], in1=rs_t[:, i % 2, :n],
            op0=ALU.mult, op1=ALU.mult), reads=[xb, A_b, rb], writes=[tb_])
        kb.op("act", lambda kc=kc: nc.scalar.activation(
            out=h_t[:, kc, hlo:hlo + n], in_=tmp_t[:, kc % 2, :n], func=AF.Identity,
            bias=mod_t[:, vshift * 8 + kc, col:col + 1], scale=1.0), reads=[tb_, mod_b], writes=[hb])


def evac(kb, k, out_ap, in_ap, reads, writes):
    nc = kb.nc
    if k % 2 == 0:
        kb.op("act", lambda: nc.scalar.copy(out=out_ap, in_=in_ap), reads=reads, writes=writes)
    else:
        kb.op("dve", lambda: nc.vector.tensor_copy(out=out_ap, in_=in_ap), reads=reads, writes=writes)


def build_L1():
    kb = KB(); nc = kb.nc
    xT = kb.dram("xT", [128, 8, NT], F32, "ExternalInput")
    mod = kb.dram("mod", [128, 48, 2], F32, "ExternalInput")
    gn = kb.dram("gn", [128, 8], F32, "ExternalInput")
    w = kb.dram("w", [1024, EVEN_IN], F32, "ExternalInput")
    cbd = kb.dram("cbd", [128, 128], F32, "ExternalInput")
    sbd = kb.dram("sbd", [128, 128], F32, "ExternalInput")
    o_qkv = kb.dram("o_qkv", [128, 18, NT], BF16, "ExternalOutput")
    o_o = kb.dram("o_o", [128, 6, NT], F32, "ExternalOutput")
    o_g = kb.dram("o_g", [24, NT], F32, "ExternalOutput")
    o_fr = kb.dram("o_fr", [128, 2, NT], F32, "ExternalOutput")
    o_fi = kb.dram("o_fi", [128, 2, NT], F32, "ExternalOutput")

    x_t = kb.sb("x_t", [128, 8, NT]); xb = [Buf(x_t) for _ in TBS]
    h_t = kb.sb("h_t", [128, 8, NT], BF16); hb = [Buf(h_t) for _ in TBS]
    sq_t = kb.sb("sq_t", [128, 2, 8, 512], BF16); sqb = [Buf(sq_t), Buf(sq_t)]
    tmp_t = kb.sb("tmp_t", [128, 2, 512]); tmpb = [Buf(tmp_t), Buf(tmp_t)]
    rs_t = kb.sb("rs_t", [128, 2, 512]); rsb = [Buf(rs_t), Buf(rs_t)]
    mod_t = kb.sb("mod_t", [128, 48, 2]); mod_b = Buf(mod_t)
    gn_t = kb.sb("gn_t", [128, 8]); gn_b = Buf(gn_t)
    A_t = kb.sb("A_t", [128, 8, 2]); A_b = Buf(A_t)
    ones_t = kb.sb("ones_t", [128, 128], BF16); ones_b = Buf(ones_t)
    eps_t = kb.sb("eps_t", [128, 1]); eps_b = Buf(eps_t)
    cbd_t = kb.sb("cbd_t", [128, 128]); cbd_b = Buf(cbd_t)
    sbd_t = kb.sb("sbd_t", [128, 128]); sbd_b = Buf(sbd_t)
    NWB = 3
    wg_t = kb.sb("wg_t", [128, NWB, 8, 512], BF16); wgb = [Buf(wg_t) for _ in range(NWB)]
    NST = 4
    stb_t = kb.sb("stb_t", [128, NST, 512], BF16); stbb = [Buf(stb_t) for _ in range(NST)]
    stf_t = kb.sb("stf_t", [128, NST, 512]); stfb = [Buf(stf_t) for _ in range(NST)]
    f_t = kb.sb("f_t", [128, 2, NT]); fb = [[Buf(f_t) for _ in TBS] for _ in range(2)]
    ring = PsumRing(kb)
    kb.begin()

    kb.dma("sp", mod_t[:], mod, writes=[mod_b])
    kb.dma("sp", gn_t[:], gn, writes=[gn_b])
    kb.dma("sp", cbd_t[:], cbd, writes=[cbd_b])
    kb.dma("sp", sbd_t[:], sbd, writes=[sbd_b])
    for i, (lo, n) in enumerate(TBS):
        kb.dma("sp", x_t[:, :, lo:lo + n], xT[:, :, lo:lo + n], writes=[xb[i]])
    kb.op("pool", lambda: nc.gpsimd.memset(ones_t[:], 1.0), writes=[ones_b])
    kb.op("pool", lambda: nc.gpsimd.memset(eps_t[:], 1024.0 * EPS), writes=[eps_b])
    groups = [(c0, 512) for c0 in range(0, 3072, 512)] + [(3072, 280)]
    gtok = {}

    def load_group(gi):
        c0, gw = groups[gi]
        gtok[gi] = kb.dma("pool", wg_t[:, gi % NWB, :, :gw], w[:, c0:c0 + gw].rearrange("(kc p) n -> p kc n", p=128),
                          writes=[wgb[gi % NWB]])
    load_group(0); load_group(1)
    emit_modprep(kb, mod_t, mod_b, gn_t, gn_b, A_t, A_b, vscale=1)
    for i in range(len(TBS)):
        emit_norm_block(kb, ring, i, x_t, xb[i], h_t, hb[i], sq_t, sqb, tmp_t, tmpb, rs_t, rsb, ones_t, ones_b,
                        A_t, A_b, mod_t, mod_b, 0, eps_t, eps_b)
    k = 0
    for gi, (c0, gw) in enumerate(groups):
        if gi + 2 < len(groups):
            load_group(gi + 2)
        if c0 < 3072:
            chunks = [(cl, 128) for cl in range(0, gw, 128)]
        else:
            chunks = [(0, 24), (24, 128), (152, 128)]
        for (cl, m) in chunks:
            c = c0 + cl
            for i, (lo, n) in enumerate(TBS):
                ps = ring.next()
                for kc in range(8):
                    kb.op("pe", lambda kc=kc, cl=cl, m=m, lo=lo, n=n, ps=ps: nc.tensor.matmul(
                        ps.t[:m, :n], wg_t[:, gi % NWB, kc, cl:cl + m], h_t[:, kc, lo:lo + n], start=(kc == 0), stop=(kc == 7)),
                        reads=[wgb[gi % NWB], hb[i]], writes=[ps])
                k += 1
                if c < 2304:
                    sb_ = stbb[k % NST]
                    evac(kb, k, stb_t[:, k % NST, :n], ps.t[:, :n], [ps], [sb_])
                    kb.dma("sp", o_qkv[:, c // 128, lo:lo + n], stb_t[:, k % NST, :n], reads=[sb_], is_out=True)
                elif c < 3072:
                    sb_ = stfb[k % NST]
                    evac(kb, k, stf_t[:, k % NST, :n], ps.t[:, :n], [ps], [sb_])
                    kb.dma("sp", o_o[:, (c - 2304) // 128, lo:lo + n], stf_t[:, k % NST, :n], reads=[sb_], is_out=True)
                elif c == 3072:
                    sb_ = stfb[k % NST]
                    evac(kb, k, stf_t[:24, k % NST, :n], ps.t[:24, :n], [ps], [sb_])
                    kb.dma("sp", o_g[:, lo:lo + n], stf_t[:24, k % NST, :n], reads=[sb_], is_out=True)
                else:
                    j = (c - 3096) // 128
                    evac(kb, k, f_t[:, j, lo:lo + n], ps.t[:, :n], [ps], [fb[j][i]])
                    for (tab_t, tab_b, dst) in ((cbd_t, cbd_b, o_fr), (sbd_t, sbd_b, o_fi)):
                        ps2 = ring.next()
                        kb.op("pe", lambda tab_t=tab_t, ps2=ps2, j=j, lo=lo, n=n: nc.tensor.matmul(
                            ps2.t[:, :n], tab_t[:, :], f_t[:, j, lo:lo + n], start=True, stop=True),
                            reads=[tab_b, fb[j][i]], writes=[ps2])
                        k += 1
                        sb_ = stfb[k % NST]
                        evac(kb, k, stf_t[:, k % NST, :n], ps2.t[:, :n], [ps2], [sb_])
                        kb.dma("sp", dst[:, j, lo:lo + n], stf_t[:, k % NST, :n], reads=[sb_], is_out=True)
    return kb.end()


def xT_for_core(x, xc, core):
    b, q = core // 4, core % 4
    t = np.concatenate([x[b, q * TOK:(q + 1) * TOK], xc[b, q * CT:(q + 1) * CT]], axis=0)
    return np.ascontiguousarray(t.reshape(NT, 8, 128).transpose(2, 1, 0))


def fm_vec(v, nchunk):
    return np.ascontiguousarray(np.asarray(v).reshape(nchunk, 128).T)


def dft_bd():
    c = np.arange(64)
    ang = 2 * np.pi * np.outer(c, c) / 64
    cb = np.zeros((128, 128), np.float32); sb = np.zeros((128, 128), np.float32)
    for g in range(2):
        cb[g * 64:(g + 1) * 64, g * 64:(g + 1) * 64] = np.cos(ang)
        sb[g * 64:(g + 1) * 64, g * 64:(g + 1) * 64] = -np.sin(ang)
    return cb, sb


def run_L1(inp, mod, xTs):
    cb, sb = dft_bd()
    w = np.ascontiguousarray(inp["w_in_even"][0])
    gn = fm_vec(inp["g_norm"][0, 0], 8)
    in_maps = [{"xT": xTs[c], "mod": mod_for_core(mod, 0, c), "gn": gn, "w": w, "cbd": cb, "sbd": sb} for c in range(NCORES)]
    return run(build_L1(), in_maps)


NCH = 66; SL = NCH * 128
ALPHA = 128 ** -0.5


def build_L2(U=3, do_mlstm=True, do_fourier=True, phase=9, nscan=NCH):
    kb = KB(); nc = kb.nc
    qTd = [kb.dram("qT%d" % u, [128, SL], BF16, "ExternalInput") for u in range(U)]
    kTd = [kb.dram("kT%d" % u, [128, SL], BF16, "ExternalInput") for u in range(U)]
    vd = [kb.dram("v%d" % u, [128, NCH, 128], BF16, "ExternalInput") for u in range(U)]
    gid = [kb.dram("gi%d" % u, [128, NCH], F32, "ExternalInput") for u in range(U)]
    gfd = [kb.dram("gf%d" % u, [128, NCH], F32, "ExternalInput") for u in range(U)]
    miscd = [kb.dram("misc%d" % u, [128, 8], F32, "ExternalInput") for u in range(U)]
    trid = kb.dram("tri", [128, 128], F32, "ExternalInput")
    maskd = kb.dram("mask", [128, 128], F32, "ExternalInput")
    identfd = kb.dram("identf", [128, 128], F32, "ExternalInput")
    identbd = kb.dram("identb", [128, 128], BF16, "ExternalInput")
    hTd = [kb.dram("hT%d" % u, [128, SL], F32, "ExternalOutput") for u in range(U)]
    XRd = kb.dram("XR", [128, 4096], F32, "ExternalInput"); XId = kb.dram("XI", [128, 4096], F32, "ExternalInput")
    c128d = kb.dram("c128", [128, 128], F32, "ExternalInput"); s128d = kb.dram("s128", [128, 128], F32, "ExternalInput")
    ns128d = kb.dram("ns128", [128, 128], F32, "ExternalInput")
    tcd = kb.dram("tc", [128, 64], F32, "ExternalInput"); tsd = kb.dram("ts", [128, 64], F32, "ExternalInput")
    c64d = kb.dram("c64", [64, 64], F32, "ExternalInput"); s64d = kb.dram("s64", [64, 64], F32, "ExternalInput")
    XcRd = kb.dram("XcR", [128, 2, 64], F32, "ExternalInput"); XcId = kb.dram("XcI", [128, 2, 64], F32, "ExternalInput")
    c256d = kb.dram("c256", [128, 2, 256], F32, "ExternalInput"); s256d = kb.dram("s256", [128, 2, 256], F32, "ExternalInput")
    Yd = kb.dram("Y", [64, 8192], F32, "ExternalOutput"); Ycd = kb.dram("Yc", [128, 2, 64], F32, "ExternalOutput")

    def sbb(name, shape, dt=F32):
        t = kb.sb(name, shape, dt); return t, Buf(t)
    qr_t, qr_b = sbb("qr", [128, SL], BF16); kr_t, kr_b = sbb("kr", [128, SL], BF16)
    qs_t, qs_b = sbb("qs", [128, SL], BF16); ks_t, ks_b = sbb("ks", [128, SL], BF16)
    kh_t, kh_b = sbb("kh", [128, NCH, 128], BF16); v_t, v_b = sbb("v", [128, NCH, 128], BF16)
    gi_t, gi_b = sbb("gi", [128, NCH]); gf_t, gf_b = sbb("gf", [128, NCH]); misc_t, misc_b = sbb("misc", [128, 8])
    nbf_t, nbf_b = sbb("nbf", [128, 1]); l1_t, l1_b = sbb("l1", [128, NCH]); nbc_t, nbc_b = sbb("nbcol", [128, NCH])
    t1_t, t1_b = sbb("t1", [128, NCH]); a_t, a_b = sbb("acol", [128, NCH]); eL_t, eL_b = sbb("eL", [128, NCH])
    gh_t, gh_b = sbb("gh", [128, NCH])
    tri_t, tri_b = sbb("tri", [128, 128]); mask_t, mask_b = sbb("mask", [128, 128])
    idf_t, idf_b = sbb("idf", [128, 128]); idb_t, idb_b = sbb("idb", [128, 128], BF16)
    onef_t, onef_b = sbb("onef", [128, 128]); oneb_t, oneb_b = sbb("oneb", [128, 128], BF16)
    ctmp_t = kb.sb("ctmp", [128, 2, 1024]); ctmp_b = [Buf(ctmp_t), Buf(ctmp_t)]
    st_t = kb.sb("st", [128, 4, 128], BF16); st_b = [Buf(st_t) for _ in range(4)]
    dg_t = kb.sb("dg", [128, 2, 128]); dg_b = [Buf(dg_t), Buf(dg_t)]
    neb_t = kb.sb("neb", [128, 4, 128]); neb_b = [Buf(neb_t) for _ in range(4)]
    dm_t = kb.sb("dm", [128, 2, 128]); dm_b = [Buf(dm_t), Buf(dm_t)]
    Cf_t, Cf_b = sbb("Cf", [128, 128]); Cb_t, Cb_b = sbb("Cb", [128, 128], BF16)
    nf_t, nf_b = sbb("nf", [128, 128]); nbb_t, nbb_b = sbb("nb16", [128, 128], BF16)
    ho_t = kb.sb("ho", [128, 2, 1024]); ho_b = [Buf(ho_t), Buf(ho_t)]
    XR_t = qr_t[:, 0:8192].bitcast(F32); XI_t = kr_t[:, 0:8192].bitcast(F32)
    Ur_t = qs_t[:, 0:8192].bitcast(F32); Ui_t = ks_t[:, 0:8192].bitcast(F32)
    XR_b, XI_b, Ur_b, Ui_b = Buf(XR_t), Buf(XI_t), Buf(Ur_t), Buf(Ui_t)
    c128_t, c128_b = sbb("c128", [128, 128]); s128_t, s128_b = sbb("s128", [128, 128]); ns128_t, ns128_b = sbb("ns128", [128, 128])
    tc_t, tc_b = sbb("tc", [128, 64]); ts_t, ts_b = sbb("ts", [128, 64])
    c64_t, c64_b = sbb("c64", [64, 64]); s64_t, s64_b = sbb("s64", [64, 64])
    XcR_t, XcR_b = sbb("XcR", [128, 2, 64]); XcI_t, XcI_b = sbb("XcI", [128, 2, 64])
    c256_t, c256_b = sbb("c256", [128, 2, 256]); s256_t, s256_b = sbb("s256", [128, 2, 256])
    ft_t = kb.sb("ft", [128, 4, 512]); ft_b = [Buf(ft_t) for _ in range(4)]
    Z_t = kb.sb("Z", [64, 4, 512]); Z_b = [Buf(Z_t) for _ in range(4)]
    Yo_t = kb.sb("Yo", [64, 2, 512]); Yo_b = [Buf(Yo_t), Buf(Yo_t)]; Yc_t, Yc_b = sbb("Yco", [128, 2, 64])
    pst = [kb.psum("pb%d" % i, [128, 512], F32) for i in range(7)]
    ptr_t = kb.psum("ptr", [128, 1024], BF16)
    kb.begin()

    for (t, b, d) in ((tri_t, tri_b, trid), (mask_t, mask_b, maskd), (idf_t, idf_b, identfd), (idb_t, idb_b, identbd)):
        kb.dma("sp", t[:], d, writes=[b])
    kb.op("pool", lambda: nc.gpsimd.memset(onef_t[:], 1.0), writes=[onef_b])
    kb.op("pool", lambda: nc.gpsimd.memset(oneb_t[:], 1.0), writes=[oneb_b])

    def slots(bank):
        b = Buf(pst[bank]); return [b, b, b, b]
    S_ps, dC_ps, dn_ps, nbc_ps, num_ps, den_ps = (slots(i) for i in range(6))
    g_ps = Buf(pst[6])
    tr_ps = [Buf(ptr_t) for _ in range(8)]

    def sl(s):
        return slice(s * 128, (s + 1) * 128)

    for u in range(U if do_mlstm else 0):
        kb.dma("sp", qr_t[:], qTd[u], writes=[qr_b])
        kb.dma("sp", kr_t[:], kTd[u], writes=[kr_b])
        kb.dma("sp", gi_t[:], gid[u], writes=[gi_b]); kb.dma("sp", gf_t[:], gfd[u], writes=[gf_b])
        kb.dma("sp", misc_t[:], miscd[u], writes=[misc_b])
        kb.dma("sp", v_t[:], vd[u], writes=[v_b])
        kb.op("dve", lambda: nc.vector.tensor_scalar(out=nbf_t[:], in0=misc_t[:, 7:8], scalar1=-1.0, scalar2=None, op0=ALU.mult),
              reads=[misc_b], writes=[nbf_b])
        kb.op("act", lambda: nc.scalar.activation(out=l1_t[:], in_=gf_t[:], func=AF.Exp, bias=nbf_t[:, 0:1], scale=-1.0),
              reads=[gf_b, nbf_b], writes=[l1_b])
        kb.op("act", lambda: nc.scalar.activation(out=l1_t[:], in_=l1_t[:], func=AF.Ln, bias=onef_t[:, 0:1], scale=1.0),
              reads=[l1_b, onef_b], writes=[l1_b])
        kb.op("pe", lambda: nc.tensor.matmul(pst[6][:, 0:NCH], tri_t[:, :], l1_t[:, :], start=True, stop=True),
              reads=[tri_b, l1_b], writes=[g_ps])
        kb.op("pe", lambda: nc.tensor.matmul(pst[6][:, 128:128 + NCH], onef_t[:, :], l1_t[:, :], start=True, stop=True),
              reads=[onef_b, l1_b], writes=[g_ps])
        kb.op("dve", lambda: nc.vector.tensor_copy(out=nbc_t[:], in_=pst[6][:, 0:NCH]), reads=[g_ps], writes=[nbc_b])
        kb.op("dve", lambda: nc.vector.scalar_tensor_tensor(out=t1_t[:], in0=gi_t[:], scalar=misc_t[:, 6:7], in1=nbc_t[:],
                                                            op0=ALU.add, op1=ALU.add), reads=[gi_b, misc_b, nbc_b], writes=[t1_b])
        kb.op("act", lambda: nc.scalar.activation(out=a_t[:], in_=t1_t[:], func=AF.Exp), reads=[t1_b], writes=[a_b])
        kb.op("act", lambda: nc.scalar.activation(out=eL_t[:], in_=pst[6][:, 128:128 + NCH], func=AF.Exp, scale=-1.0),
              reads=[g_ps], writes=[eL_b])
        kb.op("dve", lambda: nc.vector.scalar_tensor_tensor(out=gh_t[:], in0=a_t[:], scalar=ALPHA, in1=eL_t[:],
                                                            op0=ALU.mult, op1=ALU.mult), reads=[a_b, eL_b], writes=[gh_b])
        if phase < 2:
            continue
        blocks = [(0, 256, 0, 256)] + [(256 + j * 1024, 256 + (j + 1) * 1024, 256, SL) for j in range(8)]
        bi = 0
        for (raw_t, raw_b, o_t, o_b, tb0) in ((qr_t, qr_b, qs_t, qs_b, 0), (kr_t, kr_b, ks_t, ks_b, 3)):
            for (lo, hi, a, b) in blocks:
                n = hi - lo; cb = ctmp_b[bi % 2]; ci = bi % 2; bi += 1
                kb.op("act", lambda raw_t=raw_t, lo=lo, hi=hi, n=n, ci=ci, tb0=tb0: nc.scalar.activation(
                    out=ctmp_t[:, ci, :n], in_=raw_t[:, lo:hi], func=AF.Copy, scale=misc_t[:, tb0 + 1:tb0 + 2]),
                    reads=[raw_b, misc_b], writes=[cb])
                j0 = max(lo, a + 1)
                kb.op("dve", lambda raw_t=raw_t, lo=lo, hi=hi, n=n, ci=ci, tb0=tb0, j0=j0: nc.vector.scalar_tensor_tensor(
                    out=ctmp_t[:, ci, j0 - lo:n], in0=raw_t[:, j0 - 1:hi - 1], scalar=misc_t[:, tb0:tb0 + 1],
                    in1=ctmp_t[:, ci, j0 - lo:n], op0=ALU.mult, op1=ALU.add), reads=[raw_b, misc_b, cb], writes=[cb])
                j1 = min(hi, b - 1)
                kb.op("dve", lambda raw_t=raw_t, lo=lo, n=n, ci=ci, tb0=tb0, j1=j1: nc.vector.scalar_tensor_tensor(
                    out=ctmp_t[:, ci, 0:j1 - lo], in0=raw_t[:, lo + 1:j1 + 1], scalar=misc_t[:, tb0 + 2:tb0 + 3],
                    in1=ctmp_t[:, ci, 0:j1 - lo], op0=ALU.mult, op1=ALU.add), reads=[raw_b, misc_b, cb], writes=[cb])
                kb.op("act", lambda o_t=o_t, lo=lo, hi=hi, n=n, ci=ci: nc.scalar.activation(
                    out=o_t[:, lo:hi], in_=ctmp_t[:, ci, :n], func=AF.Silu), reads=[cb], writes=[o_b])
        if phase < 3:
            continue
        for c in range(NCH):
            tp = num_ps[c % 4]
            kb.op("pe", lambda c=c: nc.tensor.matmul(pst[4][:, sl(0)], ks_t[:, sl(c)], idb_t[:, :], start=True, stop=True),
                  reads=[ks_b, idb_b], writes=[tp])
            kb.op("dve", lambda c=c: nc.vector.tensor_scalar(out=kh_t[:, c, :], in0=pst[4][:, sl(0)],
                                                             scalar1=gh_t[:, c:c + 1], scalar2=None, op0=ALU.mult),
                  reads=[tp, gh_b], writes=[kh_b])
        if phase < 4:
            continue
        kb.op("pool", lambda: nc.gpsimd.memset(Cf_t[:], 0.0), writes=[Cf_b])
        kb.op("pool", lambda: nc.gpsimd.memset(Cb_t[:], 0.0), writes=[Cb_b])
        kb.op("pool", lambda: nc.gpsimd.memset(nf_t[:], 0.0), writes=[nf_b])
        kb.op("pool", lambda: nc.gpsimd.memset(nbb_t[:], 0.0), writes=[nbb_b])

        def stage_A(c):
            s = c % 4
            kb.op("pe", lambda: nc.tensor.matmul(pst[0][:, sl(0)], ks_t[:, sl(c)], qs_t[:, sl(c)], start=True, stop=True),
                  reads=[ks_b, qs_b], writes=[S_ps[s]])
            kb.op("dve", lambda: nc.vector.scalar_tensor_tensor(out=st_t[:, s, :], in0=pst[0][:, sl(0)], scalar=a_t[:, c:c + 1],
                                                                in1=mask_t[:, :], op0=ALU.mult, op1=ALU.mult),
                  reads=[S_ps[s], a_b, mask_b], writes=[st_b[s]])
            kb.op("dve", lambda: nc.vector.tensor_scalar(out=dg_t[:, c % 2, :], in0=idf_t[:, :], scalar1=nbc_t[:, c:c + 1], scalar2=None,
                                                         op0=ALU.mult), reads=[idf_b, nbc_b], writes=[dg_b[c % 2]])
            kb.op("pe", lambda: nc.tensor.matmul(pst[3][:, sl(0)], onef_t[:, :], dg_t[:, c % 2, :], start=True, stop=True),
                  reads=[onef_b, dg_b[c % 2]], writes=[nbc_ps[s]])
            kb.op("act", lambda: nc.scalar.activation(out=neb_t[:, s, :], in_=pst[3][:, sl(0)], func=AF.Exp),
                  reads=[nbc_ps[s]], writes=[neb_b[s]])

        def stage_B(c):
            s = c % 4
            kb.op("pe", lambda: nc.tensor.matmul(pst[4][:, sl(0)], v_t[:, c, :], st_t[:, s, :], start=True, stop=False),
                  reads=[v_b, st_b[s]], writes=[num_ps[s]])
            kb.op("pe", lambda: nc.tensor.matmul(pst[4][:, sl(0)], Cb_t[:, :], qs_t[:, sl(c)], start=False, stop=True),
                  reads=[Cb_b, qs_b], writes=[num_ps[s]])
            kb.op("pe", lambda: nc.tensor.matmul(pst[5][:, sl(0)], oneb_t[:, :], st_t[:, s, :], start=True, stop=False),
                  reads=[oneb_b, st_b[s]], writes=[den_ps[s]])
            kb.op("pe", lambda: nc.tensor.matmul(pst[5][:, sl(0)], nbb_t[:, :], qs_t[:, sl(c)], start=False, stop=True),
                  reads=[nbb_b, qs_b], writes=[den_ps[s]])
            d = c % 2
            kb.op("act", lambda: nc.scalar.activation(out=dm_t[:, d, :], in_=pst[5][:, sl(0)], func=AF.Abs), reads=[den_ps[s]], writes=[dm_b[d]])
            kb.op("dve", lambda: nc.vector.tensor_tensor(out=dm_t[:, d, :], in0=dm_t[:, d, :], in1=neb_t[:, s, :], op=ALU.max),
                  reads=[dm_b[d], neb_b[s]], writes=[dm_b[d]])
            kb.op("dve", lambda: nc.vector.reciprocal(out=dm_t[:, d, :], in_=dm_t[:, d, :]), reads=[dm_b[d]], writes=[dm_b[d]])
            g = c // 8; hb_ = ho_b[g % 2]
            kb.op("dve", lambda: nc.vector.tensor_tensor(out=ho_t[:, g % 2, (c % 8) * 128:(c % 8 + 1) * 128], in0=pst[4][:, sl(0)],
                                                         in1=dm_t[:, d, :], op=ALU.mult), reads=[num_ps[s], dm_b[d]], writes=[hb_])
            if c % 8 == 7 or c == NCH - 1:
                w = (c % 8 + 1) * 128
                kb.dma("sp", hTd[u][:, g * 1024:g * 1024 + w], ho_t[:, g % 2, :w], reads=[hb_], is_out=True)

        def stage_U(c):
            s = c % 4
            kb.op("pe", lambda: nc.tensor.matmul(pst[1][:, sl(0)], kh_t[:, c, :], v_t[:, c, :], start=True, stop=True),
                  reads=[kh_b, v_b], writes=[dC_ps[s]])
            kb.op("pe", lambda: nc.tensor.matmul(pst[2][:, sl(0)], kh_t[:, c, :], oneb_t[:, :], start=True, stop=True),
                  reads=[kh_b, oneb_b], writes=[dn_ps[s]])
            kb.op("dve", lambda: nc.vector.scalar_tensor_tensor(out=Cf_t[:, :], in0=Cf_t[:, :], scalar=eL_t[:, c:c + 1], in1=pst[1][:, sl(0)],
                                                                op0=ALU.mult, op1=ALU.add), reads=[Cf_b, eL_b, dC_ps[s]], writes=[Cf_b])
            kb.op("act", lambda: nc.scalar.copy(out=Cb_t[:, :], in_=Cf_t[:, :]), reads=[Cf_b], writes=[Cb_b])
            kb.op("dve", lambda: nc.vector.scalar_tensor_tensor(out=nf_t[:, :], in0=nf_t[:, :], scalar=eL_t[:, c:c + 1], in1=pst[2][:, sl(0)],
                                                                op0=ALU.mult, op1=ALU.add), reads=[nf_b, eL_b, dn_ps[s]], writes=[nf_b])
            kb.op("act", lambda: nc.scalar.copy(out=nbb_t[:, :], in_=nf_t[:, :]), reads=[nf_b], writes=[nbb_b])

        stage_A(0)
        for c in range(nscan):
            if c + 1 < nscan:
                stage_A(c + 1)
            stage_B(c)
            stage_U(c)

    kb.barrier()
    if not do_fourier:
        return kb.end()
    fps = [Buf(pst[i]) for i in range(7)]
    for (t, b, d) in ((XR_t, XR_b, XRd), (XI_t, XI_b, XId), (c128_t, c128_b, c128d), (s128_t, s128_b, s128d), (ns128_t, ns128_b, ns128d),
                      (tc_t, tc_b, tcd), (ts_t, ts_b, tsd), (c64_t, c64_b, c64d), (s64_t, s64_b, s64d), (XcR_t, XcR_b, XcRd),
                      (XcI_t, XcI_b, XcId), (c256_t, c256_b, c256d), (s256_t, s256_b, s256d)):
        kb.dma("sp", t[:], d, writes=[b])
    pi = 0
    for blk in range(8):
        cs = slice(blk * 512, (blk + 1) * 512)
        pr = fps[pi % 7]; pi += 1; pim = fps[pi % 7]; pi += 1
        prt, pit = pr.t, pim.t
        kb.op("pe", lambda: nc.tensor.matmul(prt[:, :], c128_t[:, :], XR_t[:, cs], start=True, stop=False), reads=[c128_b, XR_b], writes=[pr])
        kb.op("pe", lambda: nc.tensor.matmul(prt[:, :], s128_t[:, :], XI_t[:, cs], start=False, stop=True), reads=[s128_b, XI_b], writes=[pr])
        kb.op("pe", lambda: nc.tensor.matmul(pit[:, :], c128_t[:, :], XI_t[:, cs], start=True, stop=False), reads=[c128_b, XI_b], writes=[pim])
        kb.op("pe", lambda: nc.tensor.matmul(pit[:, :], ns128_t[:, :], XR_t[:, cs], start=False, stop=True), reads=[ns128_b, XR_b], writes=[pim])
        tcb = tc_t[:, blk * 8:(blk + 1) * 8].unsqueeze(2).to_broadcast([128, 8, 64])
        tsb = ts_t[:, blk * 8:(blk + 1) * 8].unsqueeze(2).to_broadcast([128, 8, 64])

        def v3(ap):
            return ap.rearrange("p (a b) -> p a b", b=64)
        for (o_t, o_b, pa, pa_b, pb, pb_b, op2) in ((Ur_t, Ur_b, prt, pr, pit, pim, ALU.add), (Ui_t, Ui_b, pit, pim, prt, pr, ALU.subtract)):
            kb.op("dve", lambda pa=pa: nc.vector.tensor_tensor(out=v3(ft_t[:, 0, :]), in0=v3(pa[:, :]), in1=tcb, op=ALU.mult),
                  reads=[pa_b, tc_b], writes=[ft_b[0]])
            kb.op("dve", lambda pb=pb: nc.vector.tensor_tensor(out=v3(ft_t[:, 1, :]), in0=v3(pb[:, :]), in1=tsb, op=ALU.mult),
                  reads=[pb_b, ts_b], writes=[ft_b[1]])
            kb.op("pool", lambda o_t=o_t, op2=op2: nc.gpsimd.tensor_tensor(out=o_t[:, cs], in0=ft_t[:, 0, :], in1=ft_t[:, 1, :], op=op2),
                  reads=[ft_b[0], ft_b[1]], writes=[o_b])
    for j in range(16):
        zr = fps[pi % 7]; pi += 1; zi = fps[pi % 7]; pi += 1
        for (U_t, U_b, z) in ((Ur_t, Ur_b, zr), (Ui_t, Ui_b, zi)):
            for q in range(4):
                cp = 4 * j + q
                kb.op("pe", lambda U_t=U_t, z=z, q=q, cp=cp: nc.tensor.matmul(
                    z.t[:64, q * 128:(q + 1) * 128], U_t[:, :].rearrange("p (t c) -> p t c", c=64)[:, :, cp], idf_t[:, :], start=True, stop=True),
                    reads=[U_b, idf_b], writes=[z])
        zs = (2 * j) % 4
        kb.op("act", lambda: nc.scalar.copy(out=Z_t[:, zs, :], in_=zr.t[:64, :]), reads=[zr], writes=[Z_b[zs]])
        kb.op("dve", lambda: nc.vector.tensor_copy(out=Z_t[:, zs + 1, :], in_=zi.t[:64, :]), reads=[zi], writes=[Z_b[zs + 1]])
        yp = fps[pi % 7]; pi += 1
        kb.op("pe", lambda: nc.tensor.matmul(yp.t[:64, :], c64_t[:, :], Z_t[:, zs, :], start=True, stop=False), reads=[c64_b, Z_b[zs]], writes=[yp])
        kb.op("pe", lambda: nc.tensor.matmul(yp.t[:64, :], s64_t[:, :], Z_t[:, zs + 1, :], start=False, stop=True), reads=[s64_b, Z_b[zs + 1]], writes=[yp])
        evac(kb, j, Yo_t[:, j % 2, :], yp.t[:64, :], [yp], [Yo_b[j % 2]])
        kb.dma("sp", Yd[:, j * 512:(j + 1) * 512], Yo_t[:, j % 2, :], reads=[Yo_b[j % 2]], is_out=True)
    for k2 in range(2):
        yp = fps[pi % 7]; pi += 1
        n = 0
        for (tab_t, tab_b, X_t, X_b) in ((c256_t, c256_b, XcR_t, XcR_b), (s256_t, s256_b, XcI_t, XcI_b)):
            for tcn in range(2):
                kb.op("pe", lambda tab_t=tab_t, X_t=X_t, tcn=tcn, n=n: nc.tensor.matmul(
                    yp.t[:, :64], tab_t[:, tcn, k2 * 128:(k2 + 1) * 128], X_t[:, tcn, :], start=(n == 0), stop=(n == 3)),
                    reads=[tab_b, X_b], writes=[yp])
                n += 1
        kb.op("dve", lambda: nc.vector.tensor_copy(out=Yc_t[:, k2, :], in_=yp.t[:, :64]), reads=[yp], writes=[Yc_b])
    kb.dma("sp", Ycd, Yc_t[:], reads=[Yc_b], is_out=True)
    return kb.end()


def gather_fm(res, key, nrows_chunks, dtype):
    lat = [np.zeros((nrows_chunks * 128, SEQ), dtype) for _ in range(NB)]
    ctx = [np.zeros((nrows_chunks * 128, CTX), dtype) for _ in range(NB)]
    for core in range(NCORES):
        b, q = core // 4, core % 4
        a = res[core][key]
        a = a.transpose(1, 0, 2).reshape(nrows_chunks * 128, NT)
        lat[b][:, q * TOK:(q + 1) * TOK] = a[:, :TOK]
        ctx[b][:, q * CT:(q + 1) * CT] = a[:, TOK:]
    return lat, ctx


def l2_consts():
    s = np.arange(128)
    tri = (s[:, None] <= s[None, :]).astype(np.float32)
    mask = tri * np.float32(ALPHA)
    ident = np.eye(128, dtype=np.float32)
    t1 = np.arange(128)
    a128 = 2 * np.pi * np.outer(t1, t1) / 128
    k1 = np.arange(128)[:, None]; t2 = np.arange(64)[None, :]
    atw = 2 * np.pi * k1 * t2 / SEQ
    a64 = 2 * np.pi * np.outer(np.arange(64), np.arange(64)) / 64
    sc = 1.0 / np.sqrt(SEQ * 64.0)
    t = np.arange(256); a256 = 2 * np.pi * np.outer(t, t) / 256
    scc = 1.0 / np.sqrt(256 * 64.0)
    c256 = (np.cos(a256) * scc).reshape(2, 128, 256).transpose(1, 0, 2)
    s256 = (np.sin(a256) * scc).reshape(2, 128, 256).transpose(1, 0, 2)
    f = lambda x: np.ascontiguousarray(x, dtype=np.float32)
    return dict(tri=tri, mask=mask, identf=ident, identb=ident.astype(NPBF),
                c128=f(np.cos(a128)), s128=f(np.sin(a128)), ns128=f(-np.sin(a128)),
                tc=f(np.cos(atw)), ts=f(np.sin(atw)), c64=f(np.cos(a64) * sc), s64=f(np.sin(a64) * sc),
                c256=f(c256), s256=f(s256))


def run_L2(inp, r1):
    qkv_l, qkv_c = gather_fm(r1, "o_qkv", 18, NPBF)
    fr_l, fr_c = gather_fm(r1, "o_fr", 2, np.float32)
    fi_l, fi_c = gather_fm(r1, "o_fi", 2, np.float32)
    G_l = [np.zeros((24, SEQ), np.float32) for _ in range(NB)]; G_c = [np.zeros((24, CTX), np.float32) for _ in range(NB)]
    for core in range(NCORES):
        b, q = core // 4, core % 4
        g = r1[core]["o_g"]
        G_l[b][:, q * TOK:(q + 1) * TOK] = g[:, :TOK]; G_c[b][:, q * CT:(q + 1) * CT] = g[:, TOK:]
    consts = l2_consts()
    bg = inp["b_gate"][0]; wc = inp["w_qk_conv"][0]

    def seqcat(c, l, rev):
        if rev:
            c = c[..., ::-1]; l = l[..., ::-1]
        return np.concatenate([c, l], axis=-1)
    in_maps = []
    for core in range(NCORES):
        m = dict(consts)
        for u in range(3):
            uid = core * 3 + u
            b, hd, dr = uid // 12, (uid % 12) // 2, uid % 2
            rows = lambda base: slice(base + hd * 128, base + (hd + 1) * 128)
            m["qT%d" % u] = np.ascontiguousarray(seqcat(qkv_c[b][rows(0)], qkv_l[b][rows(0)], dr))
            m["kT%d" % u] = np.ascontiguousarray(seqcat(qkv_c[b][rows(768)], qkv_l[b][rows(768)], dr))
            vT = seqcat(qkv_c[b][rows(1536)], qkv_l[b][rows(1536)], dr)
            m["v%d" % u] = np.ascontiguousarray(vT.T.reshape(NCH, 128, 128).transpose(1, 0, 2))
            gi = seqcat(G_c[b][dr * 12 + hd], G_l[b][dr * 12 + hd], dr)
            gf = seqcat(G_c[b][dr * 12 + 6 + hd], G_l[b][dr * 12 + 6 + hd], dr)
            m["gi%d" % u] = np.ascontiguousarray(gi.reshape(NCH, 128).T)
            m["gf%d" % u] = np.ascontiguousarray(gf.reshape(NCH, 128).T)
            misc = np.zeros((128, 8), np.float32)
            qt = wc[:, hd * 128:(hd + 1) * 128]; kt = wc[:, 768 + hd * 128:768 + (hd + 1) * 128]
            if dr:
                qt = qt[::-1]; kt = kt[::-1]
            misc[:, 0:3] = qt.T; misc[:, 3:6] = kt.T
            misc[:, 6] = bg[dr * 12 + hd]; misc[:, 7] = bg[dr * 12 + 6 + hd]
            m["misc%d" % u] = misc
        b, grp = core // 4, core % 4
        rows = slice(grp * 64, (grp + 1) * 64)
        m["XR"] = np.ascontiguousarray(fr_l[b][rows].T.reshape(128, 4096))
        m["XI"] = np.ascontiguousarray(fi_l[b][rows].T.reshape(128, 4096))
        m["XcR"] = np.ascontiguousarray(fr_c[b][rows].T.reshape(2, 128, 64).transpose(1, 0, 2))
        m["XcI"] = np.ascontiguousarray(fi_c[b][rows].T.reshape(2, 128, 64).transpose(1, 0, 2))
        in_maps.append(m)
    import os
    res = run(build_L2(do_mlstm=os.environ.get('NO_MLSTM') is None, do_fourier=os.environ.get('NO_FOURIER') is None, phase=int(os.environ.get('PHASE', '9')), nscan=int(os.environ.get('NSCAN', str(NCH))), U=int(os.environ.get('NU', '3'))), in_maps)
    H = [[np.zeros((768, SL), np.float32) for _ in range(2)] for _ in range(NB)]
    for core in range(NCORES):
        for u in range(3):
            uid = core * 3 + u
            b, hd, dr = uid // 12, (uid % 12) // 2, uid % 2
            h = res[core]["hT%d" % u]
            if dr:
                h = np.concatenate([h[:, :CTX][:, ::-1], h[:, CTX:][:, ::-1]], axis=1)
            H[b][dr][hd * 128:(hd + 1) * 128] = h
    four_l = [np.zeros((256, SEQ), np.float32) for _ in range(NB)]; four_c = [np.zeros((256, CTX), np.float32) for _ in range(NB)]
    for core in range(NCORES):
        b, grp = core // 4, core % 4
        Y = res[core]["Y"].reshape(64, 64, 128)
        four_l[b][grp * 64:(grp + 1) * 64] = Y.transpose(1, 0, 2).reshape(64, SEQ)
        Yc = res[core]["Yc"]
        four_c[b][grp * 64:(grp + 1) * 64] = Yc.transpose(2, 1, 0).reshape(64, CTX)
    return H, four_l, four_c


def emit_wout_res(kb, ring, i, n, col, m_t, m_b, w_t, w_b, x_ap_fn, x_b, mod_t, mod_b, vgate):
    nc = kb.nc
    for oc in range(8):
        ps = ring.next()
        for kc in range(8):
            kb.op("pe", lambda kc=kc, oc=oc, ps=ps: nc.tensor.matmul(ps.t[:, :n], w_t[:, kc, oc * 128:(oc + 1) * 128], m_t[:, kc, :n],
                                                                   start=(kc == 0), stop=(kc == 7)), reads=[w_b, m_b], writes=[ps])
        kb.op("dve", lambda oc=oc, ps=ps: nc.vector.scalar_tensor_tensor(
            out=x_ap_fn(oc), in0=ps.t[:, :n], scalar=mod_t[:, vgate * 8 + oc, col:col + 1], in1=x_ap_fn(oc), op0=ALU.mult, op1=ALU.add),
            reads=[ps, mod_b, x_b], writes=[x_b])


def emit_headnorm(kb, ring, src_t, src_b, n, sq_t, sq_b, rs_t, rs_b, ones_t, ones_b, eps_t, eps_b):
    nc = kb.nc
    kb.op("act", lambda: nc.scalar.activation(out=sq_t, in_=src_t, func=AF.Square), reads=[src_b], writes=[sq_b])
    ps = ring.next()
    kb.op("pe", lambda: nc.tensor.matmul(ps.t[:, :n], ones_t[:, :], sq_t, start=True, stop=True), reads=[sq_b, ones_b], writes=[ps])
    kb.op("act", lambda: nc.scalar.activation(out=rs_t, in_=ps.t[:, :n], func=AF.Sqrt, bias=eps_t[:, 0:1], scale=1.0),
          reads=[ps, eps_b], writes=[rs_b])
    kb.op("dve", lambda: nc.vector.reciprocal(out=rs_t, in_=rs_t), reads=[rs_b], writes=[rs_b])


def build_L3():
    kb = KB(); nc = kb.nc
    xT = kb.dram("xT", [128, 8, NT], F32, "ExternalInput")
    hfd = kb.dram("hf", [128, 6, NT], F32, "ExternalInput"); hbd = kb.dram("hb", [128, 6, NT], F32, "ExternalInput")
    od = kb.dram("o", [128, 6, NT], F32, "ExternalInput"); yfd = kb.dram("yf", [128, 2, NT], F32, "ExternalInput")
    ghd = kb.dram("gh", [128, 6], F32, "ExternalInput"); modd = kb.dram("mod", [128, 48, 2], F32, "ExternalInput")
    wd = kb.dram("w", [1024, 1024], F32, "ExternalInput")
    xo = kb.dram("xo", [128, 8, NT], F32, "ExternalOutput")

    def sbb(name, shape, dt=F32, nb=1):
        t = kb.sb(name, shape, dt)
        return (t, Buf(t)) if nb == 1 else (t, [Buf(t) for _ in range(nb)])
    x_t, x_b = sbb("x", [128, 2, 8, 512], nb=2)
    hf_t, hf_b = sbb("hf", [128, 2, 6, 512], nb=2); hb_t, hb_b = sbb("hb", [128, 2, 6, 512], nb=2)
    o_t, o_b = sbb("o", [128, 2, 6, 512], nb=2); yf_t, yf_b = sbb("yf", [128, 2, 2, 512], nb=2)
    gh_t, gh_b = sbb("gh", [128, 6]); mod_t, mod_b = sbb("mod", [128, 48, 2])
    w_t, w_b = sbb("w", [128, 8, 1024], BF16)
    hs_t, hs_b = sbb("hs", [128, 2, 512], nb=2); sq_t, sq_b = sbb("sq", [128, 2, 512], BF16, nb=2)
    rs_t, rs_b = sbb("rs", [128, 2, 512], nb=2); sg_t, sg_b = sbb("sg", [128, 2, 512], nb=2)
    m_t, m_b = sbb("m", [128, 2, 8, 512], BF16, nb=2)
    ones_t, ones_b = sbb("ones", [128, 128], BF16); eps_t, eps_b = sbb("eps", [128, 1])
    ring = PsumRing(kb)
    kb.begin()
    kb.dma("sp", gh_t[:], ghd, writes=[gh_b]); kb.dma("sp", mod_t[:], modd, writes=[mod_b])
    kb.dma("pool", w_t[:], wd.rearrange("(kc p) n -> p kc n", p=128), writes=[w_b])
    kb.op("pool", lambda: nc.gpsimd.memset(ones_t[:], 1.0), writes=[ones_b])
    kb.op("pool", lambda: nc.gpsimd.memset(eps_t[:], 128.0 * EPS), writes=[eps_b])
    kb.op("dve", lambda: nc.vector.tensor_scalar(out=gh_t[:], in0=gh_t[:], scalar1=float(np.sqrt(128.0)), scalar2=None, op0=ALU.mult),
          reads=[gh_b], writes=[gh_b])
    for i, (lo, n) in enumerate(TBS):
        d = i % 2; col = tb_col(i)
        kb.dma("sp", x_t[:, d, :, :n], xT[:, :, lo:lo + n], writes=[x_b[d]])
        kb.dma("sp", hf_t[:, d, :, :n], hfd[:, :, lo:lo + n], writes=[hf_b[d]])
        kb.dma("sp", hb_t[:, d, :, :n], hbd[:, :, lo:lo + n], writes=[hb_b[d]])
        kb.dma("sp", o_t[:, d, :, :n], od[:, :, lo:lo + n], writes=[o_b[d]])
        kb.dma("sp", yf_t[:, d, :, :n], yfd[:, :, lo:lo + n], writes=[yf_b[d]])
        for hd in range(6):
            e = hd % 2
            kb.op("dve", lambda hd=hd, e=e: nc.vector.tensor_tensor(out=hs_t[:, e, :n], in0=hf_t[:, d, hd, :n], in1=hb_t[:, d, hd, :n], op=ALU.add),
                  reads=[hf_b[d], hb_b[d]], writes=[hs_b[e]])
            emit_headnorm(kb, ring, hs_t[:, e, :n], hs_b[e], n, sq_t[:, e, :n], sq_b[e], rs_t[:, e, :n], rs_b[e], ones_t, ones_b, eps_t, eps_b)
            kb.op("act", lambda hd=hd, e=e: nc.scalar.activation(out=sg_t[:, e, :n], in_=o_t[:, d, hd, :n], func=AF.Sigmoid),
                  reads=[o_b[d]], writes=[sg_b[e]])
            kb.op("dve", lambda hd=hd, e=e: nc.vector.scalar_tensor_tensor(out=hs_t[:, e, :n], in0=hs_t[:, e, :n], scalar=gh_t[:, hd:hd + 1],
                                                                           in1=rs_t[:, e, :n], op0=ALU.mult, op1=ALU.mult),
                  reads=[hs_b[e], gh_b, rs_b[e]], writes=[hs_b[e]])
            kb.op("pool", lambda hd=hd, e=e: nc.gpsimd.tensor_tensor(out=m_t[:, d, hd, :n], in0=hs_t[:, e, :n], in1=sg_t[:, e, :n], op=ALU.mult),
                  reads=[hs_b[e], sg_b[e]], writes=[m_b[d]])
        for j in range(2):
            evac(kb, j, m_t[:, d, 6 + j, :n], yf_t[:, d, j, :n], [yf_b[d]], [m_b[d]])
        emit_wout_res(kb, ring, i, n, col, m_t[:, d], m_b[d], w_t, w_b, lambda oc, d=d, n=n: x_t[:, d, oc, :n], x_b[d], mod_t, mod_b, 2)
        kb.dma("sp", xo[:, :, lo:lo + n], x_t[:, d, :, :n], reads=[x_b[d]], is_out=True)
    return kb.end()


def build_FFN(final):
    kb = KB(); nc = kb.nc
    xT = kb.dram("xT", [128, 8, NT], F32, "ExternalInput")
    modd = kb.dram("mod", [128, 48, 2], F32, "ExternalInput")
    gnd = kb.dram("gn", [128, 8], F32, "ExternalInput")
    w1d = kb.dram("w1", [1024, 4096], F32, "ExternalInput"); w2d = kb.dram("w2", [4096, 1024], F32, "ExternalInput")
    gfd = kb.dram("gf", [128, 8], F32, "ExternalInput")
    xo = kb.dram("xo", [128, 8, NT], F32, "ExternalOutput")

    def sbb(name, shape, dt=F32, nb=1):
        t = kb.sb(name, shape, dt)
        return (t, Buf(t)) if nb == 1 else (t, [Buf(t) for _ in range(nb)])
    x_t = kb.sb("x", [128, 8, NT]); xb = [Buf(x_t) for _ in TBS]
    h_t = kb.sb("h", [128, 8, NT], BF16); hb = [Buf(h_t) for _ in TBS]
    sq_t, sqb = sbb("sq", [128, 2, 8, 512], BF16, nb=2)
    tmp_t, tmpb = sbb("tmp", [128, 2, 512], nb=2); rs_t, rsb = sbb("rs", [128, 2, 512], nb=2)
    mod_t, mod_b = sbb("mod", [128, 48, 2]); gn_t, gn_b = sbb("gn", [128, 8]); gf_t, gf_b = sbb("gf", [128, 8])
    A_t, A_b = sbb("A", [128, 8, 2]); ones_t, ones_b = sbb("ones", [128, 128], BF16); eps_t, eps_b = sbb("eps", [128, 1])
    w1_t, w1_b = sbb("w1", [128, 2, 8, 1024], BF16, nb=2); w2_t, w2_b = sbb("w2", [128, 2, 8, 1024], BF16, nb=2)
    a_t, a_b = sbb("a", [128, 2, 8, 512], BF16, nb=2); r_t, r_b = sbb("r", [128, 2, 512], nb=2)
    ring = PsumRing(kb)
    kb.begin()
    kb.dma("sp", mod_t[:], modd, writes=[mod_b]); kb.dma("sp", gn_t[:], gnd, writes=[gn_b]); kb.dma("sp", gf_t[:], gfd, writes=[gf_b])
    for i, (lo, n) in enumerate(TBS):
        kb.dma("sp", x_t[:, :, lo:lo + n], xT[:, :, lo:lo + n], writes=[xb[i]])
    kb.op("pool", lambda: nc.gpsimd.memset(ones_t[:], 1.0), writes=[ones_b])
    kb.op("pool", lambda: nc.gpsimd.memset(eps_t[:], 1024.0 * EPS), writes=[eps_b])

    def load_w(p):
        kb.dma("pool", w1_t[:, p % 2], w1d[:, p * 1024:(p + 1) * 1024].rearrange("(kc p) n -> p kc n", p=128), writes=[w1_b[p % 2]])
        kb.dma("pool", w2_t[:, p % 2], w2d[p * 1024:(p + 1) * 1024, :].rearrange("(kc p) n -> p kc n", p=128), writes=[w2_b[p % 2]])
    load_w(0)
    emit_modprep(kb, mod_t, mod_b, gn_t, gn_b, A_t, A_b, vscale=4)
    for i in range(len(TBS)):
        emit_norm_block(kb, ring, i, x_t, xb[i], h_t, hb[i], sq_t, sqb, tmp_t, tmpb, rs_t, rsb, ones_t, ones_b,
                        A_t, A_b, mod_t, mod_b, 3, eps_t, eps_b)
    k = 0
    for p in range(4):
        if p + 1 < 4:
            load_w(p + 1)
        wq = p % 2
        for i, (lo, n) in enumerate(TBS):
            col = tb_col(i); d = k % 2; k += 1
            for fc in range(8):
                ps = ring.next()
                for kc in range(8):
                    kb.op("pe", lambda kc=kc, fc=fc, ps=ps: nc.tensor.matmul(ps.t[:, :n], w1_t[:, wq, kc, fc * 128:(fc + 1) * 128], h_t[:, kc, lo:lo + n],
                                                                           start=(kc == 0), stop=(kc == 7)), reads=[w1_b[wq], hb[i]], writes=[ps])
                e = fc % 2
                kb.op("act", lambda ps=ps, e=e: nc.scalar.activation(out=r_t[:, e, :n], in_=ps.t[:, :n], func=AF.Relu), reads=[ps], writes=[r_b[e]])
                kb.op("pool", lambda fc=fc, e=e, d=d: nc.gpsimd.tensor_tensor(out=a_t[:, d, fc, :n], in0=r_t[:, e, :n], in1=r_t[:, e, :n], op=ALU.mult),
                      reads=[r_b[e]], writes=[a_b[d]])
            emit_wout_res(kb, ring, i, n, col, a_t[:, d], a_b[d], w2_t[:, wq], w2_b[wq],
                          lambda oc, lo=lo, n=n: x_t[:, oc, lo:lo + n], xb[i], mod_t, mod_b, 5)
    if not final:
        for i, (lo, n) in enumerate(TBS):
            kb.dma("sp", xo[:, :, lo:lo + n], x_t[:, :, lo:lo + n], reads=[xb[i]], is_out=True)
        return kb.end()
    kb.op("dve", lambda: nc.vector.tensor_scalar(out=gf_t[:], in0=gf_t[:], scalar1=32.0, scalar2=None, op0=ALU.mult), reads=[gf_b], writes=[gf_b])
    for i, (lo, n) in enumerate(TBS):
        sb_ = sqb[i % 2]
        for kc in range(8):
            kb.op("act", lambda kc=kc: nc.scalar.activation(out=sq_t[:, i % 2, kc, :n], in_=x_t[:, kc, lo:lo + n], func=AF.Square), reads=[xb[i]], writes=[sb_])
        ps = ring.next()
        for kc in range(8):
            kb.op("pe", lambda kc=kc, ps=ps: nc.tensor.matmul(ps.t[:, :n], ones_t[:, :], sq_t[:, i % 2, kc, :n], start=(kc == 0), stop=(kc == 7)),
                  reads=[sb_, ones_b], writes=[ps])
        rb = rsb[i % 2]
        kb.op("act", lambda ps=ps: nc.scalar.activation(out=rs_t[:, i % 2, :n], in_=ps.t[:, :n], func=AF.Sqrt, bias=eps_t[:, 0:1], scale=1.0),
              reads=[ps, eps_b], writes=[rb])
        kb.op("dve", lambda: nc.vector.reciprocal(out=rs_t[:, i % 2, :n], in_=rs_t[:, i % 2, :n]), reads=[rb], writes=[rb])
        for kc in range(8):
            kb.op("dve", lambda kc=kc: nc.vector.scalar_tensor_tensor(out=x_t[:, kc, lo:lo + n], in0=x_t[:, kc, lo:lo + n], scalar=gf_t[:, kc:kc + 1],
                                                                      in1=rs_t[:, i % 2, :n], op0=ALU.mult, op1=ALU.mult),
                  reads=[xb[i], gf_b, rb], writes=[xb[i]])
        kb.dma("sp", xo[:, :, lo:lo + n], x_t[:, :, lo:lo + n], reads=[xb[i]], is_out=True)
    return kb.end()


def slab(lat, ctx, core, nch):
    b, q = core // 4, core % 4
    a = np.concatenate([lat[b][:, q * TOK:(q + 1) * TOK], ctx[b][:, q * CT:(q + 1) * CT]], axis=1)
    return np.ascontiguousarray(a.reshape(nch, 128, NT).transpose(1, 0, 2))


def run_L3(inp, mod, xTs, r1, H, four_l, four_c):
    o_l, o_c = gather_fm(r1, "o_o", 6, np.float32)
    w = np.ascontiguousarray(inp["w_out_even"][0]); gh = fm_vec(inp["g_mlstm_head"][0], 6)
    hf_l = [H[b][0][:, CTX:] for b in range(NB)]; hf_c = [H[b][0][:, :CTX] for b in range(NB)]
    hb_l = [H[b][1][:, CTX:] for b in range(NB)]; hb_c = [H[b][1][:, :CTX] for b in range(NB)]
    in_maps = []
    for c in range(NCORES):
        in_maps.append({"xT": xTs[c], "hf": slab(hf_l, hf_c, c, 6), "hb": slab(hb_l, hb_c, c, 6), "o": slab(o_l, o_c, c, 6),
                        "yf": slab(four_l, four_c, c, 2), "gh": gh, "mod": mod_for_core(mod, 0, c), "w": w})
    res = run(build_L3(), in_maps)
    return [res[c]["xo"] for c in range(NCORES)]


def run_FFN(inp, mod, xTs, l, final):
    gn = fm_vec(inp["g_norm"][l, 1], 8); gf = fm_vec(inp["g_final"], 8)
    w1 = np.ascontiguousarray(inp["w_ff1"][l]); w2 = np.ascontiguousarray(inp["w_ff2"][l])
    in_maps = [{"xT": xTs[c], "mod": mod_for_core(mod, l, c), "gn": gn, "w1": w1, "w2": w2, "gf": gf} for c in range(NCORES)]
    res = run(build_FFN(final), in_maps)
    return [res[c]["xo"] for c in range(NCORES)]


def unslab(xTs):
    x = np.zeros((NB, SEQ, D), np.float32); xc = np.zeros((NB, CTX, D), np.float32)
    for core in range(NCORES):
        b, q = core // 4, core % 4
        t = xTs[core].transpose(2, 1, 0).reshape(NT, D)
        x[b, q * TOK:(q + 1) * TOK] = t[:TOK]; xc[b, q * CT:(q + 1) * CT] = t[TOK:]
    return x, xc


def build_L5():
    kb = KB(); nc = kb.nc
    WC = 4352
    xT = kb.dram("xT", [128, 8, NT], F32, "ExternalInput")
    mod = kb.dram("mod", [128, 48, 2], F32, "ExternalInput")
    gn = kb.dram("gn", [128, 8], F32, "ExternalInput")
    w = kb.dram("w", [1024, WC], F32, "ExternalInput")
    cosd = kb.dram("cosT", [128, TOK], F32, "ExternalInput"); sind = kb.dram("sinT", [128, TOK], F32, "ExternalInput")
    o_qkv = kb.dram("o_qkv", [128, 18, NT], BF16, "ExternalOutput")
    o_glu = kb.dram("o_glu", [128, 4, NT], F32, "ExternalOutput")
    x_t = kb.sb("x_t", [128, 8, NT]); xb = [Buf(x_t) for _ in TBS]
    h_t = kb.sb("h_t", [128, 8, NT], BF16); hb = [Buf(h_t) for _ in TBS]
    sq_t = kb.sb("sq_t", [128, 2, 8, 512], BF16); sqb = [Buf(sq_t), Buf(sq_t)]
    tmp_t = kb.sb("tmp_t", [128, 2, 512]); tmpb = [Buf(tmp_t), Buf(tmp_t)]
    rs_t = kb.sb("rs_t", [128, 2, 512]); rsb = [Buf(rs_t), Buf(rs_t)]
    mod_t = kb.sb("mod_t", [128, 48, 2]); mod_b = Buf(mod_t)
    gn_t = kb.sb("gn_t", [128, 8]); gn_b = Buf(gn_t)
    A_t = kb.sb("A_t", [128, 8, 2]); A_b = Buf(A_t)
    ones_t = kb.sb("ones_t", [128, 128], BF16); ones_b = Buf(ones_t)
    eps_t = kb.sb("eps_t", [128, 1]); eps_b = Buf(eps_t)
    cos_t = kb.sb("cos_t", [128, TOK]); cos_b = Buf(cos_t)
    sin_t = kb.sb("sin_t", [128, TOK]); sin_b = Buf(sin_t)
    NWB = 3
    wg_t = kb.sb("wg_t", [128, NWB, 8, 512], BF16); wgb = [Buf(wg_t) for _ in range(NWB)]
    NST = 4
    stb_t = kb.sb("stb_t", [128, NST, 512], BF16); stbb = [Buf(stb_t) for _ in range(NST)]
    stf_t = kb.sb("stf_t", [128, NST, 512]); stfb = [Buf(stf_t) for _ in range(NST)]
    ta_t = kb.sb("ta_t", [128, 2, 512]); ta_b = [Buf(ta_t), Buf(ta_t)]
    tb_t = kb.sb("tb_t", [128, 2, 512]); tb_b = [Buf(tb_t), Buf(tb_t)]
    ring = PsumRing(kb)
    kb.begin()
    kb.dma("sp", mod_t[:], mod, writes=[mod_b]); kb.dma("sp", gn_t[:], gn, writes=[gn_b])
    kb.dma("sp", cos_t[:], cosd, writes=[cos_b]); kb.dma("sp", sin_t[:], sind, writes=[sin_b])
    for i, (lo, n) in enumerate(TBS):
        kb.dma("sp", x_t[:, :, lo:lo + n], xT[:, :, lo:lo + n], writes=[xb[i]])
    kb.op("pool", lambda: nc.gpsimd.memset(ones_t[:], 1.0), writes=[ones_b])
    kb.op("pool", lambda: nc.gpsimd.memset(eps_t[:], 1024.0 * EPS), writes=[eps_b])
    groups = [(c0, 512) for c0 in range(0, 3584, 512)] + [(3584, 256), (3840, 512)]

    def load_group(gi):
        c0, gw = groups[gi]
        kb.dma("pool", wg_t[:, gi % NWB, :, :gw], w[:, c0:c0 + gw].rearrange("(kc p) n -> p kc n", p=128), writes=[wgb[gi % NWB]])
    load_group(0); load_group(1)
    emit_modprep(kb, mod_t, mod_b, gn_t, gn_b, A_t, A_b, vscale=1)
    for i in range(len(TBS)):
        emit_norm_block(kb, ring, i, x_t, xb[i], h_t, hb[i], sq_t, sqb, tmp_t, tmpb, rs_t, rsb, ones_t, ones_b,
                        A_t, A_b, mod_t, mod_b, 0, eps_t, eps_b)
    k = 0

    def mm(gi, cl, i, lo, n):
        ps = ring.next()
        for kc in range(8):
            kb.op("pe", lambda kc=kc: nc.tensor.matmul(ps.t[:, :n], wg_t[:, gi % NWB, kc, cl:cl + 128], h_t[:, kc, lo:lo + n],
                                                     start=(kc == 0), stop=(kc == 7)), reads=[wgb[gi % NWB], hb[i]], writes=[ps])
        return ps
    for gi, (c0, gw) in enumerate(groups):
        if gi + 2 < len(groups):
            load_group(gi + 2)
        if gi < 6:
            items = [("rope", 0, 2 * gi), ("rope", 128, 2 * gi + 1)]
        elif gi == 6:
            items = [("v", cl, 12 + cl // 128) for cl in range(0, 512, 128)]
        elif gi == 7:
            items = [("v", 0, 16), ("v", 128, 17)]
        else:
            items = [("glu", cl, cl // 128) for cl in range(0, 512, 128)]
        for (kind, cl, idx) in items:
            for i, (lo, n) in enumerate(TBS):
                k += 1
                ps = mm(gi, cl, i, lo, n)
                if kind == "glu":
                    sb_ = stfb[k % NST]
                    evac(kb, k, stf_t[:, k % NST, :n], ps.t[:, :n], [ps], [sb_])
                    kb.dma("sp", o_glu[:, idx, lo:lo + n], stf_t[:, k % NST, :n], reads=[sb_], is_out=True)
                    continue
                sb_ = stbb[k % NST]
                if kind == "v" or i == 4:
                    evac(kb, k, stb_t[:, k % NST, :n], ps.t[:, :n], [ps], [sb_])
                else:
                    ps2 = mm(gi, 256 + cl, i, lo, n)
                    e = k % 2
                    kb.op("dve", lambda ps=ps, e=e: nc.vector.tensor_tensor(out=ta_t[:, e, :n], in0=ps.t[:, :n], in1=cos_t[:, lo:lo + n], op=ALU.mult),
                          reads=[ps, cos_b], writes=[ta_b[e]])
                    kb.op("dve", lambda ps2=ps2, e=e: nc.vector.tensor_tensor(out=tb_t[:, e, :n], in0=ps2.t[:, :n], in1=sin_t[:, lo:lo + n], op=ALU.mult),
                          reads=[ps2, sin_b], writes=[tb_b[e]])
                    kb.op("pool", lambda e=e, k=k: nc.gpsimd.tensor_tensor(out=stb_t[:, k % NST, :n], in0=ta_t[:, e, :n], in1=tb_t[:, e, :n], op=ALU.add),
                          reads=[ta_b[e], tb_b[e]], writes=[sb_])
                kb.dma("sp", o_qkv[:, idx, lo:lo + n], stb_t[:, k % NST, :n], reads=[sb_], is_out=True)
    return kb.end()


def rope_tables(q):
    t = q * TOK + np.arange(TOK)
    row = (t // 64).astype(np.float64); colp = (t % 64).astype(np.float64)
    inv = 10000.0 ** (-np.arange(16) / 16.0)
    ang = np.concatenate([row[:, None] * inv[None, :], colp[:, None] * inv[None, :]], axis=1)
    cosT = np.zeros((128, TOK), np.float32); sinT = np.zeros((128, TOK), np.float32)
    for r in range(128):
        d = r % 64; half = d // 32; fi = d % 32
        cosT[r] = np.cos(ang[:, fi])
        sinT[r] = (-1.0 if half == 0 else 1.0) * np.sin(ang[:, fi])
    return cosT, sinT


def run_L5(inp, mod, xTs):
    W = inp["w_in_odd"][0]
    perm = np.arange(1536)
    r = perm % 128
    partner = np.where((r % 64) < 32, perm + 32, perm - 32)
    Wsw = W[:, :1536][:, partner]
    cols = []
    for g in range(6):
        cols += [W[:, 256 * g:256 * g + 256], Wsw[:, 256 * g:256 * g + 256]]
    cols += [W[:, 1536:2304], W[:, 2304:2816]]
    wcat = np.ascontiguousarray(np.concatenate(cols, axis=1))
    gn = fm_vec(inp["g_norm"][1, 0], 8)
    tabs = [rope_tables(q) for q in range(4)]
    in_maps = [{"xT": xTs[c], "mod": mod_for_core(mod, 1, c), "gn": gn, "w": wcat, "cosT": tabs[c % 4][0], "sinT": tabs[c % 4][1]}
               for c in range(NCORES)]
    return run(build_L5(), in_maps)


LAM_INIT = 0.8 - 0.6 * float(np.exp(-0.3))
HALO = 15; TH = TOK + 2 * HALO


def build_L6(nh=6, nqb=4, dbg=False):
    kb = KB(); nc = kb.nc
    xT = kb.dram("xT", [128, 8, TOK], F32, "ExternalInput")
    qd = kb.dram("qT", [128, 6, TOK], BF16, "ExternalInput")
    kd = kb.dram("kT", [128, 6, SL], BF16, "ExternalInput")
    vd = kb.dram("v", [128, NCH, 6, 128], BF16, "ExternalInput")
    agd = kb.dram("ag", [128, 2, 2, TH], F32, "ExternalInput")
    wdwd = kb.dram("wdw", [128, 2, 31], F32, "ExternalInput")
    lnd = kb.dram("ln", [128, 2, 2], F32, "ExternalInput")
    lampd = kb.dram("lamp", [128, 4, 64], F32, "ExternalInput")
    gsubd = kb.dram("gsub", [128, 1], F32, "ExternalInput")
    modd = kb.dram("mod", [128, 48, 2], F32, "ExternalInput")
    wd = kb.dram("w", [1024, 1024], F32, "ExternalInput")
    xo = kb.dram("xo", [128, 8, TOK], F32, "ExternalOutput")
    mdbg = kb.dram("mdbg", [128, 8, TOK], BF16, "ExternalOutput") if dbg else None

    def sbb(name, shape, dt=F32, nb=1):
        t = kb.sb(name, shape, dt)
        return (t, Buf(t)) if nb == 1 else (t, [Buf(t) for _ in range(nb)])
    kt_t, kt_b = sbb("kt", [128, 2, SL], BF16, nb=2); v_t, v_b = sbb("vv", [128, 2, NCH, 128], BF16, nb=2)
    q_t, q_b = sbb("q", [128, 6, TOK], BF16)
    m_t = kb.sb("m", [128, 8, TOK], BF16); m_b = [Buf(m_t) for _ in range(4)]
    p_t, p_b = sbb("p", [128, 3, 1024], BF16, nb=3)
    za_t = kb.sb("za", [128, 2, 2, 1024]); za_b = [[Buf(za_t), Buf(za_t)], [Buf(za_t), Buf(za_t)]]
    rzz_t, rzz_b = sbb("rzz", [128, 2, 512], nb=2)
    x_t, x_b = sbb("x", [128, 1, 8, 512], nb=1)
    x_b = [x_b, x_b]
    w_t, w_b = sbb("w", [128, 8, 1024], BF16)
    wdw_t, wdw_b = sbb("wdw", [128, 2, 31]); ln_t, ln_b = sbb("ln", [128, 2, 2])
    lamp_t, lamp_b = sbb("lamp", [128, 4, 64]); lpr_t, lpr_b = sbb("lpr", [128, 2, 64]); ls_t, ls_b = sbb("ls", [128, 2])
    nl_t, nl_b = sbb("nl", [128, 1]); gs_t, gs_b = sbb("gs", [128, 1]); mod_t, mod_b = sbb("mod", [128, 48, 2])
    ones_t, ones_b = sbb("ones", [128, 128], BF16); onef_t, onef_b = sbb("onef", [128, 128])
    eps_t, eps_b = sbb("eps", [128, 1]); epsl_t, epsl_b = sbb("epsl", [128, 1])
    rz_t, rz_b = sbb("rz", [128, 2, 512], nb=2); od_t, od_b = sbb("od", [128, 2, 512], nb=2)
    sq_t, sq_b = sbb("sq", [128, 512], BF16); rs_t, rs_b = sbb("rs", [128, 512])
    xc_t, xc_b = sbb("xc", [128, 2, 512], nb=2); sqf_t, sqf_b = sbb("sqf", [128, 2, 512], nb=2)
    spt = [kb.psum("sp%d" % i, [128, 1024], F32) for i in range(2)]
    pst = [None] * 4 + [kb.psum("pb%d" % i, [128, 512], F32) for i in range(4, 8)]
    kb.begin()
    AG = kt_t[:, :, :].rearrange("p a s -> p (a s)").bitcast(F32)
    a_v = AG[:, 0:2 * TH].rearrange("p (c t) -> p c t", c=2); g_v = AG[:, 2 * TH:4 * TH].rearrange("p (c t) -> p c t", c=2)
    ag_b = Buf(AG)
    ACC = v_t[:, :, :, :].rearrange("p a c d -> p (a c d)").bitcast(F32)
    accD = ACC[:, 0:2 * TOK].rearrange("p (c t) -> p c t", c=2); accP = ACC[:, 2 * TOK:4 * TOK].rearrange("p (c t) -> p c t", c=2)
    accD_b = Buf(ACC); accP_b = Buf(ACC)
    bankb = [None] * 4 + [Buf(pst[i]) for i in range(4, 8)]
    SP_b = [Buf(spt[0]), Buf(spt[1])]
    sring = [bankb[4], bankb[5], bankb[6], bankb[7]]
    sri = [0]
    only7 = [False]

    def rnext():
        if only7[0]:
            return bankb[7]
        b = sring[sri[0] % 4]; sri[0] += 1
        return b
    class R:
        def next(self):
            return rnext()
    ring = R()
    for (t, b, d) in ((wdw_t, wdw_b, wdwd), (ln_t, ln_b, lnd), (lamp_t, lamp_b, lampd), (gs_t, gs_b, gsubd), (mod_t, mod_b, modd), (q_t, q_b, qd)):
        kb.dma("sp", t[:], d, writes=[b])
    kb.dma("sp", a_v, agd[:, 0], writes=[ag_b]); kb.dma("sp", g_v, agd[:, 1], writes=[ag_b])
    kb.dma("pool", w_t[:], wd.rearrange("(kc p) n -> p kc n", p=128), writes=[w_b])
    kb.op("pool", lambda: nc.gpsimd.memset(ones_t[:], 1.0), writes=[ones_b])
    kb.op("pool", lambda: nc.gpsimd.memset(onef_t[:], 1.0), writes=[onef_b])
    kb.op("pool", lambda: nc.gpsimd.memset(eps_t[:], 128.0 * EPS), writes=[eps_b])
    kb.op("pool", lambda: nc.gpsimd.memset(epsl_t[:], EPS), writes=[epsl_b])
    kb.op("dve", lambda: nc.vector.tensor_tensor(out=lpr_t[:], in0=lamp_t[:, 0::2, :], in1=lamp_t[:, 1::2, :], op=ALU.mult), reads=[lamp_b], writes=[lpr_b])
    kb.op("dve", lambda: nc.vector.tensor_reduce(out=ls_t[:], in_=lpr_t[:], axis=AX.X, op=ALU.add), reads=[lpr_b], writes=[ls_b])
    kb.op("act", lambda: nc.scalar.activation(out=ls_t[:], in_=ls_t[:], func=AF.Exp), reads=[ls_b], writes=[ls_b])
    kb.op("dve", lambda: nc.vector.tensor_tensor(out=nl_t[:], in0=ls_t[:, 1:2], in1=ls_t[:, 0:1], op=ALU.subtract), reads=[ls_b], writes=[nl_b])
    kb.op("dve", lambda: nc.vector.tensor_scalar(out=nl_t[:], in0=nl_t[:], scalar1=-LAM_INIT, scalar2=None, op0=ALU.add), reads=[nl_b], writes=[nl_b])
    kb.op("dve", lambda: nc.vector.tensor_scalar(out=gs_t[:], in0=gs_t[:], scalar1=float((1.0 - LAM_INIT) * np.sqrt(128.0)), scalar2=None, op0=ALU.mult),
          reads=[gs_b], writes=[gs_b])
    kb.op("act", lambda: nc.scalar.activation(out=g_v, in_=g_v, func=AF.Sigmoid), reads=[ag_b], writes=[ag_b])
    kb.op("dve", lambda: nc.vector.tensor_tensor(out=a_v, in0=a_v, in1=g_v, op=ALU.mult), reads=[ag_b], writes=[ag_b])
    for j in range(2):
        for k in range(31):
            if k == 0:
                kb.op("dve", lambda k=k: nc.vector.tensor_scalar(out=accD[:, j, :], in0=a_v[:, j, k:k + TOK], scalar1=wdw_t[:, j, k:k + 1], scalar2=None, op0=ALU.mult),
                      reads=[ag_b, wdw_b], writes=[accD_b])
            else:
                kb.op("dve", lambda k=k: nc.vector.scalar_tensor_tensor(out=accD[:, j, :], in0=a_v[:, j, k:k + TOK], scalar=wdw_t[:, j, k:k + 1], in1=accD[:, j, :],
                                                                      op0=ALU.mult, op1=ALU.add), reads=[ag_b, wdw_b, accD_b], writes=[accD_b])
    for tb in range(4):
        cs = slice(tb * 512, (tb + 1) * 512); e = tb % 2
        ps = rnext()
        for j in range(2):
            kb.op("pe", lambda j=j, ps=ps: nc.tensor.matmul(ps.t[:, :], onef_t[:, :], accD[:, j, cs], start=(j == 0), stop=(j == 1)), reads=[onef_b, accD_b], writes=[ps])
        for j in range(2):
            kb.op("dve", lambda j=j, ps=ps: nc.vector.scalar_tensor_tensor(out=xc_t[:, e, :] if j == 0 else sqf_t[:, e, :], in0=ps.t[:, :], scalar=-1.0 / 256.0,
                                                                           in1=accD[:, j, cs], op0=ALU.mult, op1=ALU.add),
                  reads=[ps, accD_b], writes=[xc_b[e] if j == 0 else sqf_b[e]])
        kb.op("act", lambda: nc.scalar.activation(out=od_t[:, e, :], in_=xc_t[:, e, :], func=AF.Square), reads=[xc_b[e]], writes=[od_b[e]])
        kb.op("act", lambda: nc.scalar.activation(out=rz_t[:, e, :], in_=sqf_t[:, e, :], func=AF.Square), reads=[sqf_b[e]], writes=[rz_b[e]])
        ps2 = rnext()
        kb.op("pe", lambda ps2=ps2: nc.tensor.matmul(ps2.t[:, :], onef_t[:, :], od_t[:, e, :], start=True, stop=False), reads=[onef_b, od_b[e]], writes=[ps2])
        kb.op("pe", lambda ps2=ps2: nc.tensor.matmul(ps2.t[:, :], onef_t[:, :], rz_t[:, e, :], start=False, stop=True), reads=[onef_b, rz_b[e]], writes=[ps2])
        kb.op("act", lambda ps2=ps2: nc.scalar.activation(out=rs_t[:, :], in_=ps2.t[:, :], func=AF.Sqrt, bias=epsl_t[:, 0:1], scale=1.0 / 256.0),
              reads=[ps2, epsl_b], writes=[rs_b])
        kb.op("dve", lambda: nc.vector.reciprocal(out=rs_t[:, :], in_=rs_t[:, :]), reads=[rs_b], writes=[rs_b])
        for j, (src_t, src_b) in enumerate(((xc_t, xc_b), (sqf_t, sqf_b))):
            kb.op("dve", lambda j=j, src_t=src_t: nc.vector.scalar_tensor_tensor(out=src_t[:, e, :], in0=src_t[:, e, :], scalar=ln_t[:, 0, j:j + 1], in1=rs_t[:, :],
                                                                                 op0=ALU.mult, op1=ALU.mult), reads=[src_b[e], ln_b, rs_b], writes=[src_b[e]])
            kb.op("act", lambda j=j, src_t=src_t: nc.scalar.activation(out=m_t[:, 6 + j, cs], in_=src_t[:, e, :], func=AF.Silu, bias=ln_t[:, 1, j:j + 1], scale=1.0),
                  reads=[src_b[e], ln_b], writes=[m_b[tb]])
    kb.barrier()
    kt_b = [Buf(kt_t), Buf(kt_t)]; v_b = [Buf(v_t), Buf(v_t)]
    O_b = [bankb[4], bankb[5]]
    only7[0] = True
    def load_head(hd):
        d = hd % 2
        kb.dma("sp", kt_t[:, d, :], kd[:, hd, :], writes=[kt_b[d]])
        kb.dma("sp", v_t[:, d], vd[:, :, hd, :], writes=[v_b[d]])
    NKP = NCH // 2
    its = [(hd, qb, mp, kp) for hd in range(nh) for qb in range(nqb) for mp in range(2) for kp in range(NKP)]
    NP = 3

    def emit_S(i):
        hd, qb, mp, kp = its[i]
        d = hd % 2; pr = slice(64 * mp, 64 * mp + 64); qs = slice(qb * 512, (qb + 1) * 512)
        sb_ = SP_b[i % 2]; sp_ = spt[i % 2]; pb_ = p_b[i % NP]
        if qb == 0 and mp == 0 and kp == 0 and hd + 1 < nh:
            load_head(hd + 1)
        for h in range(2):
            kt = 2 * kp + h
            kb.op("pe", lambda h=h, kt=kt: nc.tensor.matmul(sp_[:, h * 512:(h + 1) * 512], kt_t[pr, d, kt * 128:(kt + 1) * 128], q_t[pr, hd, qs], start=True, stop=True),
                  reads=[kt_b[d], q_b], writes=[sb_])
        kb.op("act", lambda: nc.scalar.activation(out=p_t[:, i % NP, :], in_=sp_[:, :], func=AF.Exp, scale=0.125), reads=[sb_], writes=[pb_])

    def emit_PV(i):
        hd, qb, mp, kp = its[i]
        d = hd % 2; pb_ = p_b[i % NP]; qs = slice(qb * 512, (qb + 1) * 512)
        for h in range(2):
            kt = 2 * kp + h
            kb.op("pe", lambda h=h, kt=kt: nc.tensor.matmul(pst[4 + mp][:, :], v_t[:, d, kt, :], p_t[:, i % NP, h * 512:(h + 1) * 512],
                                                          start=(kt == 0), stop=(kt == NCH - 1)), reads=[v_b[d], pb_], writes=[O_b[mp]])
        par = 0
        eng, E = ("dve", nc.vector)
        zb = za_b[mp][par]
        if kp < 1:
            kb.op(eng, lambda: E.tensor_copy(out=za_t[:, mp, par, :], in_=p_t[:, i % NP, :]), reads=[pb_], writes=[zb])
        else:
            kb.op(eng, lambda: E.tensor_tensor(out=za_t[:, mp, par, :], in0=za_t[:, mp, par, :], in1=p_t[:, i % NP, :], op=ALU.add),
                  reads=[pb_, zb], writes=[zb])
        if kp == NKP - 1:
            n = 0
            for par2 in range(1):
                for h in range(2):
                    kb.op("pe", lambda par2=par2, h=h, n=n: nc.tensor.matmul(pst[6][:, :], onef_t[:, :], za_t[:, mp, par2, h * 512:(h + 1) * 512],
                                                                            start=(n == 0), stop=(n == 1)), reads=[onef_b, za_b[mp][par2]], writes=[bankb[6]])
                    n += 1
            kb.op("dve", lambda: nc.vector.reciprocal(out=rzz_t[:, mp, :], in_=pst[6][:, :]), reads=[bankb[6]], writes=[rzz_b[mp]])
        if kp == NKP - 1 and mp == 1:
            e = (hd * 4 + qb) % 2
            for m2 in range(2):
                kb.op("dve", lambda m2=m2: nc.vector.tensor_tensor(out=(od_t[:, e, :] if m2 == 0 else xc_t[:, e, :]), in0=pst[4 + m2][:, :], in1=rzz_t[:, m2, :], op=ALU.mult),
                      reads=[O_b[m2], rzz_b[m2]], writes=[od_b[e] if m2 == 0 else xc_b[e]])
            kb.op("dve", lambda: nc.vector.scalar_tensor_tensor(out=od_t[:, e, :], in0=xc_t[:, e, :], scalar=nl_t[:, 0:1], in1=od_t[:, e, :], op0=ALU.mult, op1=ALU.add),
                  reads=[xc_b[e], nl_b, od_b[e]], writes=[od_b[e]])
            emit_headnorm(kb, ring, od_t[:, e, :], od_b[e], 512, sq_t[:, :], sq_b, rs_t[:, :], rs_b, ones_t, ones_b, eps_t, eps_b)
            kb.op("dve", lambda: nc.vector.scalar_tensor_tensor(out=m_t[:, hd, qs], in0=od_t[:, e, :], scalar=gs_t[:, 0:1], in1=rs_t[:, :], op0=ALU.mult, op1=ALU.mult),
                  reads=[od_b[e], gs_b, rs_b], writes=[m_b[qb]])
    load_head(0)
    LA = 1
    for i in range(min(LA, len(its))):
        emit_S(i)
    for i in range(len(its)):
        if i + LA < len(its):
            emit_S(i + LA)
        emit_PV(i)
    only7[0] = False
    if dbg:
        kb.dma("sp", mdbg, m_t[:], reads=m_b, is_out=True)
    for tb in range(4):
        d = 0; lo = tb * 512
        kb.dma("sp", x_t[:, d], xT[:, :, lo:lo + 512], writes=[x_b[d]])
        emit_wout_res(kb, ring, tb, 512, 0, m_t[:, :, lo:lo + 512], m_b[tb], w_t, w_b, lambda oc, d=d: x_t[:, d, oc, :], x_b[d], mod_t, mod_b, 2)
        kb.dma("sp", xo[:, :, lo:lo + 512], x_t[:, d], reads=[x_b[d]], is_out=True)
    return kb.end()


def run_L6(inp, mod, x4, r5):
    qkv_l, qkv_c = gather_fm(r5, "o_qkv", 18, NPBF)
    glu_l, _ = gather_fm(r5, "o_glu", 4, np.float32)
    w = np.ascontiguousarray(inp["w_out_odd"][0])
    wdw = np.ascontiguousarray(inp["w_dw"][0].T.reshape(2, 128, 31).transpose(1, 0, 2))
    ln = np.ascontiguousarray(np.stack([fm_vec(inp["g_conv_ln"][0], 2), fm_vec(inp["b_conv_ln"][0], 2)], axis=1))
    lamp = np.ascontiguousarray(np.broadcast_to(inp["lam_p"][0][None], (128, 4, 64)))
    gsub = np.ascontiguousarray(inp["g_subln"][0].reshape(128, 1))
    in_maps = []
    for c in range(NCORES):
        b, q = c // 4, c % 4
        kall = np.concatenate([qkv_c[b][768:1536], qkv_l[b][768:1536]], axis=1)
        vall = np.concatenate([qkv_c[b][1536:2304], qkv_l[b][1536:2304]], axis=1)
        kT = np.ascontiguousarray(kall.reshape(6, 128, SL).transpose(1, 0, 2))
        v = np.ascontiguousarray(vall.reshape(6, 128, NCH, 128).transpose(3, 2, 0, 1))
        qT = np.ascontiguousarray(qkv_l[b][0:768, q * TOK:(q + 1) * TOK].reshape(6, 128, TOK).transpose(1, 0, 2))
        gp = np.zeros((512, SEQ + 2 * HALO), np.float32); gp[:, HALO:HALO + SEQ] = glu_l[b]
        seg = gp[:, q * TOK:q * TOK + TH]
        ag = np.ascontiguousarray(seg.reshape(2, 2, 128, TH).transpose(2, 0, 1, 3))
        in_maps.append({"xT": np.ascontiguousarray(x4[c][:, :, :TOK]), "qT": qT, "kT": kT, "v": v, "ag": ag, "wdw": wdw, "ln": ln,
                        "lamp": lamp, "gsub": gsub, "mod": mod_for_core(mod, 1, c), "w": w})
    import os
    dbg = os.environ.get('L6_DBG') is not None
    res = run(build_L6(nh=int(os.environ.get('L6_NH', '6')), nqb=int(os.environ.get('L6_NQB', '4')), dbg=dbg), in_maps)
    if dbg:
        return res
    out = []
    for c in range(NCORES):
        t = np.zeros((128, 8, NT), np.float32); t[:, :, :TOK] = res[c]["xo"]
        out.append(t)
    return out


def kernel(**inp):
    inp = {k: np.asarray(v) for k, v in inp.items()}
    mod = run_L0(inp)
    xTs = [xT_for_core(inp["x"], inp["ctx"], c) for c in range(NCORES)]
    r1 = run_L1(inp, mod, xTs)
    H, four_l, four_c = run_L2(inp, r1)
    x3 = run_L3(inp, mod, xTs, r1, H, four_l, four_c)
    x4 = run_FFN(inp, mod, x3, 0, False)
    r5 = run_L5(inp, mod, x4)
    x6 = run_L6(inp, mod, x4, r5)
    x7 = run_FFN(inp, mod, x6, 1, True)
    out, _ = unslab(x7)
    return out.astype(np.float32)
```

```python
import numpy as np, ml_dtypes
from contextlib import ExitStack
import concourse.bass as bass
import concourse.mybir as mybir
from concourse.bass_utils import run_bass_kernel_spmd

F32 = mybir.dt.float32; BF16 = mybir.dt.bfloat16
AF = mybir.ActivationFunctionType; ALU = mybir.AluOpType; AX = mybir.AxisListType
NPBF = ml_dtypes.bfloat16
NCORES = 8


class Buf:
    __slots__ = ("t", "w", "r")

    def __init__(self, t):
        self.t = t; self.w = None; self.r = {}

    def __getitem__(self, k):
        return self.t[k]


class KB:
    NSLOT = 40

    def __init__(self):
        self.nc = nc = bass.Bass("TRN2", target_bir_lowering=False)
        self.es = ExitStack()
        self.eng = dict(pe=nc.tensor, act=nc.scalar, dve=nc.vector, pool=nc.gpsimd, sp=nc.sync)
        self.sem = {e: self.es.enter_context(nc.semaphore("sem_" + e)) for e in self.eng}
        self.cnt = {e: 0 for e in self.eng}
        self.seen = {e: {} for e in self.eng}
        self.hist = {}
        self.dsem = [self.es.enter_context(nc.semaphore("dsem%d" % i)) for i in range(self.NSLOT)]
        self.dcnt = [0] * self.NSLOT
        self.dnext = 0
        self.out_tokens = []
        self.nps = 0
        self.started = False

    def dram(self, name, shape, dtype, kind):
        return self.nc.dram_tensor(name, list(shape), dtype, kind=kind).ap()

    def sb(self, name, shape, dtype=F32):
        return self.es.enter_context(self.nc.sbuf_tensor("sb_" + name, list(shape), dtype))

    def psum(self, name, shape, dtype=F32):
        return self.es.enter_context(self.nc.psum_tensor("pp_" + name, list(shape), dtype))

    def begin(self):
        self.es.enter_context(self.nc.Block())
        self.started = True

    def end(self):
        for tok in self.out_tokens:
            self._wait("sp", tok)
        self.es.close()
        return self.nc

    def _semh(self, key):
        return self.sem[key] if isinstance(key, str) else self.dsem[key]

    def _wait(self, e, tok):
        if tok is None:
            return
        key, val = tok
        s = self.seen[e]
        if s.get(key, 0) >= val:
            return
        if not (e == "pe" and key == "pe"):
            self.eng[e].wait_ge(self._semh(key), val)
        s[key] = val
        h = self.hist.get(tok)
        if h:
            for k2, v2 in h.items():
                if s.get(k2, 0) < v2:
                    s[k2] = v2

    def _deps(self, e, reads, writes):
        for b in reads:
            self._wait(e, b.w)
        for b in writes:
            self._wait(e, b.w)
            for k2, v2 in list(b.r.items()):
                self._wait(e, (k2, v2))

    def _commit(self, tok, e, reads, writes):
        self.hist[tok] = dict(self.seen[e])
        for b in reads:
            if b.r.get(tok[0], 0) < tok[1]:
                b.r[tok[0]] = tok[1]
        for b in writes:
            b.w = tok; b.r = {}

    def op(self, e, f, reads=(), writes=()):
        self._deps(e, reads, writes)
        ins = f()
        self.cnt[e] += 1
        ins.then_inc(self.sem[e], 1)
        tok = (e, self.cnt[e])
        self._commit(tok, e, reads, writes)
        return tok

    def dma(self, q, out_ap, in_ap, reads=(), writes=(), is_out=False):
        slot = self.dnext
        self.dnext = (self.dnext + 1) % self.NSLOT
        if self.dcnt[slot] > 0:
            self._wait(q, (slot, 16 * self.dcnt[slot]))
        self._deps(q, reads, writes)
        ins = self.eng[q].dma_start(out=out_ap, in_=in_ap)
        self.dcnt[slot] += 1
        ins.then_inc(self.dsem[slot], 16)
        tok = (slot, 16 * self.dcnt[slot])
        self._commit(tok, q, reads, writes)
        if is_out:
            self.out_tokens.append(tok)
        return tok

    def barrier(self):
        toks = [(e, self.cnt[e]) for e in self.eng if self.cnt[e] > 0]
        toks += [(i, 16 * self.dcnt[i]) for i in range(self.NSLOT) if self.dcnt[i] > 0]
        for e in self.eng:
            for t in toks:
                self._wait(e, t)


def run(kb_nc, in_maps):
    res = run_bass_kernel_spmd(kb_nc, in_maps, core_ids=list(range(NCORES)))
    if getattr(res, "exec_time_ns", None):
        print("  [exec_time_ns]", res.exec_time_ns, flush=True)
    return res.results


D = 1024; SEQ = 8192; CTX = 256; NB = 2
TOK = 2048; CT = 64; NT = TOK + CT
TBS = [(0, 512), (512, 512), (1024, 512), (1536, 512), (2048, 64)]
EPS = 1e-6
EVEN_IN = 3352; ODD_IN = 2816


def tb_col(i):
    return 1 if i == 4 else 0


def build_L0():
    kb = KB(); nc = kb.nc
    sT = kb.dram("sT", [128, 8, 3], F32, "ExternalInput")
    w = kb.dram("w", [2, 1024, 768], F32, "ExternalInput")
    bm = kb.dram("bm", [128, 2, 6], F32, "ExternalInput")
    out = kb.dram("out", [128, 2, 6, 3], F32, "ExternalOutput")
    s_t = kb.sb("s_t", [128, 8, 3]); s_b = Buf(s_t)
    ss_t = kb.sb("ss_t", [128, 8, 3]); ss_b = Buf(ss_t)
    w_t = kb.sb("w_t", [128, 2, 8, 768]); w_b = [Buf(w_t), Buf(w_t)]
    bm_t = kb.sb("bm_t", [128, 2, 6]); bm_b = Buf(bm_t)
    o_t = kb.sb("o_t", [128, 2, 6, 3]); o_b = Buf(o_t)
    ps_t = kb.psum("ps", [128, 2, 6, 4]); ps_b = Buf(ps_t)
    kb.begin()
    kb.dma("sp", s_t[:], sT, writes=[s_b])
    kb.dma("sp", bm_t[:], bm, writes=[bm_b])
    for l in range(2):
        kb.dma("sp", w_t[:, l], w[l].rearrange("(kc p) n -> p kc n", p=128), writes=[w_b[l]])
    kb.op("act", lambda: nc.scalar.activation(out=ss_t[:], in_=s_t[:], func=AF.Silu), reads=[s_b], writes=[ss_b])
    for l in range(2):
        for j in range(6):
            for kc in range(8):
                kb.op("pe", lambda l=l, j=j, kc=kc: nc.tensor.matmul(
                    ps_t[:, l, j, 0:3], w_t[:, l, kc, j * 128:(j + 1) * 128], ss_t[:, kc, :],
                    start=(kc == 0), stop=(kc == 7)), reads=[w_b[l], ss_b], writes=[ps_b])
    kb.op("dve", lambda: nc.vector.tensor_tensor(
        out=o_t[:], in0=ps_t[:, :, :, 0:3], in1=bm_t[:].unsqueeze(3).to_broadcast([128, 2, 6, 3]), op=ALU.add),
        reads=[ps_b, bm_b], writes=[o_b])
    kb.dma("sp", out, o_t[:], reads=[o_b], is_out=True)
    return kb.end()


def run_L0(inp):
    c = inp["c"]; c_ctx = inp["c_ctx"]
    s = np.stack([c[0], c[1], c_ctx], axis=1)
    sT = np.ascontiguousarray(s.reshape(8, 128, 3).transpose(1, 0, 2))
    in_maps = []
    for core in range(NCORES):
        w = np.ascontiguousarray(inp["w_mod"][:, :, core * 768:(core + 1) * 768])
        bm = inp["b_mod"][:, core * 768:(core + 1) * 768].reshape(2, 6, 128).transpose(2, 0, 1)
        in_maps.append({"sT": sT, "w": w, "bm": np.ascontiguousarray(bm)})
    res = run(build_L0(), in_maps)
    mod = np.zeros((2, 6144, 3), np.float32)
    for core in range(NCORES):
        o = res[core]["out"]
        mod[:, core * 768:(core + 1) * 768, :] = o.transpose(1, 2, 0, 3).reshape(2, 768, 3)
    return mod


def mod_for_core(mod, l, core):
    b = core // 4
    m = mod[l][:, [b, 2]]
    return np.ascontiguousarray(m.reshape(48, 128, 2).transpose(1, 0, 2))


class PsumRing:
    def __init__(self, kb, n=8, name="ps"):
        self.kb = kb
        self.t = [kb.psum("%s%d" % (name, i), [128, 512], F32) for i in range(n)]
        self.b = [Buf(t) for t in self.t]
        self.i = 0

    def next(self):
        b = self.b[self.i]
        self.i = (self.i + 1) % len(self.b)
        return b


def emit_modprep(kb, mod_t, mod_b, gn_t, gn_b, A_t, A_b, vscale):
    nc = kb.nc
    for col in range(2):
        kb.op("dve", lambda col=col: nc.vector.tensor_scalar(
            out=A_t[:, :, col], in0=mod_t[:, vscale * 8:(vscale + 1) * 8, col], scalar1=1.0, scalar2=32.0,
            op0=ALU.add, op1=ALU.mult), reads=[mod_b], writes=[A_b])
        kb.op("dve", lambda col=col: nc.vector.tensor_tensor(
            out=A_t[:, :, col], in0=A_t[:, :, col], in1=gn_t[:, :], op=ALU.mult), reads=[A_b, gn_b], writes=[A_b])


def emit_norm_block(kb, ring, i, x_t, xb, h_t, hb, sq_t, sqb, tmp_t, tmpb, rs_t, rsb, ones_t, ones_b,
                    A_t, A_b, mod_t, mod_b, vshift, eps_t, eps_b, hlo=None):
    nc = kb.nc
    lo, n = TBS[i]; col = tb_col(i)
    if hlo is None:
        hlo = lo
    sb = sqb[i % 2]; s_t = sq_t
    for kc in range(8):
        kb.op("act", lambda kc=kc: nc.scalar.activation(out=s_t[:, i % 2, kc, :n], in_=x_t[:, kc, lo:lo + n], func=AF.Square),
              reads=[xb], writes=[sb])
    ps = ring.next()
    for kc in range(8):
        kb.op("pe", lambda kc=kc: nc.tensor.matmul(ps.t[:, :n], ones_t[:, :], s_t[:, i % 2, kc, :n], start=(kc == 0), stop=(kc == 7)),
              reads=[sb, ones_b], writes=[ps])
    rb = rsb[i % 2]
    kb.op("act", lambda: nc.scalar.activation(out=rs_t[:, i % 2, :n], in_=ps.t[:, :n], func=AF.Sqrt, bias=eps_t[:, 0:1], scale=1.0),
          reads=[ps, eps_b], writes=[rb])
    kb.op("dve", lambda: nc.vector.reciprocal(out=rs_t[:, i % 2, :n], in_=rs_t[:, i % 2, :n]), reads=[rb], writes=[rb])
    for kc in range(8):
        tb_ = tmpb[kc % 2]
        kb.op("dve", lambda kc=kc: nc.vector.scalar_tensor_tensor(
            out=tmp_t[:, kc % 2, :n], in0=x_t[:, kc, lo:lo + n], scalar=A_t[:, kc, col:col + 1], in1=rs_t[:, i % 2, :n],
            op0=ALU.mult, op1=ALU.mult), reads=[xb, A_b, rb], writes=[tb_])
        kb.op("act", lambda kc=kc: nc.scalar.activation(
            out=h_t[:, kc, hlo:hlo + n], in_=tmp_t[:, kc % 2, :n], func=AF.Identity,
            bias=mod_t[:, vshift * 8 + kc, col:col + 1], scale=1.0), reads=[tb_, mod_b], writes=[hb])


def evac(kb, k, out_ap, in_ap, reads, writes):
    nc = kb.nc
    if k % 2 == 0:
        kb.op("act", lambda: nc.scalar.copy(out=out_ap, in_=in_ap), reads=reads, writes=writes)
    else:
        kb.op("dve", lambda: nc.vector.tensor_copy(out=out_ap, in_=in_ap), reads=reads, writes=writes)


def build_L1():
    kb = KB(); nc = kb.nc
    xT = kb.dram("xT", [128, 8, NT], F32, "ExternalInput")
    mod = kb.dram("mod", [128, 48, 2], F32, "ExternalInput")
    gn = kb.dram("gn", [128, 8], F32, "ExternalInput")
    w = kb.dram("w", [1024, EVEN_IN], F32, "ExternalInput")
    cbd = kb.dram("cbd", [128, 128], F32, "ExternalInput")
    sbd = kb.dram("sbd", [128, 128], F32, "ExternalInput")
    o_qkv = kb.dram("o_qkv", [128, 18, NT], BF16, "ExternalOutput")
    o_o = kb.dram("o_o", [128, 6, NT], F32, "ExternalOutput")
    o_g = kb.dram("o_g", [24, NT], F32, "ExternalOutput")
    o_fr = kb.dram("o_fr", [128, 2, NT], F32, "ExternalOutput")
    o_fi = kb.dram("o_fi", [128, 2, NT], F32, "ExternalOutput")

    x_t = kb.sb("x_t", [128, 8, NT]); xb = [Buf(x_t) for _ in TBS]
    h_t = kb.sb("h_t", [128, 8, NT], BF16); hb = [Buf(h_t) for _ in TBS]
    sq_t = kb.sb("sq_t", [128, 2, 8, 512], BF16); sqb = [Buf(sq_t), Buf(sq_t)]
    tmp_t = kb.sb("tmp_t", [128, 2, 512]); tmpb = [Buf(tmp_t), Buf(tmp_t)]
    rs_t = kb.sb("rs_t", [128, 2, 512]); rsb = [Buf(rs_t), Buf(rs_t)]
    mod_t = kb.sb("mod_t", [128, 48, 2]); mod_b = Buf(mod_t)
    gn_t = kb.sb("gn_t", [128, 8]); gn_b = Buf(gn_t)
    A_t = kb.sb("A_t", [128, 8, 2]); A_b = Buf(A_t)
    ones_t = kb.sb("ones_t", [128, 128], BF16); ones_b = Buf(ones_t)
    eps_t = kb.sb("eps_t", [128, 1]); eps_b = Buf(eps_t)
    cbd_t = kb.sb("cbd_t", [128, 128]); cbd_b = Buf(cbd_t)
    sbd_t = kb.sb("sbd_t", [128, 128]); sbd_b = Buf(sbd_t)
    NWB = 3
    wg_t = kb.sb("wg_t", [128, NWB, 8, 512], BF16); wgb = [Buf(wg_t) for _ in range(NWB)]
    NST = 4
    stb_t = kb.sb("stb_t", [128, NST, 512], BF16); stbb = [Buf(stb_t) for _ in range(NST)]
    stf_t = kb.sb("stf_t", [128, NST, 512]); stfb = [Buf(stf_t) for _ in range(NST)]
    f_t = kb.sb("f_t", [128, 2, NT]); fb = [[Buf(f_t) for _ in TBS] for _ in range(2)]
    ring = PsumRing(kb)
    kb.begin()

    kb.dma("sp", mod_t[:], mod, writes=[mod_b])
    kb.dma("sp", gn_t[:], gn, writes=[gn_b])
    kb.dma("sp", cbd_t[:], cbd, writes=[cbd_b])
    kb.dma("sp", sbd_t[:], sbd, writes=[sbd_b])
    for i, (lo, n) in enumerate(TBS):
        kb.dma("sp", x_t[:, :, lo:lo + n], xT[:, :, lo:lo + n], writes=[xb[i]])
    kb.op("pool", lambda: nc.gpsimd.memset(ones_t[:], 1.0), writes=[ones_b])
    kb.op("pool", lambda: nc.gpsimd.memset(eps_t[:], 1024.0 * EPS), writes=[eps_b])
    groups = [(c0, 512) for c0 in range(0, 3072, 512)] + [(3072, 280)]
    gtok = {}

    def load_group(gi):
        c0, gw = groups[gi]
        gtok[gi] = kb.dma("pool", wg_t[:, gi % NWB, :, :gw], w[:, c0:c0 + gw].rearrange("(kc p) n -> p kc n", p=128),
                          writes=[wgb[gi % NWB]])
    load_group(0); load_group(1)
    emit_modprep(kb, mod_t, mod_b, gn_t, gn_b, A_t, A_b, vscale=1)
    for i in range(len(TBS)):
        emit_norm_block(kb, ring, i, x_t, xb[i], h_t, hb[i], sq_t, sqb, tmp_t, tmpb, rs_t, rsb, ones_t, ones_b,
                        A_t, A_b, mod_t, mod_b, 0, eps_t, eps_b)
    k = 0
    for gi, (c0, gw) in enumerate(groups):
        if gi + 2 < len(groups):
            load_group(gi + 2)
        if c0 < 3072:
            chunks = [(cl, 128) for cl in range(0, gw, 128)]
        else:
            chunks = [(0, 24), (24, 128), (152, 128)]
        for (cl, m) in chunks:
            c = c0 + cl
            for i, (lo, n) in enumerate(TBS):
                ps = ring.next()
                for kc in range(8):
                    kb.op("pe", lambda kc=kc, cl=cl, m=m, lo=lo, n=n, ps=ps: nc.tensor.matmul(
                        ps.t[:m, :n], wg_t[:, gi % NWB, kc, cl:cl + m], h_t[:, kc, lo:lo + n], start=(kc == 0), stop=(kc == 7)),
                        reads=[wgb[gi % NWB], hb[i]], writes=[ps])
                k += 1
                if c < 2304:
                    sb_ = stbb[k % NST]
                    evac(kb, k, stb_t[:, k % NST, :n], ps.t[:, :n], [ps], [sb_])
                    kb.dma("sp", o_qkv[:, c // 128, lo:lo + n], stb_t[:, k % NST, :n], reads=[sb_], is_out=True)
                elif c < 3072:
                    sb_ = stfb[k % NST]
                    evac(kb, k, stf_t[:, k % NST, :n], ps.t[:, :n], [ps], [sb_])
                    kb.dma("sp", o_o[:, (c - 2304) // 128, lo:lo + n], stf_t[:, k % NST, :n], reads=[sb_], is_out=True)
                elif c == 3072:
                    sb_ = stfb[k % NST]
                    evac(kb, k, stf_t[:24, k % NST, :n], ps.t[:24, :n], [ps], [sb_])
                    kb.dma("sp", o_g[:, lo:lo + n], stf_t[:24, k % NST, :n], reads=[sb_], is_out=True)
                else:
                    j = (c - 3096) // 128
                    evac(kb, k, f_t[:, j, lo:lo + n], ps.t[:, :n], [ps], [fb[j][i]])
                    for (tab_t, tab_b, dst) in ((cbd_t, cbd_b, o_fr), (sbd_t, sbd_b, o_fi)):
                        ps2 = ring.next()
                        kb.op("pe", lambda tab_t=tab_t, ps2=ps2, j=j, lo=lo, n=n: nc.tensor.matmul(
                            ps2.t[:, :n], tab_t[:, :], f_t[:, j, lo:lo + n], start=True, stop=True),
                            reads=[tab_b, fb[j][i]], writes=[ps2])
                        k += 1
                        sb_ = stfb[k % NST]
                        evac(kb, k, stf_t[:, k % NST, :n], ps2.t[:, :n], [ps2], [sb_])
                        kb.dma("sp", dst[:, j, lo:lo + n], stf_t[:, k % NST, :n], reads=[sb_], is_out=True)
    return kb.end()


def xT_for_core(x, xc, core):
    b, q = core // 4, core % 4
    t = np.concatenate([x[b, q * TOK:(q + 1) * TOK], xc[b, q * CT:(q + 1) * CT]], axis=0)
    return np.ascontiguousarray(t.reshape(NT, 8, 128).transpose(2, 1, 0))


def fm_vec(v, nchunk):
    return np.ascontiguousarray(np.asarray(v).reshape(nchunk, 128).T)


def dft_bd():
    c = np.arange(64)
    ang = 2 * np.pi * np.outer(c, c) / 64
    cb = np.zeros((128, 128), np.float32); sb = np.zeros((128, 128), np.float32)
    for g in range(2):
        cb[g * 64:(g + 1) * 64, g * 64:(g + 1) * 64] = np.cos(ang)
        sb[g * 64:(g + 1) * 64, g * 64:(g + 1) * 64] = -np.sin(ang)
    return cb, sb


def run_L1(inp, mod, xTs):
    cb, sb = dft_bd()
    w = np.ascontiguousarray(inp["w_in_even"][0])
    gn = fm_vec(inp["g_norm"][0, 0], 8)
    in_maps = [{"xT": xTs[c], "mod": mod_for_core(mod, 0, c), "gn": gn, "w": w, "cbd": cb, "sbd": sb} for c in range(NCORES)]
    return run(build_L1(), in_maps)


NCH = 66; SL = NCH * 128
ALPHA = 128 ** -0.5


def build_L2(U=3, do_mlstm=True, do_fourier=True, phase=9, nscan=NCH):
    kb = KB(); nc = kb.nc
    qTd = [kb.dram("qT%d" % u, [128, SL], BF16, "ExternalInput") for u in range(U)]
    kTd = [kb.dram("kT%d" % u, [128, SL], BF16, "ExternalInput") for u in range(U)]
    vd = [kb.dram("v%d" % u, [128, NCH, 128], BF16, "ExternalInput") for u in range(U)]
    gid = [kb.dram("gi%d" % u, [128, NCH], F32, "ExternalInput") for u in range(U)]
    gfd = [kb.dram("gf%d" % u, [128, NCH], F32, "ExternalInput") for u in range(U)]
    miscd = [kb.dram("misc%d" % u, [128, 8], F32, "ExternalInput") for u in range(U)]
    trid = kb.dram("tri", [128, 128], F32, "ExternalInput")
    maskd = kb.dram("mask", [128, 128], F32, "ExternalInput")
    identfd = kb.dram("identf", [128, 128], F32, "ExternalInput")
    identbd = kb.dram("identb", [128, 128], BF16, "ExternalInput")
    hTd = [kb.dram("hT%d" % u, [128, SL], F32, "ExternalOutput") for u in range(U)]
    XRd = kb.dram("XR", [128, 4096], F32, "ExternalInput"); XId = kb.dram("XI", [128, 4096], F32, "ExternalInput")
    c128d = kb.dram("c128", [128, 128], F32, "ExternalInput"); s128d = kb.dram("s128", [128, 128], F32, "ExternalInput")
    ns128d = kb.dram("ns128", [128, 128], F32, "ExternalInput")
    tcd = kb.dram("tc", [128, 64], F32, "ExternalInput"); tsd = kb.dram("ts", [128, 64], F32, "ExternalInput")
    c64d = kb.dram("c64", [64, 64], F32, "ExternalInput"); s64d = kb.dram("s64", [64, 64], F32, "ExternalInput")
    XcRd = kb.dram("XcR", [128, 2, 64], F32, "ExternalInput"); XcId = kb.dram("XcI", [128, 2, 64], F32, "ExternalInput")
    c256d = kb.dram("c256", [128, 2, 256], F32, "ExternalInput"); s256d = kb.dram("s256", [128, 2, 256], F32, "ExternalInput")
    Yd = kb.dram("Y", [64, 8192], F32, "ExternalOutput"); Ycd = kb.dram("Yc", [128, 2, 64], F32, "ExternalOutput")

    def sbb(name, shape, dt=F32):
        t = kb.sb(name, shape, dt); return t, Buf(t)
    qr_t, qr_b = sbb("qr", [128, SL], BF16); kr_t, kr_b = sbb("kr", [128, SL], BF16)
    qs_t, qs_b = sbb("qs", [128, SL], BF16); ks_t, ks_b = sbb("ks", [128, SL], BF16)
    kh_t, kh_b = sbb("kh", [128, NCH, 128], BF16); v_t, v_b = sbb("v", [128, NCH, 128], BF16)
    gi_t, gi_b = sbb("gi", [128, NCH]); gf_t, gf_b = sbb("gf", [128, NCH]); misc_t, misc_b = sbb("misc", [128, 8])
    nbf_t, nbf_b = sbb("nbf", [128, 1]); l1_t, l1_b = sbb("l1", [128, NCH]); nbc_t, nbc_b = sbb("nbcol", [128, NCH])
    t1_t, t1_b = sbb("t1", [128, NCH]); a_t, a_b = sbb("acol", [128, NCH]); eL_t, eL_b = sbb("eL", [128, NCH])
    gh_t, gh_b = sbb("gh", [128, NCH])
    tri_t, tri_b = sbb("tri", [128, 128]); mask_t, mask_b = sbb("mask", [128, 128])
    idf_t, idf_b = sbb("idf", [128, 128]); idb_t, idb_b = sbb("idb", [128, 128], BF16)
    onef_t, onef_b = sbb("onef", [128, 128]); oneb_t, oneb_b = sbb("oneb", [128, 128], BF16)
    ctmp_t = kb.sb("ctmp", [128, 2, 1024]); ctmp_b = [Buf(ctmp_t), Buf(ctmp_t)]
    st_t = kb.sb("st", [128, 4, 128], BF16); st_b = [Buf(st_t) for _ in range(4)]
    dg_t = kb.sb("dg", [128, 2, 128]); dg_b = [Buf(dg_t), Buf(dg_t)]
    neb_t = kb.sb("neb", [128, 4, 128]); neb_b = [Buf(neb_t) for _ in range(4)]
    dm_t = kb.sb("dm", [128, 2, 128]); dm_b = [Buf(dm_t), Buf(dm_t)]
    Cf_t, Cf_b = sbb("Cf", [128, 128]); Cb_t, Cb_b = sbb("Cb", [128, 128], BF16)
    nf_t, nf_b = sbb("nf", [128, 128]); nbb_t, nbb_b = sbb("nb16", [128, 128], BF16)
    ho_t = kb.sb("ho", [128, 2, 1024]); ho_b = [Buf(ho_t), Buf(ho_t)]
    XR_t = qr_t[:, 0:8192].bitcast(F32); XI_t = kr_t[:, 0:8192].bitcast(F32)
    Ur_t = qs_t[:, 0:8192].bitcast(F32); Ui_t = ks_t[:, 0:8192].bitcast(F32)
    XR_b, XI_b, Ur_b, Ui_b = Buf(XR_t), Buf(XI_t), Buf(Ur_t), Buf(Ui_t)
    c128_t, c128_b = sbb("c128", [128, 128]); s128_t, s128_b = sbb("s128", [128, 128]); ns128_t, ns128_b = sbb("ns128", [128, 128])
    tc_t, tc_b = sbb("tc", [128, 64]); ts_t, ts_b = sbb("ts", [128, 64])
    c64_t, c64_b = sbb("c64", [64, 64]); s64_t, s64_b = sbb("s64", [64, 64])
    XcR_t, XcR_b = sbb("XcR", [128, 2, 64]); XcI_t, XcI_b = sbb("XcI", [128, 2, 64])
    c256_t, c256_b = sbb("c256", [128, 2, 256]); s256_t, s256_b = sbb("s256", [128, 2, 256])
    ft_t = kb.sb("ft", [128, 4, 512]); ft_b = [Buf(ft_t) for _ in range(4)]
    Z_t = kb.sb("Z", [64, 4, 512]); Z_b = [Buf(Z_t) for _ in range(4)]
    Yo_t = kb.sb("Yo", [64, 2, 512]); Yo_b = [Buf(Yo_t), Buf(Yo_t)]; Yc_t, Yc_b = sbb("Yco", [128, 2, 64])
    pst = [kb.psum("pb%d" % i, [128, 512], F32) for i in range(7)]
    ptr_t = kb.psum("ptr", [128, 1024], BF16)
    kb.begin()

    for (t, b, d) in ((tri_t, tri_b, trid), (mask_t, mask_b, maskd), (idf_t, idf_b, identfd), (idb_t, idb_b, identbd)):
        kb.dma("sp", t[:], d, writes=[b])
    kb.op("pool", lambda: nc.gpsimd.memset(onef_t[:], 1.0), writes=[onef_b])
    kb.op("pool", lambda: nc.gpsimd.memset(oneb_t[:], 1.0), writes=[oneb_b])

    def slots(bank):
        b = Buf(pst[bank]); return [b, b, b, b]
    S_ps, dC_ps, dn_ps, nbc_ps, num_ps, den_ps = (slots(i) for i in range(6))
    g_ps = Buf(pst[6])
    tr_ps = [Buf(ptr_t) for _ in range(8)]

    def sl(s):
        return slice(s * 128, (s + 1) * 128)

    for u in range(U if do_mlstm else 0):
        kb.dma("sp", qr_t[:], qTd[u], writes=[qr_b])
        kb.dma("sp", kr_t[:], kTd[u], writes=[kr_b])
        kb.dma("sp", gi_t[:], gid[u], writes=[gi_b]); kb.dma("sp", gf_t[:], gfd[u], writes=[gf_b])
        kb.dma("sp", misc_t[:], miscd[u], writes=[misc_b])
        kb.dma("sp", v_t[:], vd[u], writes=[v_b])
        kb.op("dve", lambda: nc.vector.tensor_scalar(out=nbf_t[:], in0=misc_t[:, 7:8], scalar1=-1.0, scalar2=None, op0=ALU.mult),
              reads=[misc_b], writes=[nbf_b])
        kb.op("act", lambda: nc.scalar.activation(out=l1_t[:], in_=gf_t[:], func=AF.Exp, bias=nbf_t[:, 0:1], scale=-1.0),
              reads=[gf_b, nbf_b], writes=[l1_b])
        kb.op("act", lambda: nc.scalar.activation(out=l1_t[:], in_=l1_t[:], func=AF.Ln, bias=onef_t[:, 0:1], scale=1.0),
              reads=[l1_b, onef_b], writes=[l1_b])
        kb.op("pe", lambda: nc.tensor.matmul(pst[6][:, 0:NCH], tri_t[:, :], l1_t[:, :], start=True, stop=True),
              reads=[tri_b, l1_b], writes=[g_ps])
        kb.op("pe", lambda: nc.tensor.matmul(pst[6][:, 128:128 + NCH], onef_t[:, :], l1_t[:, :], start=True, stop=True),
              reads=[onef_b, l1_b], writes=[g_ps])
        kb.op("dve", lambda: nc.vector.tensor_copy(out=nbc_t[:], in_=pst[6][:, 0:NCH]), reads=[g_ps], writes=[nbc_b])
        kb.op("dve", lambda: nc.vector.scalar_tensor_tensor(out=t1_t[:], in0=gi_t[:], scalar=misc_t[:, 6:7], in1=nbc_t[:],
                                                            op0=ALU.add, op1=ALU.add), reads=[gi_b, misc_b, nbc_b], writes=[t1_b])
        kb.op("act", lambda: nc.scalar.activation(out=a_t[:], in_=t1_t[:], func=AF.Exp), reads=[t1_b], writes=[a_b])
        kb.op("act", lambda: nc.scalar.activation(out=eL_t[:], in_=pst[6][:, 128:128 + NCH], func=AF.Exp, scale=-1.0),
              reads=[g_ps], writes=[eL_b])
        kb.op("dve", lambda: nc.vector.scalar_tensor_tensor(out=gh_t[:], in0=a_t[:], scalar=ALPHA, in1=eL_t[:],
                                                            op0=ALU.mult, op1=ALU.mult), reads=[a_b, eL_b], writes=[gh_b])
        if phase < 2:
            continue
        blocks = [(0, 256, 0, 256)] + [(256 + j * 1024, 256 + (j + 1) * 1024, 256, SL) for j in range(8)]
        bi = 0
        for (raw_t, raw_b, o_t, o_b, tb0) in ((qr_t, qr_b, qs_t, qs_b, 0), (kr_t, kr_b, ks_t, ks_b, 3)):
            for (lo, hi, a, b) in blocks:
                n = hi - lo; cb = ctmp_b[bi % 2]; ci = bi % 2; bi += 1
                kb.op("act", lambda raw_t=raw_t, lo=lo, hi=hi, n=n, ci=ci, tb0=tb0: nc.scalar.activation(
                    out=ctmp_t[:, ci, :n], in_=raw_t[:, lo:hi], func=AF.Copy, scale=misc_t[:, tb0 + 1:tb0 + 2]),
                    reads=[raw_b, misc_b], writes=[cb])
                j0 = max(lo, a + 1)
                kb.op("dve", lambda raw_t=raw_t, lo=lo, hi=hi, n=n, ci=ci, tb0=tb0, j0=j0: nc.vector.scalar_tensor_tensor(
                    out=ctmp_t[:, ci, j0 - lo:n], in0=raw_t[:, j0 - 1:hi - 1], scalar=misc_t[:, tb0:tb0 + 1],
                    in1=ctmp_t[:, ci, j0 - lo:n], op0=ALU.mult, op1=ALU.add), reads=[raw_b, misc_b, cb], writes=[cb])
                j1 = min(hi, b - 1)
                kb.op("dve", lambda raw_t=raw_t, lo=lo, n=n, ci=ci, tb0=tb0, j1=j1: nc.vector.scalar_tensor_tensor(
                    out=ctmp_t[:, ci, 0:j1 - lo], in0=raw_t[:, lo + 1:j1 + 1], scalar=misc_t[:, tb0 + 2:tb0 + 3],
                    in1=ctmp_t[:, ci, 0:j1 - lo], op0=ALU.mult, op1=ALU.add), reads=[raw_b, misc_b, cb], writes=[cb])
                kb.op("act", lambda o_t=o_t, lo=lo, hi=hi, n=n, ci=ci: nc.scalar.activation(
                    out=o_t[:, lo:hi], in_=ctmp_t[:, ci, :n], func=AF.Silu), reads=[cb], writes=[o_b])
        if phase < 3:
            continue
        for c in range(NCH):
            tp = num_ps[c % 4]
            kb.op("pe", lambda c=c: nc.tensor.matmul(pst[4][:, sl(0)], ks_t[:, sl(c)], idb_t[:, :], start=True, stop=True),
                  reads=[ks_b, idb_b], writes=[tp])
            kb.op("dve", lambda c=c: nc.vector.tensor_scalar(out=kh_t[:, c, :], in0=pst[4][:, sl(0)],
                                                             scalar1=gh_t[:, c:c + 1], scalar2=None, op0=ALU.mult),
                  reads=[tp, gh_b], writes=[kh_b])
        if phase < 4:
            continue
        kb.op("pool", lambda: nc.gpsimd.memset(Cf_t[:], 0.0), writes=[Cf_b])
        kb.op("pool", lambda: nc.gpsimd.memset(Cb_t[:], 0.0), writes=[Cb_b])
        kb.op("pool", lambda: nc.gpsimd.memset(nf_t[:], 0.0), writes=[nf_b])
        kb.op("pool", lambda: nc.gpsimd.memset(nbb_t[:], 0.0), writes=[nbb_b])

        def stage_A(c):
            s = c % 4
            kb.op("pe", lambda: nc.tensor.matmul(pst[0][:, sl(0)], ks_t[:, sl(c)], qs_t[:, sl(c)], start=True, stop=True),
                  reads=[ks_b, qs_b], writes=[S_ps[s]])
            kb.op("dve", lambda: nc.vector.scalar_tensor_tensor(out=st_t[:, s, :], in0=pst[0][:, sl(0)], scalar=a_t[:, c:c + 1],
                                                                in1=mask_t[:, :], op0=ALU.mult, op1=ALU.mult),
                  reads=[S_ps[s], a_b, mask_b], writes=[st_b[s]])
            kb.op("dve", lambda: nc.vector.tensor_scalar(out=dg_t[:, c % 2, :], in0=idf_t[:, :], scalar1=nbc_t[:, c:c + 1], scalar2=None,
                                                         op0=ALU.mult), reads=[idf_b, nbc_b], writes=[dg_b[c % 2]])
            kb.op("pe", lambda: nc.tensor.matmul(pst[3][:, sl(0)], onef_t[:, :], dg_t[:, c % 2, :], start=True, stop=True),
                  reads=[onef_b, dg_b[c % 2]], writes=[nbc_ps[s]])
            kb.op("act", lambda: nc.scalar.activation(out=neb_t[:, s, :], in_=pst[3][:, sl(0)], func=AF.Exp),
                  reads=[nbc_ps[s]], writes=[neb_b[s]])

        def stage_B(c):
            s = c % 4
            kb.op("pe", lambda: nc.tensor.matmul(pst[4][:, sl(0)], v_t[:, c, :], st_t[:, s, :], start=True, stop=False),
                  reads=[v_b, st_b[s]], writes=[num_ps[s]])
            kb.op("pe", lambda: nc.tensor.matmul(pst[4][:, sl(0)], Cb_t[:, :], qs_t[:, sl(c)], start=False, stop=True),
                  reads=[Cb_b, qs_b], writes=[num_ps[s]])
            kb.op("pe", lambda: nc.tensor.matmul(pst[5][:, sl(0)], oneb_t[:, :], st_t[:, s, :], start=True, stop=False),
                  reads=[oneb_b, st_b[s]], writes=[den_ps[s]])
            kb.op("pe", lambda: nc.tensor.matmul(pst[5][:, sl(0)], nbb_t[:, :], qs_t[:, sl(c)], start=False, stop=True),
                  reads=[nbb_b, qs_b], writes=[den_ps[s]])
            d = c % 2
            kb.op("act", lambda: nc.scalar.activation(out=dm_t[:, d, :], in_=pst[5][:, sl(0)], func=AF.Abs), reads=[den_ps[s]], writes=[dm_b[d]])
            kb.op("dve", lambda: nc.vector.tensor_tensor(out=dm_t[:, d, :], in0=dm_t[:, d, :], in1=neb_t[:, s, :], op=ALU.max),
                  reads=[dm_b[d], neb_b[s]], writes=[dm_b[d]])
            kb.op("dve", lambda: nc.vector.reciprocal(out=dm_t[:, d, :], in_=dm_t[:, d, :]), reads=[dm_b[d]], writes=[dm_b[d]])
            g = c // 8; hb_ = ho_b[g % 2]
            kb.op("dve", lambda: nc.vector.tensor_tensor(out=ho_t[:, g % 2, (c % 8) * 128:(c % 8 + 1) * 128], in0=pst[4][:, sl(0)],
                                                         in1=dm_t[:, d, :], op=ALU.mult), reads=[num_ps[s], dm_b[d]], writes=[hb_])
            if c % 8 == 7 or c == NCH - 1:
                w = (c % 8 + 1) * 128
                kb.dma("sp", hTd[u][:, g * 1024:g * 1024 + w], ho_t[:, g % 2, :w], reads=[hb_], is_out=True)

        def stage_U(c):
            s = c % 4
            kb.op("pe", lambda: nc.tensor.matmul(pst[1][:, sl(0)], kh_t[:, c, :], v_t[:, c, :], start=True, stop=True),
                  reads=[kh_b, v_b], writes=[dC_ps[s]])
            kb.op("pe", lambda: nc.tensor.matmul(pst[2][:, sl(0)], kh_t[:, c, :], oneb_t[:, :], start=True, stop=True),
                  reads=[kh_b, oneb_b], writes=[dn_ps[s]])
            kb.op("dve", lambda: nc.vector.scalar_tensor_tensor(out=Cf_t[:, :], in0=Cf_t[:, :], scalar=eL_t[:, c:c + 1], in1=pst[1][:, sl(0)],
                                                                op0=ALU.mult, op1=ALU.add), reads=[Cf_b, eL_b, dC_ps[s]], writes=[Cf_b])
            kb.op("act", lambda: nc.scalar.copy(out=Cb_t[:, :], in_=Cf_t[:, :]), reads=[Cf_b], writes=[Cb_b])
            kb.op("dve", lambda: nc.vector.scalar_tensor_tensor(out=nf_t[:, :], in0=nf_t[:, :], scalar=eL_t[:, c:c + 1], in1=pst[2][:, sl(0)],
                                                                op0=ALU.mult, op1=ALU.add), reads=[nf_b, eL_b, dn_ps[s]], writes=[nf_b])
            kb.op("act", lambda: nc.scalar.copy(out=nbb_t[:, :], in_=nf_t[:, :]), reads=[nf_b], writes=[nbb_b])

        stage_A(0)
        for c in range(nscan):
            if c + 1 < nscan:
                stage_A(c + 1)
            stage_B(c)
            stage_U(c)

    kb.barrier()
    if not do_fourier:
        return kb.end()
    fps = [Buf(pst[i]) for i in range(7)]
    for (t, b, d) in ((XR_t, XR_b, XRd), (XI_t, XI_b, XId), (c128_t, c128_b, c128d), (s128_t, s128_b, s128d), (ns128_t, ns128_b, ns128d),
                      (tc_t, tc_b, tcd), (ts_t, ts_b, tsd), (c64_t, c64_b, c64d), (s64_t, s64_b, s64d), (XcR_t, XcR_b, XcRd),
                      (XcI_t, XcI_b, XcId), (c256_t, c256_b, c256d), (s256_t, s256_b, s256d)):
        kb.dma("sp", t[:], d, writes=[b])
    pi = 0
    for blk in range(8):
        cs = slice(blk * 512, (blk + 1) * 512)
        pr = fps[pi % 7]; pi += 1; pim = fps[pi % 7]; pi += 1
        prt, pit = pr.t, pim.t
        kb.op("pe", lambda: nc.tensor.matmul(prt[:, :], c128_t[:, :], XR_t[:, cs], start=True, stop=False), reads=[c128_b, XR_b], writes=[pr])
        kb.op("pe", lambda: nc.tensor.matmul(prt[:, :], s128_t[:, :], XI_t[:, cs], start=False, stop=True), reads=[s128_b, XI_b], writes=[pr])
        kb.op("pe", lambda: nc.tensor.matmul(pit[:, :], c128_t[:, :], XI_t[:, cs], start=True, stop=False), reads=[c128_b, XI_b], writes=[pim])
        kb.op("pe", lambda: nc.tensor.matmul(pit[:, :], ns128_t[:, :], XR_t[:, cs], start=False, stop=True), reads=[ns128_b, XR_b], writes=[pim])
        tcb = tc_t[:, blk * 8:(blk + 1) * 8].unsqueeze(2).to_broadcast([128, 8, 64])
        tsb = ts_t[:, blk * 8:(blk + 1) * 8].unsqueeze(2).to_broadcast([128, 8, 64])

        def v3(ap):
            return ap.rearrange("p (a b) -> p a b", b=64)
        for (o_t, o_b, pa, pa_b, pb, pb_b, op2) in ((Ur_t, Ur_b, prt, pr, pit, pim, ALU.add), (Ui_t, Ui_b, pit, pim, prt, pr, ALU.subtract)):
            kb.op("dve", lambda pa=pa: nc.vector.tensor_tensor(out=v3(ft_t[:, 0, :]), in0=v3(pa[:, :]), in1=tcb, op=ALU.mult),
                  reads=[pa_b, tc_b], writes=[ft_b[0]])
            kb.op("dve", lambda pb=pb: nc.vector.tensor_tensor(out=v3(ft_t[:, 1, :]), in0=v3(pb[:, :]), in1=tsb, op=ALU.mult),
                  reads=[pb_b, ts_b], writes=[ft_b[1]])
            kb.op("pool", lambda o_t=o_t, op2=op2: nc.gpsimd.tensor_tensor(out=o_t[:, cs], in0=ft_t[:, 0, :], in1=ft_t[:, 1, :], op=op2),
                  reads=[ft_b[0], ft_b[1]], writes=[o_b])
    for j in range(16):
        zr = fps[pi % 7]; pi += 1; zi = fps[pi % 7]; pi += 1
        for (U_t, U_b, z) in ((Ur_t, Ur_b, zr), (Ui_t, Ui_b, zi)):
            for q in range(4):
                cp = 4 * j + q
                kb.op("pe", lambda U_t=U_t, z=z, q=q, cp=cp: nc.tensor.matmul(
                    z.t[:64, q * 128:(q + 1) * 128], U_t[:, :].rearrange("p (t c) -> p t c", c=64)[:, :, cp], idf_t[:, :], start=True, stop=True),
                    reads=[U_b, idf_b], writes=[z])
        zs = (2 * j) % 4
        kb.op("act", lambda: nc.scalar.copy(out=Z_t[:, zs, :], in_=zr.t[:64, :]), reads=[zr], writes=[Z_b[zs]])
        kb.op("dve", lambda: nc.vector.tensor_copy(out=Z_t[:, zs + 1, :], in_=zi.t[:64, :]), reads=[zi], writes=[Z_b[zs + 1]])
        yp = fps[pi % 7]; pi += 1
        kb.op("pe", lambda: nc.tensor.matmul(yp.t[:64, :], c64_t[:, :], Z_t[:, zs, :], start=True, stop=False), reads=[c64_b, Z_b[zs]], writes=[yp])
        kb.op("pe", lambda: nc.tensor.matmul(yp.t[:64, :], s64_t[:, :], Z_t[:, zs + 1, :], start=False, stop=True), reads=[s64_b, Z_b[zs + 1]], writes=[yp])
        evac(kb, j, Yo_t[:, j % 2, :], yp.t[:64, :], [yp], [Yo_b[j % 2]])
        kb.dma("sp", Yd[:, j * 512:(j + 1) * 512], Yo_t[:, j % 2, :], reads=[Yo_b[j % 2]], is_out=True)
    for k2 in range(2):
        yp = fps[pi % 7]; pi += 1
        n = 0
        for (tab_t, tab_b, X_t, X_b) in ((c256_t, c256_b, XcR_t, XcR_b), (s256_t, s256_b, XcI_t, XcI_b)):
            for tcn in range(2):
                kb.op("pe", lambda tab_t=tab_t, X_t=X_t, tcn=tcn, n=n: nc.tensor.matmul(
                    yp.t[:, :64], tab_t[:, tcn, k2 * 128:(k2 + 1) * 128], X_t[:, tcn, :], start=(n == 0), stop=(n == 3)),
                    reads=[tab_b, X_b], writes=[yp])
                n += 1
        kb.op("dve", lambda: nc.vector.tensor_copy(out=Yc_t[:, k2, :], in_=yp.t[:, :64]), reads=[yp], writes=[Yc_b])
    kb.dma("sp", Ycd, Yc_t[:], reads=[Yc_b], is_out=True)
    return kb.end()


def gather_fm(res, key, nrows_chunks, dtype):
    lat = [np.zeros((nrows_chunks * 128, SEQ), dtype) for _ in range(NB)]
    ctx = [np.zeros((nrows_chunks * 128, CTX), dtype) for _ in range(NB)]
    for core in range(NCORES):
        b, q = core // 4, core % 4
        a = res[core][key]
        a = a.transpose(1, 0, 2).reshape(nrows_chunks * 128, NT)
        lat[b][:, q * TOK:(q + 1) * TOK] = a[:, :TOK]
        ctx[b][:, q * CT:(q + 1) * CT] = a[:, TOK:]
    return lat, ctx


def l2_consts():
    s = np.arange(128)
    tri = (s[:, None] <= s[None, :]).astype(np.float32)
    mask = tri * np.float32(ALPHA)
    ident = np.eye(128, dtype=np.float32)
    t1 = np.arange(128)
    a128 = 2 * np.pi * np.outer(t1, t1) / 128
    k1 = np.arange(128)[:, None]; t2 = np.arange(64)[None, :]
    atw = 2 * np.pi * k1 * t2 / SEQ
    a64 = 2 * np.pi * np.outer(np.arange(64), np.arange(64)) / 64
    sc = 1.0 / np.sqrt(SEQ * 64.0)
    t = np.arange(256); a256 = 2 * np.pi * np.outer(t, t) / 256
    scc = 1.0 / np.sqrt(256 * 64.0)
    c256 = (np.cos(a256) * scc).reshape(2, 128, 256).transpose(1, 0, 2)
    s256 = (np.sin(a256) * scc).reshape(2, 128, 256).transpose(1, 0, 2)
    f = lambda x: np.ascontiguousarray(x, dtype=np.float32)
    return dict(tri=tri, mask=mask, identf=ident, identb=ident.astype(NPBF),
                c128=f(np.cos(a128)), s128=f(np.sin(a128)), ns128=f(-np.sin(a128)),
                tc=f(np.cos(atw)), ts=f(np.sin(atw)), c64=f(np.cos(a64) * sc), s64=f(np.sin(a64) * sc),
                c256=f(c256), s256=f(s256))


def run_L2(inp, r1):
    qkv_l, qkv_c = gather_fm(r1, "o_qkv", 18, NPBF)
    fr_l, fr_c = gather_fm(r1, "o_fr", 2, np.float32)
    fi_l, fi_c = gather_fm(r1, "o_fi", 2, np.float32)
    G_l = [np.zeros((24, SEQ), np.float32) for _ in range(NB)]; G_c = [np.zeros((24, CTX), np.float32) for _ in range(NB)]
    for core in range(NCORES):
        b, q = core // 4, core % 4
        g = r1[core]["o_g"]
        G_l[b][:, q * TOK:(q + 1) * TOK] = g[:, :TOK]; G_c[b][:, q * CT:(q + 1) * CT] = g[:, TOK:]
    consts = l2_consts()
    bg = inp["b_gate"][0]; wc = inp["w_qk_conv"][0]

    def seqcat(c, l, rev):
        if rev:
            c = c[..., ::-1]; l = l[..., ::-1]
        return np.concatenate([c, l], axis=-1)
    in_maps = []
    for core in range(NCORES):
        m = dict(consts)
        for u in range(3):
            uid = core * 3 + u
            b, hd, dr = uid // 12, (uid % 12) // 2, uid % 2
            rows = lambda base: slice(base + hd * 128, base + (hd + 1) * 128)
            m["qT%d" % u] = np.ascontiguousarray(seqcat(qkv_c[b][rows(0)], qkv_l[b][rows(0)], dr))
            m["kT%d" % u] = np.ascontiguousarray(seqcat(qkv_c[b][rows(768)], qkv_l[b][rows(768)], dr))
            vT = seqcat(qkv_c[b][rows(1536)], qkv_l[b][rows(1536)], dr)
            m["v%d" % u] = np.ascontiguousarray(vT.T.reshape(NCH, 128, 128).transpose(1, 0, 2))
            gi = seqcat(G_c[b][dr * 12 + hd], G_l[b][dr * 12 + hd], dr)
            gf = seqcat(G_c[b][dr * 12 + 6 + hd], G_l[b][dr * 12 + 6 + hd], dr)
            m["gi%d" % u] = np.ascontiguousarray(gi.reshape(NCH, 128).T)
            m["gf%d" % u] = np.ascontiguousarray(gf.reshape(NCH, 128).T)
            misc = np.zeros((128, 8), np.float32)
            qt = wc[:, hd * 128:(hd + 1) * 128]; kt = wc[:, 768 + hd * 128:768 + (hd + 1) * 128]
            if dr:
                qt = qt[::-1]; kt = kt[::-1]
            misc[:, 0:3] = qt.T; misc[:, 3:6] = kt.T
            misc[:, 6] = bg[dr * 12 + hd]; misc[:, 7] = bg[dr * 12 + 6 + hd]
            m["misc%d" % u] = misc
        b, grp = core // 4, core % 4
        rows = slice(grp * 64, (grp + 1) * 64)
        m["XR"] = np.ascontiguousarray(fr_l[b][rows].T.reshape(128, 4096))
        m["XI"] = np.ascontiguousarray(fi_l[b][rows].T.reshape(128, 4096))
        m["XcR"] = np.ascontiguousarray(fr_c[b][rows].T.reshape(2, 128, 64).transpose(1, 0, 2))
        m["XcI"] = np.ascontiguousarray(fi_c[b][rows].T.reshape(2, 128, 64).transpose(1, 0, 2))
        in_maps.append(m)
    import os
    res = run(build_L2(do_mlstm=os.environ.get('NO_MLSTM') is None, do_fourier=os.environ.get('NO_FOURIER') is None, phase=int(os.environ.get('PHASE', '9')), nscan=int(os.environ.get('NSCAN', str(NCH))), U=int(os.environ.get('NU', '3'))), in_maps)
    H = [[np.zeros((768, SL), np.float32) for _ in range(2)] for _ in range(NB)]
    for core in range(NCORES):
        for u in range(3):
            uid = core * 3 + u
            b, hd, dr = uid // 12, (uid % 12) // 2, uid % 2
            h = res[core]["hT%d" % u]
            if dr:
                h = np.concatenate([h[:, :CTX][:, ::-1], h[:, CTX:][:, ::-1]], axis=1)
            H[b][dr][hd * 128:(hd + 1) * 128] = h
    four_l = [np.zeros((256, SEQ), np.float32) for _ in range(NB)]; four_c = [np.zeros((256, CTX), np.float32) for _ in range(NB)]
    for core in range(NCORES):
        b, grp = core // 4, core % 4
        Y = res[core]["Y"].reshape(64, 64, 128)
        four_l[b][grp * 64:(grp + 1) * 64] = Y.transpose(1, 0, 2).reshape(64, SEQ)
        Yc = res[core]["Yc"]
        four_c[b][grp * 64:(grp + 1) * 64] = Yc.transpose(2, 1, 0).reshape(64, CTX)
    return H, four_l, four_c


def emit_wout_res(kb, ring, i, n, col, m_t, m_b, w_t, w_b, x_ap_fn, x_b, mod_t, mod_b, vgate):
    nc = kb.nc
    for oc in range(8):
        ps = ring.next()
        for kc in range(8):
            kb.op("pe", lambda kc=kc, oc=oc, ps=ps: nc.tensor.matmul(ps.t[:, :n], w_t[:, kc, oc * 128:(oc + 1) * 128], m_t[:, kc, :n],
                                                                   start=(kc == 0), stop=(kc == 7)), reads=[w_b, m_b], writes=[ps])
        kb.op("dve", lambda oc=oc, ps=ps: nc.vector.scalar_tensor_tensor(
            out=x_ap_fn(oc), in0=ps.t[:, :n], scalar=mod_t[:, vgate * 8 + oc, col:col + 1], in1=x_ap_fn(oc), op0=ALU.mult, op1=ALU.add),
            reads=[ps, mod_b, x_b], writes=[x_b])


def emit_headnorm(kb, ring, src_t, src_b, n, sq_t, sq_b, rs_t, rs_b, ones_t, ones_b, eps_t, eps_b):
    nc = kb.nc
    kb.op("act", lambda: nc.scalar.activation(out=sq_t, in_=src_t, func=AF.Square), reads=[src_b], writes=[sq_b])
    ps = ring.next()
    kb.op("pe", lambda: nc.tensor.matmul(ps.t[:, :n], ones_t[:, :], sq_t, start=True, stop=True), reads=[sq_b, ones_b], writes=[ps])
    kb.op("act", lambda: nc.scalar.activation(out=rs_t, in_=ps.t[:, :n], func=AF.Sqrt, bias=eps_t[:, 0:1], scale=1.0),
          reads=[ps, eps_b], writes=[rs_b])
    kb.op("dve", lambda: nc.vector.reciprocal(out=rs_t, in_=rs_t), reads=[rs_b], writes=[rs_b])


def build_L3():
    kb = KB(); nc = kb.nc
    xT = kb.dram("xT", [128, 8, NT], F32, "ExternalInput")
    hfd = kb.dram("hf", [128, 6, NT], F32, "ExternalInput"); hbd = kb.dram("hb", [128, 6, NT], F32, "ExternalInput")
    od = kb.dram("o", [128, 6, NT], F32, "ExternalInput"); yfd = kb.dram("yf", [128, 2, NT], F32, "ExternalInput")
    ghd = kb.dram("gh", [128, 6], F32, "ExternalInput"); modd = kb.dram("mod", [128, 48, 2], F32, "ExternalInput")
    wd = kb.dram("w", [1024, 1024], F32, "ExternalInput")
    xo = kb.dram("xo", [128, 8, NT], F32, "ExternalOutput")

    def sbb(name, shape, dt=F32, nb=1):
        t = kb.sb(name, shape, dt)
        return (t, Buf(t)) if nb == 1 else (t, [Buf(t) for _ in range(nb)])
    x_t, x_b = sbb("x", [128, 2, 8, 512], nb=2)
    hf_t, hf_b = sbb("hf", [128, 2, 6, 512], nb=2); hb_t, hb_b = sbb("hb", [128, 2, 6, 512], nb=2)
    o_t, o_b = sbb("o", [128, 2, 6, 512], nb=2); yf_t, yf_b = sbb("yf", [128, 2, 2, 512], nb=2)
    gh_t, gh_b = sbb("gh", [128, 6]); mod_t, mod_b = sbb("mod", [128, 48, 2])
    w_t, w_b = sbb("w", [128, 8, 1024], BF16)
    hs_t, hs_b = sbb("hs", [128, 2, 512], nb=2); sq_t, sq_b = sbb("sq", [128, 2, 512], BF16, nb=2)
    rs_t, rs_b = sbb("rs", [128, 2, 512], nb=2); sg_t, sg_b = sbb("sg", [128, 2, 512], nb=2)
    m_t, m_b = sbb("m", [128, 2, 8, 512], BF16, nb=2)
    ones_t, ones_b = sbb("ones", [128, 128], BF16); eps_t, eps_b = sbb("eps", [128, 1])
    ring = PsumRing(kb)
    kb.begin()
    kb.dma("sp", gh_t[:], ghd, writes=[gh_b]); kb.dma("sp", mod_t[:], modd, writes=[mod_b])
    kb.dma("pool", w_t[:], wd.rearrange("(kc p) n -> p kc n", p=128), writes=[w_b])
    kb.op("pool", lambda: nc.gpsimd.memset(ones_t[:], 1.0), writes=[ones_b])
    kb.op("pool", lambda: nc.gpsimd.memset(eps_t[:], 128.0 * EPS), writes=[eps_b])
    kb.op("dve", lambda: nc.vector.tensor_scalar(out=gh_t[:], in0=gh_t[:], scalar1=float(np.sqrt(128.0)), scalar2=None, op0=ALU.mult),
          reads=[gh_b], writes=[gh_b])
    for i, (lo, n) in enumerate(TBS):
        d = i % 2; col = tb_col(i)
        kb.dma("sp", x_t[:, d, :, :n], xT[:, :, lo:lo + n], writes=[x_b[d]])
        kb.dma("sp", hf_t[:, d, :, :n], hfd[:, :, lo:lo + n], writes=[hf_b[d]])
        kb.dma("sp", hb_t[:, d, :, :n], hbd[:, :, lo:lo + n], writes=[hb_b[d]])
        kb.dma("sp", o_t[:, d, :, :n], od[:, :, lo:lo + n], writes=[o_b[d]])
        kb.dma("sp", yf_t[:, d, :, :n], yfd[:, :, lo:lo + n], writes=[yf_b[d]])
        for hd in range(6):
            e = hd % 2
            kb.op("dve", lambda hd=hd, e=e: nc.vector.tensor_tensor(out=hs_t[:, e, :n], in0=hf_t[:, d, hd, :n], in1=hb_t[:, d, hd, :n], op=ALU.add),
                  reads=[hf_b[d], hb_b[d]], writes=[hs_b[e]])
            emit_headnorm(kb, ring, hs_t[:, e, :n], hs_b[e], n, sq_t[:, e, :n], sq_b[e], rs_t[:, e, :n], rs_b[e], ones_t, ones_b, eps_t, eps_b)
            kb.op("act", lambda hd=hd, e=e: nc.scalar.activation(out=sg_t[:, e, :n], in_=o_t[:, d, hd, :n], func=AF.Sigmoid),
                  reads=[o_b[d]], writes=[sg_b[e]])
            kb.op("dve", lambda hd=hd, e=e: nc.vector.scalar_tensor_tensor(out=hs_t[:, e, :n], in0=hs_t[:, e, :n], scalar=gh_t[:, hd:hd + 1],
                                                                           in1=rs_t[:, e, :n], op0=ALU.mult, op1=ALU.mult),
                  reads=[hs_b[e], gh_b, rs_b[e]], writes=[hs_b[e]])
            kb.op("pool", lambda hd=hd, e=e: nc.gpsimd.tensor_tensor(out=m_t[:, d, hd, :n], in0=hs_t[:, e, :n], in1=sg_t[:, e, :n], op=ALU.mult),
                  reads=[hs_b[e], sg_b[e]], writes=[m_b[d]])
        for j in range(2):
            evac(kb, j, m_t[:, d, 6 + j, :n], yf_t[:, d, j, :n], [yf_b[d]], [m_b[d]])
        emit_wout_res(kb, ring, i, n, col, m_t[:, d], m_b[d], w_t, w_b, lambda oc, d=d, n=n: x_t[:, d, oc, :n], x_b[d], mod_t, mod_b, 2)
        kb.dma("sp", xo[:, :, lo:lo + n], x_t[:, d, :, :n], reads=[x_b[d]], is_out=True)
    return kb.end()


def build_FFN(final):
    kb = KB(); nc = kb.nc
    xT = kb.dram("xT", [128, 8, NT], F32, "ExternalInput")
    modd = kb.dram("mod", [128, 48, 2], F32, "ExternalInput")
    gnd = kb.dram("gn", [128, 8], F32, "ExternalInput")
    w1d = kb.dram("w1", [1024, 4096], F32, "ExternalInput"); w2d = kb.dram("w2", [4096, 1024], F32, "ExternalInput")
    gfd = kb.dram("gf", [128, 8], F32, "ExternalInput")
    xo = kb.dram("xo", [128, 8, NT], F32, "ExternalOutput")

    def sbb(name, shape, dt=F32, nb=1):
        t = kb.sb(name, shape, dt)
        return (t, Buf(t)) if nb == 1 else (t, [Buf(t) for _ in range(nb)])
    x_t = kb.sb("x", [128, 8, NT]); xb = [Buf(x_t) for _ in TBS]
    h_t = kb.sb("h", [128, 8, NT], BF16); hb = [Buf(h_t) for _ in TBS]
    sq_t, sqb = sbb("sq", [128, 2, 8, 512], BF16, nb=2)
    tmp_t, tmpb = sbb("tmp", [128, 2, 512], nb=2); rs_t, rsb = sbb("rs", [128, 2, 512], nb=2)
    mod_t, mod_b = sbb("mod", [128, 48, 2]); gn_t, gn_b = sbb("gn", [128, 8]); gf_t, gf_b = sbb("gf", [128, 8])
    A_t, A_b = sbb("A", [128, 8, 2]); ones_t, ones_b = sbb("ones", [128, 128], BF16); eps_t, eps_b = sbb("eps", [128, 1])
    w1_t, w1_b = sbb("w1", [128, 2, 8, 1024], BF16, nb=2); w2_t, w2_b = sbb("w2", [128, 2, 8, 1024], BF16, nb=2)
    a_t, a_b = sbb("a", [128, 2, 8, 512], BF16, nb=2); r_t, r_b = sbb("r", [128, 2, 512], nb=2)
    ring = PsumRing(kb)
    kb.begin()
    kb.dma("sp", mod_t[:], modd, writes=[mod_b]); kb.dma("sp", gn_t[:], gnd, writes=[gn_b]); kb.dma("sp", gf_t[:], gfd, writes=[gf_b])
    for i, (lo, n) in enumerate(TBS):
        kb.dma("sp", x_t[:, :, lo:lo + n], xT[:, :, lo:lo + n], writes=[xb[i]])
    kb.op("pool", lambda: nc.gpsimd.memset(ones_t[:], 1.0), writes=[ones_b])
    kb.op("pool", lambda: nc.gpsimd.memset(eps_t[:], 1024.0 * EPS), writes=[eps_b])

    def load_w(p):
        kb.dma("pool", w1_t[:, p % 2], w1d[:, p * 1024:(p + 1) * 1024].rearrange("(kc p) n -> p kc n", p=128), writes=[w1_b[p % 2]])
        kb.dma("pool", w2_t[:, p % 2], w2d[p * 1024:(p + 1) * 1024, :].rearrange("(kc p) n -> p kc n", p=128), writes=[w2_b[p % 2]])
    load_w(0)
    emit_modprep(kb, mod_t, mod_b, gn_t, gn_b, A_t, A_b, vscale=4)
    for i in range(len(TBS)):
        emit_norm_block(kb, ring, i, x_t, xb[i], h_t, hb[i], sq_t, sqb, tmp_t, tmpb, rs_t, rsb, ones_t, ones_b,
                        A_t, A_b, mod_t, mod_b, 3, eps_t, eps_b)
    k = 0
    for p in range(4):
        if p + 1 < 4:
            load_w(p + 1)
        wq = p % 2
        for i, (lo, n) in enumerate(TBS):
            col = tb_col(i); d = k % 2; k += 1
            for fc in range(8):
                ps = ring.next()
                for kc in range(8):
                    kb.op("pe", lambda kc=kc, fc=fc, ps=ps: nc.tensor.matmul(ps.t[:, :n], w1_t[:, wq, kc, fc * 128:(fc + 1) * 128], h_t[:, kc, lo:lo + n],
                                                                           start=(kc == 0), stop=(kc == 7)), reads=[w1_b[wq], hb[i]], writes=[ps])
                e = fc % 2
                kb.op("act", lambda ps=ps, e=e: nc.scalar.activation(out=r_t[:, e, :n], in_=ps.t[:, :n], func=AF.Relu), reads=[ps], writes=[r_b[e]])
                kb.op("pool", lambda fc=fc, e=e, d=d: nc.gpsimd.tensor_tensor(out=a_t[:, d, fc, :n], in0=r_t[:, e, :n], in1=r_t[:, e, :n], op=ALU.mult),
                      reads=[r_b[e]], writes=[a_b[d]])
            emit_wout_res(kb, ring, i, n, col, a_t[:, d], a_b[d], w2_t[:, wq], w2_b[wq],
                          lambda oc, lo=lo, n=n: x_t[:, oc, lo:lo + n], xb[i], mod_t, mod_b, 5)
    if not final:
        for i, (lo, n) in enumerate(TBS):
            kb.dma("sp", xo[:, :, lo:lo + n], x_t[:, :, lo:lo + n], reads=[xb[i]], is_out=True)
        return kb.end()
    kb.op("dve", lambda: nc.vector.tensor_scalar(out=gf_t[:], in0=gf_t[:], scalar1=32.0, scalar2=None, op0=ALU.mult), reads=[gf_b], writes=[gf_b])
    for i, (lo, n) in enumerate(TBS):
        sb_ = sqb[i % 2]
        for kc in range(8):
            kb.op("act", lambda kc=kc: nc.scalar.activation(out=sq_t[:, i % 2, kc, :n], in_=x_t[:, kc, lo:lo + n], func=AF.Square), reads=[xb[i]], writes=[sb_])
        ps = ring.next()
        for kc in range(8):
            kb.op("pe", lambda kc=kc, ps=ps: nc.tensor.matmul(ps.t[:, :n], ones_t[:, :], sq_t[:, i % 2, kc, :n], start=(kc == 0), stop=(kc == 7)),
                  reads=[sb_, ones_b], writes=[ps])
        rb = rsb[i % 2]
        kb.op("act", lambda ps=ps: nc.scalar.activation(out=rs_t[:, i % 2, :n], in_=ps.t[:, :n], func=AF.Sqrt, bias=eps_t[:, 0:1], scale=1.0),
              reads=[ps, eps_b], writes=[rb])
        kb.op("dve", lambda: nc.vector.reciprocal(out=rs_t[:, i % 2, :n], in_=rs_t[:, i % 2, :n]), reads=[rb], writes=[rb])
        for kc in range(8):
            kb.op("dve", lambda kc=kc: nc.vector.scalar_tensor_tensor(out=x_t[:, kc, lo:lo + n], in0=x_t[:, kc, lo:lo + n], scalar=gf_t[:, kc:kc + 1],
                                                                      in1=rs_t[:, i % 2, :n], op0=ALU.mult, op1=ALU.mult),
                  reads=[xb[i], gf_b, rb], writes=[xb[i]])
        kb.dma("sp", xo[:, :, lo:lo + n], x_t[:, :, lo:lo + n], reads=[xb[i]], is_out=True)
    return kb.end()


def slab(lat, ctx, core, nch):
    b, q = core // 4, core % 4
    a = np.concatenate([lat[b][:, q * TOK:(q + 1) * TOK], ctx[b][:, q * CT:(q + 1) * CT]], axis=1)
    return np.ascontiguousarray(a.reshape(nch, 128, NT).transpose(1, 0, 2))


def run_L3(inp, mod, xTs, r1, H, four_l, four_c):
    o_l, o_c = gather_fm(r1, "o_o", 6, np.float32)
    w = np.ascontiguousarray(inp["w_out_even"][0]); gh = fm_vec(inp["g_mlstm_head"][0], 6)
    hf_l = [H[b][0][:, CTX:] for b in range(NB)]; hf_c = [H[b][0][:, :CTX] for b in range(NB)]
    hb_l = [H[b][1][:, CTX:] for b in range(NB)]; hb_c = [H[b][1][:, :CTX] for b in range(NB)]
    in_maps = []
    for c in range(NCORES):
        in_maps.append({"xT": xTs[c], "hf": slab(hf_l, hf_c, c, 6), "hb": slab(hb_l, hb_c, c, 6), "o": slab(o_l, o_c, c, 6),
                        "yf": slab(four_l, four_c, c, 2), "gh": gh, "mod": mod_for_core(mod, 0, c), "w": w})
    res = run(build_L3(), in_maps)
    return [res[c]["xo"] for c in range(NCORES)]


def run_FFN(inp, mod, xTs, l, final):
    gn = fm_vec(inp["g_norm"][l, 1], 8); gf = fm_vec(inp["g_final"], 8)
    w1 = np.ascontiguousarray(inp["w_ff1"][l]); w2 = np.ascontiguousarray(inp["w_ff2"][l])
    in_maps = [{"xT": xTs[c], "mod": mod_for_core(mod, l, c), "gn": gn, "w1": w1, "w2": w2, "gf": gf} for c in range(NCORES)]
    res = run(build_FFN(final), in_maps)
    return [res[c]["xo"] for c in range(NCORES)]


def unslab(xTs):
    x = np.zeros((NB, SEQ, D), np.float32); xc = np.zeros((NB, CTX, D), np.float32)
    for core in range(NCORES):
        b, q = core // 4, core % 4
        t = xTs[core].transpose(2, 1, 0).reshape(NT, D)
        x[b, q * TOK:(q + 1) * TOK] = t[:TOK]; xc[b, q * CT:(q + 1) * CT] = t[TOK:]
    return x, xc


def build_L5():
    kb = KB(); nc = kb.nc
    WC = 4352
    xT = kb.dram("xT", [128, 8, NT], F32, "ExternalInput")
    mod = kb.dram("mod", [128, 48, 2], F32, "ExternalInput")
    gn = kb.dram("gn", [128, 8], F32, "ExternalInput")
    w = kb.dram("w", [1024, WC], F32, "ExternalInput")
    cosd = kb.dram("cosT", [128, TOK], F32, "ExternalInput"); sind = kb.dram("sinT", [128, TOK], F32, "ExternalInput")
    o_qkv = kb.dram("o_qkv", [128, 18, NT], BF16, "ExternalOutput")
    o_glu = kb.dram("o_glu", [128, 4, NT], F32, "ExternalOutput")
    x_t = kb.sb("x_t", [128, 8, NT]); xb = [Buf(x_t) for _ in TBS]
    h_t = kb.sb("h_t", [128, 8, NT], BF16); hb = [Buf(h_t) for _ in TBS]
    sq_t = kb.sb("sq_t", [128, 2, 8, 512], BF16); sqb = [Buf(sq_t), Buf(sq_t)]
    tmp_t = kb.sb("tmp_t", [128, 2, 512]); tmpb = [Buf(tmp_t), Buf(tmp_t)]
    rs_t = kb.sb("rs_t", [128, 2, 512]); rsb = [Buf(rs_t), Buf(rs_t)]
    mod_t = kb.sb("mod_t", [128, 48, 2]); mod_b = Buf(mod_t)
    gn_t = kb.sb("gn_t", [128, 8]); gn_b = Buf(gn_t)
    A_t = kb.sb("A_t", [128, 8, 2]); A_b = Buf(A_t)
    ones_t = kb.sb("ones_t", [128, 128], BF16); ones_b = Buf(ones_t)
    eps_t = kb.sb("eps_t", [128, 1]); eps_b = Buf(eps_t)
    cos_t = kb.sb("cos_t", [128, TOK]); cos_b = Buf(cos_t)
    sin_t = kb.sb("sin_t", [128, TOK]); sin_b = Buf(sin_t)
    NWB = 3
    wg_t = kb.sb("wg_t", [128, NWB, 8, 512], BF16); wgb = [Buf(wg_t) for _ in range(NWB)]
    NST = 4
    stb_t = kb.sb("stb_t", [128, NST, 512], BF16); stbb = [Buf(stb_t) for _ in range(NST)]
    stf_t = kb.sb("stf_t", [128, NST, 512]); stfb = [Buf(stf_t) for _ in range(NST)]
    ta_t = kb.sb("ta_t", [128, 2, 512]); ta_b = [Buf(ta_t), Buf(ta_t)]
    tb_t = kb.sb("tb_t", [128, 2, 512]); tb_b = [Buf(tb_t), Buf(tb_t)]
    ring = PsumRing(kb)
    kb.begin()
    kb.dma("sp", mod_t[:], mod, writes=[mod_b]); kb.dma("sp", gn_t[:], gn, writes=[gn_b])
    kb.dma("sp", cos_t[:], cosd, writes=[cos_b]); kb.dma("sp", sin_t[:], sind, writes=[sin_b])
    for i, (lo, n) in enumerate(TBS):
        kb.dma("sp", x_t[:, :, lo:lo + n], xT[:, :, lo:lo + n], writes=[xb[i]])
    kb.op("pool", lambda: nc.gpsimd.memset(ones_t[:], 1.0), writes=[ones_b])
    kb.op("pool", lambda: nc.gpsimd.memset(eps_t[:], 1024.0 * EPS), writes=[eps_b])
    groups = [(c0, 512) for c0 in range(0, 3584, 512)] + [(3584, 256), (3840, 512)]

    def load_group(gi):
        c0, gw = groups[gi]
        kb.dma("pool", wg_t[:, gi % NWB, :, :gw], w[:, c0:c0 + gw].rearrange("(kc p) n -> p kc n", p=128), writes=[wgb[gi % NWB]])
    load_group(0); load_group(1)
    emit_modprep(kb, mod_t, mod_b, gn_t, gn_b, A_t, A_b, vscale=1)
    for i in range(len(TBS)):
        emit_norm_block(kb, ring, i, x_t, xb[i], h_t, hb[i], sq_t, sqb, tmp_t, tmpb, rs_t, rsb, ones_t, ones_b,
                        A_t, A_b, mod_t, mod_b, 0, eps_t, eps_b)
    k = 0

    def mm(gi, cl, i, lo, n):
        ps = ring.next()
        for kc in range(8):
            kb.op("pe", lambda kc=kc: nc.tensor.matmul(ps.t[:, :n], wg_t[:, gi % NWB, kc, cl:cl + 128], h_t[:, kc, lo:lo + n],
                                                     start=(kc == 0), stop=(kc == 7)), reads=[wgb[gi % NWB], hb[i]], writes=[ps])
        return ps
    for gi, (c0, gw) in enumerate(groups):
        if gi + 2 < len(groups):
            load_group(gi + 2)
        if gi < 6:
            items = [("rope", 0, 2 * gi), ("rope", 128, 2 * gi + 1)]
        elif gi == 6:
            items = [("v", cl, 12 + cl // 128) for cl in range(0, 512, 128)]
        elif gi == 7:
            items = [("v", 0, 16), ("v", 128, 17)]
        else:
            items = [("glu", cl, cl // 128) for cl in range(0, 512, 128)]
        for (kind, cl, idx) in items:
            for i, (lo, n) in enumerate(TBS):
                k += 1
                ps = mm(gi, cl, i, lo, n)
                if kind == "glu":
                    sb_ = stfb[k % NST]
                    evac(kb, k, stf_t[:, k % NST, :n], ps.t[:, :n], [ps], [sb_])
                    kb.dma("sp", o_glu[:, idx, lo:lo + n], stf_t[:, k % NST, :n], reads=[sb_], is_out=True)
                    continue
                sb_ = stbb[k % NST]
                if kind == "v" or i == 4:
                    evac(kb, k, stb_t[:, k % NST, :n], ps.t[:, :n], [ps], [sb_])
                else:
                    ps2 = mm(gi, 256 + cl, i, lo, n)
                    e = k % 2
                    kb.op("dve", lambda ps=ps, e=e: nc.vector.tensor_tensor(out=ta_t[:, e, :n], in0=ps.t[:, :n], in1=cos_t[:, lo:lo + n], op=ALU.mult),
                          reads=[ps, cos_b], writes=[ta_b[e]])
                    kb.op("dve", lambda ps2=ps2, e=e: nc.vector.tensor_tensor(out=tb_t[:, e, :n], in0=ps2.t[:, :n], in1=sin_t[:, lo:lo + n], op=ALU.mult),
                          reads=[ps2, sin_b], writes=[tb_b[e]])
                    kb.op("pool", lambda e=e, k=k: nc.gpsimd.tensor_tensor(out=stb_t[:, k % NST, :n], in0=ta_t[:, e, :n], in1=tb_t[:, e, :n], op=ALU.add),
                          reads=[ta_b[e], tb_b[e]], writes=[sb_])
                kb.dma("sp", o_qkv[:, idx, lo:lo + n], stb_t[:, k % NST, :n], reads=[sb_], is_out=True)
    return kb.end()


def rope_tables(q):
    t = q * TOK + np.arange(TOK)
    row = (t // 64).astype(np.float64); colp = (t % 64).astype(np.float64)
    inv = 10000.0 ** (-np.arange(16) / 16.0)
    ang = np.concatenate([row[:, None] * inv[None, :], colp[:, None] * inv[None, :]], axis=1)
    cosT = np.zeros((128, TOK), np.float32); sinT = np.zeros((128, TOK), np.float32)
    for r in range(128):
        d = r % 64; half = d // 32; fi = d % 32
        cosT[r] = np.cos(ang[:, fi])
        sinT[r] = (-1.0 if half == 0 else 1.0) * np.sin(ang[:, fi])
    return cosT, sinT


def run_L5(inp, mod, xTs):
    W = inp["w_in_odd"][0]
    perm = np.arange(1536)
    r = perm % 128
    partner = np.where((r % 64) < 32, perm + 32, perm - 32)
    Wsw = W[:, :1536][:, partner]
    cols = []
    for g in range(6):
        cols += [W[:, 256 * g:256 * g + 256], Wsw[:, 256 * g:256 * g + 256]]
    cols += [W[:, 1536:2304], W[:, 2304:2816]]
    wcat = np.ascontiguousarray(np.concatenate(cols, axis=1))
    gn = fm_vec(inp["g_norm"][1, 0], 8)
    tabs = [rope_tables(q) for q in range(4)]
    in_maps = [{"xT": xTs[c], "mod": mod_for_core(mod, 1, c), "gn": gn, "w": wcat, "cosT": tabs[c % 4][0], "sinT": tabs[c % 4][1]}
               for c in range(NCORES)]
    return run(build_L5(), in_maps)


LAM_INIT = 0.8 - 0.6 * float(np.exp(-0.3))
HALO = 15; TH = TOK + 2 * HALO


def build_L6(nh=6, nqb=4, dbg=False):
    kb = KB(); nc = kb.nc
    xT = kb.dram("xT", [128, 8, TOK], F32, "ExternalInput")
    qd = kb.dram("qT", [128, 6, TOK], BF16, "ExternalInput")
    kd = kb.dram("kT", [128, 6, SL], BF16, "ExternalInput")
    vd = kb.dram("v", [128, NCH, 6, 128], BF16, "ExternalInput")
    agd = kb.dram("ag", [128, 2, 2, TH], F32, "ExternalInput")
    wdwd = kb.dram("wdw", [128, 2, 31], F32, "ExternalInput")
    lnd = kb.dram("ln", [128, 2, 2], F32, "ExternalInput")
    lampd = kb.dram("lamp", [128, 4, 64], F32, "ExternalInput")
    gsubd = kb.dram("gsub", [128, 1], F32, "ExternalInput")
    modd = kb.dram("mod", [128, 48, 2], F32, "ExternalInput")
    wd = kb.dram("w", [1024, 1024], F32, "ExternalInput")
    xo = kb.dram("xo", [128, 8, TOK], F32, "ExternalOutput")
    mdbg = kb.dram("mdbg", [128, 8, TOK], BF16, "ExternalOutput") if dbg else None

    def sbb(name, shape, dt=F32, nb=1):
        t = kb.sb(name, shape, dt)
        return (t, Buf(t)) if nb == 1 else (t, [Buf(t) for _ in range(nb)])
    kt_t, kt_b = sbb("kt", [128, 2, SL], BF16, nb=2); v_t, v_b = sbb("vv", [128, 2, NCH, 128], BF16, nb=2)
    q_t, q_b = sbb("q", [128, 6, TOK], BF16)
    m_t = kb.sb("m", [128, 8, TOK], BF16); m_b = [Buf(m_t) for _ in range(4)]
    p_t, p_b = sbb("p", [128, 3, 1024], BF16, nb=3)
    za_t = kb.sb("za", [128, 2, 2, 1024]); za_b = [[Buf(za_t), Buf(za_t)], [Buf(za_t), Buf(za_t)]]
    rzz_t, rzz_b = sbb("rzz", [128, 2, 512], nb=2)
    x_t, x_b = sbb("x", [128, 1, 8, 512], nb=1)
    x_b = [x_b, x_b]
    w_t, w_b = sbb("w", [128, 8, 1024], BF16)
    wdw_t, wdw_b = sbb("wdw", [128, 2, 31]); ln_t, ln_b = sbb("ln", [128, 2, 2])
    lamp_t, lamp_b = sbb("lamp", [128, 4, 64]); lpr_t, lpr_b = sbb("lpr", [128, 2, 64]); ls_t, ls_b = sbb("ls", [128, 2])
    nl_t, nl_b = sbb("nl", [128, 1]); gs_t, gs_b = sbb("gs", [128, 1]); mod_t, mod_b = sbb("mod", [128, 48, 2])
    ones_t, ones_b = sbb("ones", [128, 128], BF16); onef_t, onef_b = sbb("onef", [128, 128])
    eps_t, eps_b = sbb("eps", [128, 1]); epsl_t, epsl_b = sbb("epsl", [128, 1])
    rz_t, rz_b = sbb("rz", [128, 2, 512], nb=2); od_t, od_b = sbb("od", [128, 2, 512], nb=2)
    sq_t, sq_b = sbb("sq", [128, 512], BF16); rs_t, rs_b = sbb("rs", [128, 512])
    xc_t, xc_b = sbb("xc", [128, 2, 512], nb=2); sqf_t, sqf_b = sbb("sqf", [128, 2, 512], nb=2)
    spt = [kb.psum("sp%d" % i, [128, 1024], F32) for i in range(2)]
    pst = [None] * 4 + [kb.psum("pb%d" % i, [128, 512], F32) for i in range(4, 8)]
    kb.begin()
    AG = kt_t[:, :, :].rearrange("p a s -> p (a s)").bitcast(F32)
    a_v = AG[:, 0:2 * TH].rearrange("p (c t) -> p c t", c=2); g_v = AG[:, 2 * TH:4 * TH].rearrange("p (c t) -> p c t", c=2)
    ag_b = Buf(AG)
    ACC = v_t[:, :, :, :].rearrange("p a c d -> p (a c d)").bitcast(F32)
    accD = ACC[:, 0:2 * TOK].rearrange("p (c t) -> p c t", c=2); accP = ACC[:, 2 * TOK:4 * TOK].rearrange("p (c t) -> p c t", c=2)
    accD_b = Buf(ACC); accP_b = Buf(ACC)
    bankb = [None] * 4 + [Buf(pst[i]) for i in range(4, 8)]
    SP_b = [Buf(spt[0]), Buf(spt[1])]
    sring = [bankb[4], bankb[5], bankb[6], bankb[7]]
    sri = [0]
    only7 = [False]

    def rnext():
        if only7[0]:
            return bankb[7]
        b = sring[sri[0] % 4]; sri[0] += 1
        return b
    class R:
        def next(self):
            return rnext()
    ring = R()
    for (t, b, d) in ((wdw_t, wdw_b, wdwd), (ln_t, ln_b, lnd), (lamp_t, lamp_b, lampd), (gs_t, gs_b, gsubd), (mod_t, mod_b, modd), (q_t, q_b, qd)):
        kb.dma("sp", t[:], d, writes=[b])
    kb.dma("sp", a_v, agd[:, 0], writes=[ag_b]); kb.dma("sp", g_v, agd[:, 1], writes=[ag_b])
    kb.dma("pool", w_t[:], wd.rearrange("(kc p) n -> p kc n", p=128), writes=[w_b])
    kb.op("pool", lambda: nc.gpsimd.memset(ones_t[:], 1.0), writes=[ones_b])
    kb.op("pool", lambda: nc.gpsimd.memset(onef_t[:], 1.0), writes=[onef_b])
    kb.op("pool", lambda: nc.gpsimd.memset(eps_t[:], 128.0 * EPS), writes=[eps_b])
    kb.op("pool", lambda: nc.gpsimd.memset(epsl_t[:], EPS), writes=[epsl_b])
    kb.op("dve", lambda: nc.vector.tensor_tensor(out=lpr_t[:], in0=lamp_t[:, 0::2, :], in1=lamp_t[:, 1::2, :], op=ALU.mult), reads=[lamp_b], writes=[lpr_b])
    kb.op("dve", lambda: nc.vector.tensor_reduce(out=ls_t[:], in_=lpr_t[:], axis=AX.X, op=ALU.add), reads=[lpr_b], writes=[ls_b])
    kb.op("act", lambda: nc.scalar.activation(out=ls_t[:], in_=ls_t[:], func=AF.Exp), reads=[ls_b], writes=[ls_b])
    kb.op("dve", lambda: nc.vector.tensor_tensor(out=nl_t[:], in0=ls_t[:, 1:2], in1=ls_t[:, 0:1], op=ALU.subtract), reads=[ls_b], writes=[nl_b])
    kb.op("dve", lambda: nc.vector.tensor_scalar(out=nl_t[:], in0=nl_t[:], scalar1=-LAM_INIT, scalar2=None, op0=ALU.add), reads=[nl_b], writes=[nl_b])
    kb.op("dve", lambda: nc.vector.tensor_scalar(out=gs_t[:], in0=gs_t[:], scalar1=float((1.0 - LAM_INIT) * np.sqrt(128.0)), scalar2=None, op0=ALU.mult),
          reads=[gs_b], writes=[gs_b])
    kb.op("act", lambda: nc.scalar.activation(out=g_v, in_=g_v, func=AF.Sigmoid), reads=[ag_b], writes=[ag_b])
    kb.op("dve", lambda: nc.vector.tensor_tensor(out=a_v, in0=a_v, in1=g_v, op=ALU.mult), reads=[ag_b], writes=[ag_b])
    for j in range(2):
        for k in range(31):
            if k == 0:
                kb.op("dve", lambda k=k: nc.vector.tensor_scalar(out=accD[:, j, :], in0=a_v[:, j, k:k + TOK], scalar1=wdw_t[:, j, k:k + 1], scalar2=None, op0=ALU.mult),
                      reads=[ag_b, wdw_b], writes=[accD_b])
            else:
                kb.op("dve", lambda k=k: nc.vector.scalar_tensor_tensor(out=accD[:, j, :], in0=a_v[:, j, k:k + TOK], scalar=wdw_t[:, j, k:k + 1], in1=accD[:, j, :],
                                                                      op0=ALU.mult, op1=ALU.add), reads=[ag_b, wdw_b, accD_b], writes=[accD_b])
    for tb in range(4):
        cs = slice(tb * 512, (tb + 1) * 512); e = tb % 2
        ps = rnext()
        for j in range(2):
            kb.op("pe", lambda j=j, ps=ps: nc.tensor.matmul(ps.t[:, :], onef_t[:, :], accD[:, j, cs], start=(j == 0), stop=(j == 1)), reads=[onef_b, accD_b], writes=[ps])
        for j in range(2):
            kb.op("dve", lambda j=j, ps=ps: nc.vector.scalar_tensor_tensor(out=xc_t[:, e, :] if j == 0 else sqf_t[:, e, :], in0=ps.t[:, :], scalar=-1.0 / 256.0,
                                                                           in1=accD[:, j, cs], op0=ALU.mult, op1=ALU.add),
                  reads=[ps, accD_b], writes=[xc_b[e] if j == 0 else sqf_b[e]])
        kb.op("act", lambda: nc.scalar.activation(out=od_t[:, e, :], in_=xc_t[:, e, :], func=AF.Square), reads=[xc_b[e]], writes=[od_b[e]])
        kb.op("act", lambda: nc.scalar.activation(out=rz_t[:, e, :], in_=sqf_t[:, e, :], func=AF.Square), reads=[sqf_b[e]], writes=[rz_b[e]])
        ps2 = rnext()
        kb.op("pe", lambda ps2=ps2: nc.tensor.matmul(ps2.t[:, :], onef_t[:, :], od_t[:, e, :], start=True, stop=False), reads=[onef_b, od_b[e]], writes=[ps2])
        kb.op("pe", lambda ps2=ps2: nc.tensor.matmul(ps2.t[:, :], onef_t[:, :], rz_t[:, e, :], start=False, stop=True), reads=[onef_b, rz_b[e]], writes=[ps2])
        kb.op("act", lambda ps2=ps2: nc.scalar.activation(out=rs_t[:, :], in_=ps2.t[:, :], func=AF.Sqrt, bias=epsl_t[:, 0:1], scale=1.0 / 256.0),
              reads=[ps2, epsl_b], writes=[rs_b])
        kb.op("dve", lambda: nc.vector.reciprocal(out=rs_t[:, :], in_=rs_t[:, :]), reads=[rs_b], writes=[rs_b])
        for j, (src_t, src_b) in enumerate(((xc_t, xc_b), (sqf_t, sqf_b))):
            kb.op("dve", lambda j=j, src_t=src_t: nc.vector.scalar_tensor_tensor(out=src_t[:, e, :], in0=src_t[:, e, :], scalar=ln_t[:, 0, j:j + 1], in1=rs_t[:, :],
                                                                                 op0=ALU.mult, op1=ALU.mult), reads=[src_b[e], ln_b, rs_b], writes=[src_b[e]])
            kb.op("act", lambda j=j, src_t=src_t: nc.scalar.activation(out=m_t[:, 6 + j, cs], in_=src_t[:, e, :], func=AF.Silu, bias=ln_t[:, 1, j:j + 1], scale=1.0),
                  reads=[src_b[e], ln_b], writes=[m_b[tb]])
    kb.barrier()
    kt_b = [Buf(kt_t), Buf(kt_t)]; v_b = [Buf(v_t), Buf(v_t)]
    O_b = [bankb[4], bankb[5]]
    only7[0] = True
    def load_head(hd):
        d = hd % 2
        kb.dma("sp", kt_t[:, d, :], kd[:, hd, :], writes=[kt_b[d]])
        kb.dma("sp", v_t[:, d], vd[:, :, hd, :], writes=[v_b[d]])
    NKP = NCH // 2
    its = [(hd, qb, mp, kp) for hd in range(nh) for qb in range(nqb) for mp in range(2) for kp in range(NKP)]
    NP = 3

    def emit_S(i):
        hd, qb, mp, kp = its[i]
        d = hd % 2; pr = slice(64 * mp, 64 * mp + 64); qs = slice(qb * 512, (qb + 1) * 512)
        sb_ = SP_b[i % 2]; sp_ = spt[i % 2]; pb_ = p_b[i % NP]
        if qb == 0 and mp == 0 and kp == 0 and hd + 1 < nh:
            load_head(hd + 1)
        for h in range(2):
            kt = 2 * kp + h
            kb.op("pe", lambda h=h, kt=kt: nc.tensor.matmul(sp_[:, h * 512:(h + 1) * 512], kt_t[pr, d, kt * 128:(kt + 1) * 128], q_t[pr, hd, qs], start=True, stop=True),
                  reads=[kt_b[d], q_b], writes=[sb_])
        kb.op("act", lambda: nc.scalar.activation(out=p_t[:, i % NP, :], in_=sp_[:, :], func=AF.Exp, scale=0.125), reads=[sb_], writes=[pb_])

    def emit_PV(i):
        hd, qb, mp, kp = its[i]
        d = hd % 2; pb_ = p_b[i % NP]; qs = slice(qb * 512, (qb + 1) * 512)
        for h in range(2):
            kt = 2 * kp + h
            kb.op("pe", lambda h=h, kt=kt: nc.tensor.matmul(pst[4 + mp][:, :], v_t[:, d, kt, :], p_t[:, i % NP, h * 512:(h + 1) * 512],
                                                          start=(kt == 0), stop=(kt == NCH - 1)), reads=[v_b[d], pb_], writes=[O_b[mp]])
        for h in range(2):
            kt = 2 * kp + h
            kb.op("pe", lambda h=h, kt=kt: nc.tensor.matmul(pst[6][:, :], ones_t[:, :], p_t[:, i % NP, h * 512:(h + 1) * 512],
                                                          start=(kt == 0), stop=(kt == NCH - 1)), reads=[ones_b, pb_], writes=[bankb[6]])
        if kp == NKP - 1:
            kb.op("dve", lambda: nc.vector.reciprocal(out=rzz_t[:, mp, :], in_=pst[6][:, :]), reads=[bankb[6]], writes=[rzz_b[mp]])
        if kp == NKP - 1 and mp == 1:
            e = (hd * 4 + qb) % 2
            for m2 in range(2):
                kb.op("dve", lambda m2=m2: nc.vector.tensor_tensor(out=(od_t[:, e, :] if m2 == 0 else xc_t[:, e, :]), in0=pst[4 + m2][:, :], in1=rzz_t[:, m2, :], op=ALU.mult),
                      reads=[O_b[m2], rzz_b[m2]], writes=[od_b[e] if m2 == 0 else xc_b[e]])
            kb.op("dve", lambda: nc.vector.scalar_tensor_tensor(out=od_t[:, e, :], in0=xc_t[:, e, :], scalar=nl_t[:, 0:1], in1=od_t[:, e, :], op0=ALU.mult, op1=ALU.add),
                  reads=[xc_b[e], nl_b, od_b[e]], writes=[od_b[e]])
            emit_headnorm(kb, ring, od_t[:, e, :], od_b[e], 512, sq_t[:, :], sq_b, rs_t[:, :], rs_b, ones_t, ones_b, eps_t, eps_b)
            kb.op("dve", lambda: nc.vector.scalar_tensor_tensor(out=m_t[:, hd, qs], in0=od_t[:, e, :], scalar=gs_t[:, 0:1], in1=rs_t[:, :], op0=ALU.mult, op1=ALU.mult),
                  reads=[od_b[e], gs_b, rs_b], writes=[m_b[qb]])
    load_head(0)
    LA = 1
    for i in range(min(LA, len(its))):
        emit_S(i)
    for i in range(len(its)):
        if i + LA < len(its):
            emit_S(i + LA)
        emit_PV(i)
    only7[0] = False
    if dbg:
        kb.dma("sp", mdbg, m_t[:], reads=m_b, is_out=True)
    for tb in range(4):
        d = 0; lo = tb * 512
        kb.dma("sp", x_t[:, d], xT[:, :, lo:lo + 512], writes=[x_b[d]])
        emit_wout_res(kb, ring, tb, 512, 0, m_t[:, :, lo:lo + 512], m_b[tb], w_t, w_b, lambda oc, d=d: x_t[:, d, oc, :], x_b[d], mod_t, mod_b, 2)
        kb.dma("sp", xo[:, :, lo:lo + 512], x_t[:, d], reads=[x_b[d]], is_out=True)
    return kb.end()


def run_L6(inp, mod, x4, r5):
    qkv_l, qkv_c = gather_fm(r5, "o_qkv", 18, NPBF)
    glu_l, _ = gather_fm(r5, "o_glu", 4, np.float32)
    w = np.ascontiguousarray(inp["w_out_odd"][0])
    wdw = np.ascontiguousarray(inp["w_dw"][0].T.reshape(2, 128, 31).transpose(1, 0, 2))
    ln = np.ascontiguousarray(np.stack([fm_vec(inp["g_conv_ln"][0], 2), fm_vec(inp["b_conv_ln"][0], 2)], axis=1))
    lamp = np.ascontiguousarray(np.broadcast_to(inp["lam_p"][0][None], (128, 4, 64)))
    gsub = np.ascontiguousarray(inp["g_subln"][0].reshape(128, 1))
    in_maps = []
    for c in range(NCORES):
        b, q = c // 4, c % 4
        kall = np.concatenate([qkv_c[b][768:1536], qkv_l[b][768:1536]], axis=1)
        vall = np.concatenate([qkv_c[b][1536:2304], qkv_l[b][1536:2304]], axis=1)
        kT = np.ascontiguousarray(kall.reshape(6, 128, SL).transpose(1, 0, 2))
        v = np.ascontiguousarray(vall.reshape(6, 128, NCH, 128).transpose(3, 2, 0, 1))
        qT = np.ascontiguousarray(qkv_l[b][0:768, q * TOK:(q + 1) * TOK].reshape(6, 128, TOK).transpose(1, 0, 2))
        gp = np.zeros((512, SEQ + 2 * HALO), np.float32); gp[:, HALO:HALO + SEQ] = glu_l[b]
        seg = gp[:, q * TOK:q * TOK + TH]
        ag = np.ascontiguousarray(seg.reshape(2, 2, 128, TH).transpose(2, 0, 1, 3))
        in_maps.append({"xT": np.ascontiguousarray(x4[c][:, :, :TOK]), "qT": qT, "kT": kT, "v": v, "ag": ag, "wdw": wdw, "ln": ln,
                        "lamp": lamp, "gsub": gsub, "mod": mod_for_core(mod, 1, c), "w": w})
    import os
    dbg = os.environ.get('L6_DBG') is not None
    res = run(build_L6(nh=int(os.environ.get('L6_NH', '6')), nqb=int(os.environ.get('L6_NQB', '4')), dbg=dbg), in_maps)
    if dbg:
        return res
    out = []
    for c in range(NCORES):
        t = np.zeros((128, 8, NT), np.float32); t[:, :, :TOK] = res[c]["xo"]
        out.append(t)
    return out


def kernel(**inp):
    inp = {k: np.asarray(v) for k, v in inp.items()}
    mod = run_L0(inp)
    xTs = [xT_for_core(inp["x"], inp["ctx"], c) for c in range(NCORES)]
    r1 = run_L1(inp, mod, xTs)
    H, four_l, four_c = run_L2(inp, r1)
    x3 = run_L3(inp, mod, xTs, r1, H, four_l, four_c)
    x4 = run_FFN(inp, mod, x3, 0, False)
    r5 = run_L5(inp, mod, x4)
    x6 = run_L6(inp, mod, x4, r5)
    x7 = run_FFN(inp, mod, x6, 1, True)
    out, _ = unslab(x7)
    return out.astype(np.float32)
```
